# Optimizing a Trainium2 kernel written in Bass

```python
import math
import jax, jax.numpy as jnp
from jax import lax
import numpy as np

D_MODEL = 1024
BATCH = 32
SEQ = 2048
DEPTH = 2

DEEPNORM_ALPHA = (2 * DEPTH) ** 0.25
DEEPNORM_BETA = (8 * DEPTH) ** -0.25
LN_EPS = 1e-5
NEG_INF = -1e30

RET_HEADS = 4
RET_HEAD_DIM = D_MODEL // (2 * RET_HEADS)
RET_CHUNK = 128
ROPE_BASE = 10000.0

NSA_HEADS = 8
NSA_KV_HEADS = 2
NSA_GROUP = NSA_HEADS // NSA_KV_HEADS
NSA_HEAD_DIM = D_MODEL // (2 * NSA_HEADS)
CMP_BLOCK = 32
CMP_STRIDE = 16
CMP_HIDDEN = 4 * NSA_HEAD_DIM
SEL_BLOCK = 64
SEL_TOP_N = 16
SEL_QUERY_CHUNK = 16
SEL_FORCE_SCORE = 1e4
WIN_SIZE = 512
ATTN_BLOCK = 128

DIL_PATTERNS = ((128, 1), (512, 4), (2048, 16))
DIL_HEADS = 8
DIL_HEAD_DIM = D_MODEL // DIL_HEADS

REL_BUCKETS = 32
REL_MAX_DIST = 128
BIAS_HEADS = 8

D_FF = -(-8 * D_MODEL // 768) * 256

RET_W = RET_HEADS * RET_HEAD_DIM
NSA_QW = NSA_HEADS * NSA_HEAD_DIM
NSA_KVW = NSA_KV_HEADS * NSA_HEAD_DIM
AB_SPLITS = (RET_W, RET_W, RET_W, RET_W, NSA_QW) + (NSA_KVW,) * 6 + (3 * NSA_HEADS,)
AB_IN_COLS = sum(AB_SPLITS)
AB_OUT_COLS = RET_W + NSA_QW
DIL_WIDTH = DIL_HEADS * DIL_HEAD_DIM
DIL_IN_COLS = len(DIL_PATTERNS) * 3 * DIL_WIDTH

kernel_name = 'hybrid_retnet_nsa_dilated_trunk'


def layer_norm(x, g, b):
    xf = x.astype(jnp.float32)
    mu = jnp.mean(xf, axis=-1, keepdims=True)
    var = jnp.mean(jnp.square(xf - mu), axis=-1, keepdims=True)
    return ((xf - mu) * lax.rsqrt(var + LN_EPS) * g + b).astype(x.dtype)


def masked_softmax(s, valid):
    s = jnp.where(valid, s, NEG_INF)
    m = jnp.max(s, axis=-1, keepdims=True)
    e = jnp.where(valid, jnp.exp(s - m), 0.0)
    den = jnp.maximum(jnp.sum(e, axis=-1, keepdims=True), 1e-30)
    return e / den, (m + jnp.log(den))[..., 0]


def t5_bucket(dist):
    n = jnp.maximum(dist, 0)
    max_exact = REL_BUCKETS // 2
    nf = jnp.maximum(n, 1).astype(jnp.float32)
    large = max_exact + (jnp.log(nf / max_exact) / math.log(REL_MAX_DIST / max_exact)
                         * (REL_BUCKETS - max_exact)).astype(jnp.int32)
    large = jnp.minimum(large, REL_BUCKETS - 1)
    return jnp.where(n < max_exact, n, large)


def rope(t, pos):
    d = t.shape[-1]
    inv = ROPE_BASE ** (-jnp.arange(0, d, 2, dtype=jnp.float32) / d)
    ang = pos.astype(jnp.float32)[:, None] * inv[None, :]
    cos, sin = jnp.cos(ang), jnp.sin(ang)
    t1, t2 = t[..., : d // 2], t[..., d // 2:]
    return jnp.concatenate([t1 * cos - t2 * sin, t1 * sin + t2 * cos], axis=-1)


def split_heads(t, h):
    B, S, _ = t.shape
    return t.reshape(B, S, h, -1).transpose(0, 2, 1, 3)


def retention(q, k, v):
    B, H, S, dk = q.shape
    dv = v.shape[-1]
    C = RET_CHUNK
    N = S // C
    log_gamma = jnp.log1p(-jnp.exp2(-5.0 - jnp.arange(H, dtype=jnp.float32)))
    idx = jnp.arange(C, dtype=jnp.float32)
    diff = idx[:, None] - idx[None, :]
    inner_decay = jnp.where(diff >= 0, jnp.exp(log_gamma[:, None, None] * jnp.maximum(diff, 0.0)), 0.0)
    qc = (q * dk ** -0.5).reshape(B, H, N, C, dk)
    kc = k.reshape(B, H, N, C, dk)
    vc = v.reshape(B, H, N, C, dv)
    scores = jnp.einsum('bhncd,bhnkd->bhnck', qc, kc) * inner_decay[None, :, None]
    inner = jnp.einsum('bhnck,bhnke->bhnce', scores, vc)
    k_decay = jnp.exp(log_gamma[:, None] * (C - 1 - idx)[None, :])
    kv = jnp.einsum('bhnkd,bhnke->bhnde', kc * k_decay[None, :, None, :, None], vc)
    chunk_decay = jnp.exp(log_gamma * C)[None, :, None, None]

    def step(state, kv_n):
        return state * chunk_decay + kv_n, state

    _, prev = lax.scan(step, jnp.zeros((B, H, dk, dv), jnp.float32), jnp.moveaxis(kv, 2, 0))
    q_decay = jnp.exp(log_gamma[:, None] * (idx + 1.0)[None, :])
    cross = jnp.einsum('bhncd,nbhde->bhnce', qc * q_decay[None, :, None, :, None], prev)
    return (inner + cross).reshape(B, H, S, dv)


def banded_attention(q, k, v, max_dist, dist_scale, rel_bias):
    N, G, R, L, dh = q.shape
    scale = dh ** -0.5
    pad_end = (-L) % ATTN_BLOCK
    Lp = L + pad_end
    pad_front = -(-max_dist // ATTN_BLOCK) * ATTN_BLOCK
    span = pad_front + ATTN_BLOCK
    qp = jnp.pad(q, ((0, 0), (0, 0), (0, 0), (0, pad_end), (0, 0)))
    kp = jnp.pad(k, ((0, 0), (0, 0), (pad_front, pad_end), (0, 0)))
    vp = jnp.pad(v, ((0, 0), (0, 0), (pad_front, pad_end), (0, 0)))

    def block(i):
        start = i * ATTN_BLOCK
        qb = lax.dynamic_slice_in_dim(qp, start, ATTN_BLOCK, axis=3)
        kb = lax.dynamic_slice_in_dim(kp, start, span, axis=2)
        vb = lax.dynamic_slice_in_dim(vp, start, span, axis=2)
        qi = start + jnp.arange(ATTN_BLOCK)
        ki = start - pad_front + jnp.arange(span)
        dist = qi[:, None] - ki[None, :]
        valid = (dist >= 0) & (dist <= max_dist) & (ki[None, :] >= 0)
        bias = rel_bias[t5_bucket(dist * dist_scale)].reshape(ATTN_BLOCK, span, G, R).transpose(2, 3, 0, 1)
        s = jnp.einsum('ngrqd,ngkd->ngrqk', qb, kb).astype(jnp.float32) * scale + bias
        p, lse = masked_softmax(s, valid)
        return jnp.einsum('ngrqk,ngkd->ngrqd', p, vb.astype(jnp.float32)), lse

    o, lse = lax.map(block, jnp.arange(Lp // ATTN_BLOCK))
    o = jnp.moveaxis(o, 0, 3).reshape(N, G, R, Lp, dh)[..., :L, :]
    lse = jnp.moveaxis(lse, 0, 3).reshape(N, G, R, Lp)[..., :L]
    return o, lse


def nsa_attention(q, k_cmp, v_cmp, k_slc, v_slc, k_win, v_win, gates,
                  cmp_pos_k, cmp_pos_v, cmp_k_w1, cmp_k_w2, cmp_v_w1, cmp_v_w2, rel_bias):
    B, G, R, S, dh = q.shape
    scale = dh ** -0.5
    qpos = jnp.arange(S)

    n_cmp = (S - CMP_BLOCK) // CMP_STRIDE + 1
    cmp_start = jnp.arange(n_cmp) * CMP_STRIDE
    win_idx = cmp_start[:, None] + jnp.arange(CMP_BLOCK)[None, :]

    def compress(t, pos_emb, w1, w2):
        blocks = t[:, :, win_idx] + pos_emb
        return jax.nn.silu(blocks.reshape(B, G, n_cmp, CMP_BLOCK * dh) @ w1) @ w2

    kc = compress(k_cmp, cmp_pos_k, cmp_k_w1, cmp_k_w2)
    vc = compress(v_cmp, cmp_pos_v, cmp_v_w1, cmp_v_w2)
    cmp_end = cmp_start + CMP_BLOCK - 1
    cmp_valid = cmp_end[None, :] <= qpos[:, None]
    cmp_bias = rel_bias[t5_bucket(qpos[:, None] - cmp_end[None, :])].reshape(S, n_cmp, G, R).transpose(2, 3, 0, 1)
    s = jnp.einsum('bgrqd,bgcd->bgrqc', q, kc).astype(jnp.float32) * scale + cmp_bias
    p_cmp, _ = masked_softmax(s, cmp_valid)
    o_cmp = jnp.einsum('bgrqc,bgcd->bgrqd', p_cmp, vc.astype(jnp.float32))

    n_sel = S // SEL_BLOCK
    sel_start = jnp.arange(n_sel) * SEL_BLOCK
    overlap = jnp.clip(jnp.minimum(cmp_start[:, None] + CMP_BLOCK, sel_start[None, :] + SEL_BLOCK)
                       - jnp.maximum(cmp_start[:, None], sel_start[None, :]), 0, None).astype(jnp.float32) / CMP_BLOCK
    imp = jnp.einsum('bgrqc,cj->bgqj', p_cmp, overlap)
    blk_q = qpos // SEL_BLOCK
    j = jnp.arange(n_sel)
    sel_valid = j[None, :] <= blk_q[:, None]
    forced = (j[None, :] == 0) | (j[None, :] == blk_q[:, None]) | (j[None, :] == blk_q[:, None] - 1)
    sel_score = jnp.where(forced, SEL_FORCE_SCORE, jnp.where(sel_valid, imp, -1.0))
    top_n = min(SEL_TOP_N, n_sel)
    _, sel_idx = lax.top_k(sel_score, top_n)

    kb = k_slc.reshape(B, G, n_sel, SEL_BLOCK * dh)
    vb = v_slc.reshape(B, G, n_sel, SEL_BLOCK * dh)
    bias_gt = rel_bias.reshape(REL_BUCKETS, G, R).transpose(1, 0, 2)
    g_arr = jnp.arange(G)[None, :, None, None]
    Qc = SEL_QUERY_CHUNK
    n_keys = top_n * SEL_BLOCK

    def sel_chunk(i):
        start = i * Qc
        qc = lax.dynamic_slice_in_dim(q, start, Qc, axis=3)
        ic = lax.dynamic_slice_in_dim(sel_idx, start, Qc, axis=2)
        flat = ic.reshape(B, G, Qc * top_n)[..., None]
        kg = jnp.take_along_axis(kb, flat, axis=2).reshape(B, G, Qc, n_keys, dh)
        vg = jnp.take_along_axis(vb, flat, axis=2).reshape(B, G, Qc, n_keys, dh)
        kpos = (ic[..., None] * SEL_BLOCK + jnp.arange(SEL_BLOCK)).reshape(B, G, Qc, n_keys)
        qp = (start + jnp.arange(Qc))[None, None, :, None]
        valid = (kpos <= qp)[:, :, None]
        bias = jnp.moveaxis(bias_gt[g_arr, t5_bucket(qp - kpos)], -1, 2)
        s = jnp.einsum('bgrqd,bgqkd->bgrqk', qc, kg).astype(jnp.float32) * scale + bias
        p, _ = masked_softmax(s, valid)
        return jnp.einsum('bgrqk,bgqkd->bgrqd', p, vg.astype(jnp.float32))

    o_slc = lax.map(sel_chunk, jnp.arange(S // Qc))
    o_slc = jnp.moveaxis(o_slc, 0, 3).reshape(B, G, R, S, dh)

    o_win, _ = banded_attention(q, k_win, v_win, WIN_SIZE - 1, 1, rel_bias)

    return gates[..., 0:1] * o_cmp + gates[..., 1:2] * o_slc + gates[..., 2:3] * o_win


def mixer_retention_nsa(u, w_in, w_out, gn_g, gn_b, cmp_pos_k, cmp_pos_v,
                        cmp_k_w1, cmp_k_w2, cmp_v_w1, cmp_v_w2, rel_bias):
    B, S, _ = u.shape
    proj = u @ w_in
    (q_r, k_r, v_r, g_r, q_n, k_c, v_c, k_s, v_s, k_w, v_w, gates) = jnp.split(
        proj, np.cumsum(AB_SPLITS)[:-1].tolist(), axis=-1)

    pos = jnp.arange(S)
    qh = rope(split_heads(q_r, RET_HEADS).astype(jnp.float32), pos)
    kh = rope(split_heads(k_r, RET_HEADS).astype(jnp.float32), pos)
    vh = split_heads(v_r, RET_HEADS).astype(jnp.float32)
    y = retention(qh, kh, vh)
    mu = jnp.mean(y, axis=-1, keepdims=True)
    var = jnp.mean(jnp.square(y - mu), axis=-1, keepdims=True)
    y = ((y - mu) * lax.rsqrt(var + LN_EPS)).transpose(0, 2, 1, 3).reshape(B, S, RET_W)
    y_ret = (y * gn_g + gn_b) * jax.nn.silu(g_r.astype(jnp.float32))

    qn = q_n.reshape(B, S, NSA_KV_HEADS, NSA_GROUP, NSA_HEAD_DIM).transpose(0, 2, 3, 1, 4)
    kv_heads = lambda t: split_heads(t, NSA_KV_HEADS)
    gt = jax.nn.sigmoid(gates.astype(jnp.float32)).reshape(B, S, NSA_KV_HEADS, NSA_GROUP, 3).transpose(0, 2, 3, 1, 4)
    o = nsa_attention(qn, kv_heads(k_c), kv_heads(v_c), kv_heads(k_s), kv_heads(v_s),
                      kv_heads(k_w), kv_heads(v_w), gt, cmp_pos_k, cmp_pos_v,
                      cmp_k_w1, cmp_k_w2, cmp_v_w1, cmp_v_w2, rel_bias)
    y_nsa = o.transpose(0, 3, 1, 2, 4).reshape(B, S, NSA_QW)

    return jnp.concatenate([y_ret, y_nsa], axis=-1).astype(u.dtype) @ w_out


def dilated_group(q, k, v, window, dilation, rel_bias):
    B, H, S, dh = q.shape
    L = S // dilation
    to_res = lambda t: t.reshape(B, H, L, dilation, dh).transpose(3, 0, 1, 2, 4)
    qr, kr, vr = to_res(q)[:, :, :, None], to_res(k), to_res(v)
    o, lse = lax.map(lambda a: banded_attention(a[0], a[1], a[2], window // dilation, dilation, rel_bias),
                     (qr, kr, vr))
    o = o[:, :, :, 0].transpose(1, 2, 3, 0, 4).reshape(B, H, S, dh)
    lse = lse[:, :, :, 0].transpose(1, 2, 3, 0).reshape(B, H, S)
    return o, lse


def mixer_dilated(u, w_in, w_out, rel_bias):
    B, S, _ = u.shape
    proj = (u @ w_in).reshape(B, S, len(DIL_PATTERNS), 3, DIL_HEADS, DIL_HEAD_DIM)
    outs, lses = [], []
    for gi, (window, dilation) in enumerate(DIL_PATTERNS):
        q, k, v = (proj[:, :, gi, m].transpose(0, 2, 1, 3) for m in range(3))
        o, lse = dilated_group(q, k, v, window, dilation, rel_bias)
        outs.append(o)
        lses.append(lse)
    weight = jax.nn.softmax(jnp.stack(lses), axis=0)
    o = jnp.sum(weight[..., None] * jnp.stack(outs), axis=0)
    return o.transpose(0, 2, 1, 3).reshape(B, S, DIL_WIDTH).astype(u.dtype) @ w_out


def swiglu(u, w_gate, w_up, w_down):
    return (jax.nn.silu(u @ w_gate) * (u @ w_up)) @ w_down


def setup_inputs(seed: int = 0) -> dict:
    key = jax.random.key(seed)
    ks = jax.random.split(key, 24)
    n_even = (DEPTH + 1) // 2
    n_odd = DEPTH // 2
    nrm = lambda k, shape, s: jax.random.normal(k, shape, jnp.float32) * s
    return {
        'x': nrm(ks[0], (BATCH, SEQ, D_MODEL), 1.0),
        'c': nrm(ks[1], (BATCH, D_MODEL), 1.0),
        'rel_bias': nrm(ks[2], (REL_BUCKETS, BIAS_HEADS), 0.2),
        'ada_w': nrm(ks[3], (DEPTH, D_MODEL, 6 * D_MODEL), 0.5 * D_MODEL ** -0.5),
        'ada_b': nrm(ks[4], (DEPTH, 6 * D_MODEL), 0.02),
        'ln_g': 1.0 + nrm(ks[5], (DEPTH, 2, D_MODEL), 0.02),
        'ln_b': nrm(ks[6], (DEPTH, 2, D_MODEL), 0.02),
        'ab_w_in': nrm(ks[7], (n_even, D_MODEL, AB_IN_COLS), D_MODEL ** -0.5),
        'ab_w_out': nrm(ks[8], (n_even, AB_OUT_COLS, D_MODEL), DEEPNORM_BETA * AB_OUT_COLS ** -0.5),
        'ret_gn_g': 1.0 + nrm(ks[9], (n_even, RET_W), 0.02),
        'ret_gn_b': nrm(ks[10], (n_even, RET_W), 0.02),
        'cmp_pos_k': nrm(ks[11], (n_even, CMP_BLOCK, NSA_HEAD_DIM), 0.2),
        'cmp_pos_v': nrm(ks[12], (n_even, CMP_BLOCK, NSA_HEAD_DIM), 0.2),
        'cmp_k_w1': nrm(ks[13], (n_even, CMP_BLOCK * NSA_HEAD_DIM, CMP_HIDDEN), (CMP_BLOCK * NSA_HEAD_DIM) ** -0.5),
        'cmp_k_w2': nrm(ks[14], (n_even, CMP_HIDDEN, NSA_HEAD_DIM), CMP_HIDDEN ** -0.5),
        'cmp_v_w1': nrm(ks[15], (n_even, CMP_BLOCK * NSA_HEAD_DIM, CMP_HIDDEN), (CMP_BLOCK * NSA_HEAD_DIM) ** -0.5),
        'cmp_v_w2': nrm(ks[16], (n_even, CMP_HIDDEN, NSA_HEAD_DIM), CMP_HIDDEN ** -0.5),
        'dil_w_in': nrm(ks[17], (n_odd, D_MODEL, DIL_IN_COLS), D_MODEL ** -0.5),
        'dil_w_out': nrm(ks[18], (n_odd, DIL_WIDTH, D_MODEL), DEEPNORM_BETA * DIL_WIDTH ** -0.5),
        'ffn_w_gate': nrm(ks[19], (DEPTH, D_MODEL, D_FF), D_MODEL ** -0.5),
        'ffn_w_up': nrm(ks[20], (DEPTH, D_MODEL, D_FF), D_MODEL ** -0.5),
        'ffn_w_down': nrm(ks[21], (DEPTH, D_FF, D_MODEL), DEEPNORM_BETA * D_FF ** -0.5),
    }


def reference(x, c, rel_bias, ada_w, ada_b, ln_g, ln_b, ab_w_in, ab_w_out, ret_gn_g, ret_gn_b,
              cmp_pos_k, cmp_pos_v, cmp_k_w1, cmp_k_w2, cmp_v_w1, cmp_v_w2,
              dil_w_in, dil_w_out, ffn_w_gate, ffn_w_up, ffn_w_down):
    h = x
    for layer in range(DEPTH):
        mod = jax.nn.silu(c) @ ada_w[layer] + ada_b[layer]
        sh_m, sc_m, g_m, sh_f, sc_f, g_f = (t[:, None, :] for t in jnp.split(mod, 6, axis=-1))
        i = layer // 2
        u = h * (1.0 + sc_m) + sh_m
        if layer % 2 == 0:
            y = mixer_retention_nsa(u, ab_w_in[i], ab_w_out[i], ret_gn_g[i], ret_gn_b[i],
                                    cmp_pos_k[i], cmp_pos_v[i], cmp_k_w1[i], cmp_k_w2[i],
                                    cmp_v_w1[i], cmp_v_w2[i], rel_bias)
        else:
            y = mixer_dilated(u, dil_w_in[i], dil_w_out[i], rel_bias)
        h = layer_norm(DEEPNORM_ALPHA * h + g_m * y, ln_g[layer, 0], ln_b[layer, 0])
        u = h * (1.0 + sc_f) + sh_f
        y = swiglu(u, ffn_w_gate[layer], ffn_w_up[layer], ffn_w_down[layer])
        h = layer_norm(DEEPNORM_ALPHA * h + g_f * y, ln_g[layer, 1], ln_b[layer, 1])
    return h
```

```python
import math
from contextlib import ExitStack

import numpy as np
import ml_dtypes

import concourse.bass as bass
import concourse.mybir as mybir
from concourse.bass_utils import run_bass_kernel_spmd

F32 = mybir.dt.float32
BF16 = mybir.dt.bfloat16
AF = mybir.ActivationFunctionType
ALU = mybir.AluOpType

NCORES = 8
D = 1024
S = 2048
KC = 8
TB = 512
NTB = S // TB
DFF = 2816
NFT = DFF // 128
ALPHA = (2 * 2) ** 0.25
LN_EPS = 1e-5
NEG = -30000.0
DEBUG = {}


class Prog:
    ENGS = ("pe", "act", "dve", "pool", "sp")

    def __init__(self, nc, stack, n_dma_sems=12):
        self.nc = nc
        self.streams = {e: [] for e in self.ENGS}
        self.cnt = {e: 0 for e in self.ENGS}
        self.sem = {}
        for e in self.ENGS:
            self.sem[e] = stack.enter_context(nc.semaphore("sem_" + e))
        self.dma_sems = {}
        self.dma_tot = {}
        self.dma_rr = {}
        for q in ("sp", "pool"):
            self.dma_sems[q] = [stack.enter_context(nc.semaphore("dsem_%s_%d" % (q, i))) for i in range(n_dma_sems)]
            self.dma_tot[q] = [0] * n_dma_sems
            self.dma_rr[q] = 0
        self.waited = {}
        self.res = {}
        self.n_ops = 0

    def _need(self, eng, reads, writes):
        need = {}

        def add(tok):
            if tok is None:
                return
            k, v = tok
            if eng == "pe" and k == "pe":
                return
            if need.get(k, 0) < v:
                need[k] = v

        for key in reads:
            r = self.res.get(key)
            if r is not None:
                add(r[0])
        for key in writes:
            r = self.res.get(key)
            if r is not None:
                add(r[0])
                for k, v in r[1].items():
                    add((k, v))
        out = []
        for k, v in need.items():
            if self.waited.get((eng, k), 0) < v:
                self.waited[(eng, k)] = v
                out.append((k, v))
        return out

    def _semh(self, k):
        if isinstance(k, tuple):
            return self.dma_sems[k[1]][k[2]]
        return self.sem[k]

    def _commit(self, tok, reads, writes):
        k, v = tok
        for key in reads:
            r = self.res.setdefault(key, [None, {}])
            if r[1].get(k, 0) < v:
                r[1][k] = v
        for key in writes:
            self.res[key] = [tok, {}]

    def op(self, eng, fn, reads=(), writes=()):
        st = self.streams[eng]
        for k, v in self._need(eng, reads, writes):
            st.append(("wait", k, v))
        self.cnt[eng] += 1
        tok = (eng, self.cnt[eng])
        st.append(("op", fn, True))
        self._commit(tok, reads, writes)
        self.n_ops += 1
        return tok

    def mm_group(self, fns, reads=(), writes=()):
        st = self.streams["pe"]
        for k, v in self._need("pe", reads, writes):
            st.append(("wait", k, v))
        self.cnt["pe"] += 1
        tok = ("pe", self.cnt["pe"])
        for i, fn in enumerate(fns):
            st.append(("op", fn, i == len(fns) - 1))
        self._commit(tok, reads, writes)
        self.n_ops += len(fns)
        return tok

    def dma(self, q, out, in_, reads=(), writes=()):
        st = self.streams[q]
        i = self.dma_rr[q]
        self.dma_rr[q] = (i + 1) % len(self.dma_sems[q])
        key = ("dma", q, i)
        prev = self.dma_tot[q][i]
        if prev and self.waited.get((q, key), 0) < prev:
            self.waited[(q, key)] = prev
            st.append(("wait", key, prev))
        for k, v in self._need(q, reads, writes):
            st.append(("wait", k, v))
        self.dma_tot[q][i] += 16
        tok = (key, self.dma_tot[q][i])
        st.append(("dma", out, in_, self.dma_sems[q][i]))
        self._commit(tok, reads, writes)
        return tok

    def barrier(self):
        toks = [(e, self.cnt[e]) for e in self.ENGS if self.cnt[e]]
        for q in ("sp", "pool"):
            for i, tot in enumerate(self.dma_tot[q]):
                if tot:
                    toks.append((("dma", q, i), tot))
        for e in self.ENGS:
            for tok in toks:
                if tok[0] != e or e != "pe":
                    self.wait_tok(e, tok)

    def wait_tok(self, eng, tok):
        k, v = tok
        if self.waited.get((eng, k), 0) < v:
            self.waited[(eng, k)] = v
            self.streams[eng].append(("wait", k, v))

    def replay(self):
        nc = self.nc
        engmap = {"pe": "tensor", "act": "scalar", "dve": "vector", "pool": "gpsimd", "sp": "sync"}
        waited = {e: set() for e in self.ENGS}
        for ename in self.ENGS:
            for it in self.streams[ename]:
                if it[0] == "wait" and not isinstance(it[1], tuple):
                    waited[it[1]].add(it[2])
        remap = {}
        for e in self.ENGS:
            remap[e] = {v: i + 1 for i, v in enumerate(sorted(waited[e]))}
        with nc.Block() as block:
            for ename in self.ENGS:
                lst = self.streams[ename]
                semh = self.sem[ename]
                rm_own = remap[ename]

                def body(e, lst=lst, semh=semh, rm_own=rm_own):
                    idx = 0
                    for it in lst:
                        if it[0] == "wait":
                            k_, v = it[1], it[2]
                            if isinstance(k_, tuple):
                                e.wait_ge(self.dma_sems[k_[1]][k_[2]], v)
                            else:
                                e.wait_ge(self.sem[k_], remap[k_][v])
                        elif it[0] == "op":
                            ins = it[1](e)
                            if it[2]:
                                idx += 1
                                if idx in rm_own:
                                    ins.then_inc(semh, 1)
                        else:
                            e.dma_start(out=it[1], in_=it[2]).then_inc(it[3], 16)

                getattr(block, engmap[ename])(body)


class Arena:
    def __init__(self, nc, stack, name, nbytes):
        assert nbytes % 4 == 0
        self.t = stack.enter_context(nc.sbuf_tensor(name, [128, nbytes // 4], F32))
        self.nbytes = nbytes
        self.off = 0
        self.marks = []

    def alloc(self, shape, dtype, parts=128):
        esz = 4 if dtype == F32 else 2
        n = int(np.prod(shape))
        nb = (n * esz + 3) // 4 * 4
        assert self.off + nb <= self.nbytes, ("arena overflow", self.off, nb, self.nbytes)
        w0 = self.off // 4
        ap = self.t[0:parts, w0:w0 + nb // 4]
        if dtype != F32:
            ap = ap.bitcast(dtype)
            ap = ap[:, 0:n]
        self.off += nb
        if len(shape) == 2:
            ap = ap.rearrange("p (a b) -> p a b", a=shape[0])
        elif len(shape) == 3:
            ap = ap.rearrange("p (a b c) -> p a b c", a=shape[0], b=shape[1])
        elif len(shape) == 4:
            ap = ap.rearrange("p (a b c d) -> p a b c d", a=shape[0], b=shape[1], c=shape[2])
        return ap

    def mark(self):
        self.marks.append(self.off)

    def release(self):
        self.off = self.marks.pop()


def _wtiles(W, tile_cols):
    K = W.shape[0]
    out = []
    for cols in tile_cols:
        sub = W[:, cols]
        out.append(sub.reshape(K // 128, 128, len(cols)).transpose(1, 0, 2))
    return np.ascontiguousarray(np.stack(out, 0), dtype=np.float32)


def _t5_bucket(dist):
    n = np.maximum(dist, 0)
    nf = np.maximum(n, 1).astype(np.float32)
    large = 16 + (np.log(nf / np.float32(16)) / np.float32(math.log(128 / 16)) * np.float32(16)).astype(np.int32)
    large = np.minimum(large, 31)
    return np.where(n < 16, n, large)


def _bias_tile(rel_bias, dist, valid):
    b = _t5_bucket(dist)
    t = rel_bias[b]
    t = np.where(valid[..., None], t, np.float32(NEG))
    return np.ascontiguousarray(t.transpose(2, 0, 1), dtype=np.float32)


DIL = ((128, 1), (512, 4), (2048, 16))


def _const_tables(rel_bias):
    kk = np.arange(128)[:, None]
    qq = np.arange(128)[None, :]
    c = {}
    tabs = []
    for (win, dl) in DIL:
        md = win // dl
        d0 = qq - kk
        tabs.append(_bias_tile(rel_bias, d0 * dl, (d0 >= 0) & (d0 <= md)))
        d1 = 128 + qq - kk
        tabs.append(_bias_tile(rel_bias, d1 * dl, (d1 >= 0) & (d1 <= md)))
    tab1 = np.stack([tabs[0], tabs[1], tabs[2], tabs[3], tabs[4]], 1)
    c["tab1"] = np.ascontiguousarray(tab1.transpose(0, 2, 1, 3))
    d0 = qq - kk
    T0 = _bias_tile(rel_bias, d0, d0 >= 0)
    d1 = 128 + qq - kk
    T1 = _bias_tile(rel_bias, d1, d1 >= 0)
    tab0 = np.stack([T0, T1], 1)
    c["tab0"] = np.ascontiguousarray(tab0.transpose(0, 2, 1, 3))
    d4 = 512 + qq - kk
    c["m4"] = np.where(d4 <= 511, np.float32(0), np.float32(NEG)).astype(np.float32)
    c["b31"] = np.ascontiguousarray(np.broadcast_to(rel_bias[31][None, :], (128, 8)), dtype=np.float32)
    cpp = (np.arange(128) - 96)[:, None]
    xx = np.arange(512)[None, :]
    dist = xx - 16 * cpp - 31
    cb = rel_bias[_t5_bucket(dist)]
    cb = np.where((dist >= 0)[..., None], cb, np.float32(NEG))
    c["cmpt"] = np.ascontiguousarray(cb.transpose(2, 0, 1), dtype=np.float32)
    return c


class K:
    pass


def _tok_ap(ap2d, start_l, n, dl, r):
    if dl == 1:
        return ap2d[:, start_l:start_l + n]
    return ap2d.rearrange("p (l s) -> p l s", s=dl)[:, start_l:start_l + n, r]


def build_program(nseq, layers, has0, has1):
    nc = bass.Bass("TRN2", target_bir_lowering=False)
    k = K()
    k.nc = nc
    k.nseq = nseq
    dr = {}

    def din(name, shape):
        dr[name] = nc.dram_tensor(name, list(shape), F32, kind="ExternalInput").ap()

    din("xT", [nseq, 128, KC, S])
    din("cT", [128, KC, 4])
    din("adaw", [2, 8, 128, KC, 768])
    din("adab", [128, 2, 48])
    din("lng", [128, 2, 2, KC])
    din("lnb", [128, 2, 2, KC])
    din("ffg", [2, NFT, 128, KC, 128])
    din("ffu", [2, NFT, 128, KC, 128])
    din("ffd", [2, KC, 128, NFT, 128])
    din("b31", [128, 8])
    if has1:
        din("dwin", [72, 128, KC, 128])
        din("dwout", [KC, 128, KC, 128])
        din("tab1", [8, 128, 5, 128])
    if has0:
        din("abw", [34, 128, KC, 128])
        din("abtok", [2, 128, KC, 140])
        din("abwout", [KC, 128, KC, 128])
        din("tab0", [8, 128, 2, 128])
        din("m4", [128, 128])
        din("cmpt", [8, 128, 512])
        dr["cscr"] = nc.dram_tensor("cscr", [8, 128, 2, 512], BF16).ap()
        din("w1kv", [128, 32, 256])
        din("w2kv", [128, 2, 192])
        din("poskv", [128, 32])
        din("gngb", [128, 2, 4])
        din("rope", [128, 2, S])
        din("retc", [128, 4 * 2 + 128 + 4 * 128])
        din("selc", [128, 2, 8, 32])
        din("ovl", [127, 32])
        din("exm", [128, 16, 128])
    outT = nc.dram_tensor("outT", [nseq, 128, KC, S], F32, kind="ExternalOutput").ap()
    k.dr = dr

    with ExitStack() as st:
        P = Prog(nc, st)
        k.P = P
        ps = [st.enter_context(nc.psum_tensor("ps%d" % i, [128, 512], F32)) for i in range(8)]
        k.ps = ps
        per = Arena(nc, st, "persist", 65536 + 32768 + 45056 + 8192)
        hT = per.alloc([KC, S], F32)
        uT = per.alloc([KC, S], BF16)
        scrA = per.alloc([45056 // 2], BF16)
        k.hT, k.uT = hT, uT
        k.yT = scrA[:, 0:KC * S].rearrange("p (c t) -> p c t", c=KC)
        k.aT = scrA[:, 0:NFT * 1024].rearrange("p (c t) -> p c t", c=NFT)
        k.scr_tail = (scrA, KC * S)
        mod = per.alloc([2, 48, 4], F32)
        lng = per.alloc([2, 2, KC], F32)
        lnb = per.alloc([2, 2, KC], F32)
        adab = per.alloc([2, 48], F32)
        ones32 = per.alloc([128], F32)
        onesbf = per.alloc([128], BF16)
        identf = per.alloc([128], F32)
        ident = per.alloc([128], BF16)
        b31 = per.alloc([8], F32)
        cT = per.alloc([KC, 4], F32)
        k.hc = per.alloc([4], F32)
        k.ones128 = per.alloc([128], F32)
        k.onesb1k = per.alloc([128], BF16)
        k.hc_done = False
        scT = per.alloc([KC, 4], BF16)
        k.mod, k.lng, k.lnb, k.ones32, k.onesbf, k.ident, k.b31 = mod, lng, lnb, ones32, onesbf, ident, b31
        wk = Arena(nc, st, "work", (int(nc.sbuf_bytes_remaining) - 512) // 4 * 4)
        k.wk = wk
        print("persist bytes", per.off, "of", per.nbytes, "work", wk.nbytes)

        NW = 4
        wbuf = per_w = None
        wk.mark()
        wbuf = wk.alloc([NW, KC, 128], BF16)
        k.wbuf = wbuf
        k.wrr = 0

        def load_w(src, nsl=1):
            s0 = (k.wrr + nsl - 1) // nsl * nsl
            if s0 + nsl > NW:
                s0 = 0
            k.wrr = (s0 + nsl) % NW
            srcs = src if isinstance(src, list) else [src]
            for j, s_ in enumerate(srcs):
                P.dma("pool", wbuf[:, s0 + j], s_, writes=[("wbuf", s0 + j)])
            return s0
        k.load_w = load_w

        P.dma("sp", lng, dr["lng"], writes=["lng"])
        P.dma("sp", lnb, dr["lnb"], writes=["lnb"])
        P.dma("sp", adab, dr["adab"], writes=["adab"])
        P.dma("sp", b31, dr["b31"], writes=["b31"])
        P.dma("sp", cT, dr["cT"], writes=["cT"])
        P.op("dve", lambda e: e.memset(ones32, 1.0 / 1024), writes=["ones32"])
        P.op("dve", lambda e: e.memset(onesbf, 1.0), writes=["onesbf"])
        P.op("dve", lambda e: e.memset(k.ones128, 1.0 / 128), writes=["ones128"])
        P.op("dve", lambda e: e.memset(k.onesb1k, 1.0 / 1024), writes=["onesb1k"])
        P.op("pool", lambda e: e.memset(identf, 0.0), writes=["identf"])
        P.op("pool", lambda e: e.affine_select(out=identf, in_=identf, pattern=[[-1, 128]], compare_op=ALU.not_equal,
                                               fill=1.0, base=0, channel_multiplier=1), writes=["identf"])
        P.op("dve", lambda e: e.tensor_copy(out=ident, in_=identf), reads=["identf"], writes=["ident"])
        P.op("act", lambda e: e.activation(out=scT[:, :, 0:nseq], in_=cT[:, :, 0:nseq], func=AF.Silu), reads=["cT"], writes=["scT"])

        wk.mark()
        adabuf = wk.alloc([2, KC, 768], BF16)
        for l in layers:
            for grp in range(8):
                ab = adabuf[:, grp % 2]
                P.dma("pool", ab, dr["adaw"][l, grp], writes=[("adabuf", grp % 2)])
                bank = grp % 2
                for j in range(6):
                    ft = grp * 6 + j
                    P.mm_group([lambda e, kc=kc, j=j, ab=ab, bank=bank: e.matmul(
                        ps[bank][:, j * 4:j * 4 + nseq], lhsT=ab[:, kc, j * 128:(j + 1) * 128], rhs=scT[:, kc, 0:nseq],
                        start=(kc == 0), stop=(kc == KC - 1)) for kc in range(KC)],
                        reads=[("adabuf", grp % 2), "scT"], writes=[("ps", bank)])
                for j in range(6):
                    ft = grp * 6 + j
                    P.op("dve", lambda e, j=j, ft=ft, l=l, bank=bank: e.tensor_scalar(
                        out=mod[:, l, ft, 0:nseq], in0=ps[bank][:, j * 4:j * 4 + nseq], scalar1=adab[:, l, ft:ft + 1],
                        scalar2=None, op0=ALU.add), reads=["adab"], writes=[("ps", bank), "mod"])
            for j0 in (8, 32):
                P.op("dve", lambda e, l=l, j0=j0: e.tensor_scalar(out=mod[:, l, j0:j0 + 8, :], in0=mod[:, l, j0:j0 + 8, :],
                                                                  scalar1=1.0, scalar2=None, op0=ALU.add), writes=["mod"])
            for j0 in (16, 40):
                P.op("dve", lambda e, l=l, j0=j0: e.tensor_scalar(out=mod[:, l, j0:j0 + 8, :], in0=mod[:, l, j0:j0 + 8, :],
                                                                  scalar1=1.0 / ALPHA, scalar2=None, op0=ALU.mult), writes=["mod"])
        wk.release()
        P.barrier()

        def mv(l, j, c, b):
            return mod[:, l, j * 8 + c, b:b + 1]
        k.mv = mv

        def load_x(b_, tlo, thi):
            cs_ = slice(tlo * TB, thi * TB)
            for c in range(KC):
                P.dma("sp", hT[:, c, cs_], dr["xT"][b_, :, c, cs_], writes=[("hT", c, t) for t in range(tlo, thi)])
            l0_ = layers[0]
            for c in range(KC):
                for t in range(tlo, thi):
                    eng = "act" if (c + t) % 2 == 0 else "dve"
                    emit_affine(k, eng, uT[:, c, t * TB:(t + 1) * TB], hT[:, c, t * TB:(t + 1) * TB],
                                mv(l0_, 1, c, b_), mv(l0_, 0, c, b_), reads=[("hT", c, t), "mod"], writes=[("uT", c, t)])

        def store_out(b_, tlo, thi):
            cs_ = slice(tlo * TB, thi * TB)
            for c in range(KC):
                P.dma("sp", outT[b_, :, c, cs_], hT[:, c, cs_], reads=[("hT", c, t) for t in range(tlo, thi)])

        for b in range(nseq):
            if b == 0:
                load_x(0, 0, NTB)
            else:
                load_x(b, 2, NTB)
            for li, l in enumerate(layers):
                if l == 0:
                    mixer0(k, b)
                    wout = dr["abwout"]
                else:
                    mixer1(k, b)
                    wout = dr["dwout"]
                P.barrier()
                if DEBUG.get("stop") == "mixer":
                    for c in range(KC):
                        for t in range(NTB):
                            P.op("dve", lambda e, c=c, t=t: e.tensor_copy(out=hT[:, c, t * TB:(t + 1) * TB], in_=k.yT[:, c, t * TB:(t + 1) * TB]),
                                 reads=[("yT", c, t)], writes=[("hT", c, t)])
                    break
                nxt = None
                if li + 1 < len(layers):
                    nxt = (mv, layers[li + 1], 1, 0)
                early = None
                if li + 1 == len(layers) and not DEBUG.get("stop"):
                    def early(b=b):
                        store_out(b, 0, 2)
                        if b + 1 < nseq:
                            load_x(b + 1, 0, 2)
                post_mixer(k, l, b, wout, nxt, early)
                P.barrier()
            if DEBUG.get("stop"):
                store_out(b, 0, NTB)
            else:
                store_out(b, 2, NTB)
        for i, s_ in enumerate(P.dma_sems["sp"]):
            if P.dma_tot["sp"][i]:
                P.streams["sp"].append(("wait", ("dma", "sp", i), P.dma_tot["sp"][i]))
        print("ops", P.n_ops, {e: len(v) for e, v in P.streams.items()})
        P.replay()
    return nc


def emit_affine(k, eng, out, in_, scale_ap, bias_ap, reads, writes):
    if eng == "act":
        k.P.op("act", lambda e: e.activation(out=out, in_=in_, func=AF.Identity, scale=scale_ap, bias=bias_ap),
               reads=reads, writes=writes)
    else:
        k.P.op(eng, lambda e: e.tensor_scalar(out=out, in0=in_, scalar1=scale_ap, scalar2=bias_ap, op0=ALU.mult, op1=ALU.add),
               reads=reads, writes=writes)


def outproj(k, l, b, wout, blocks=(0, 1, 2, 3), pending=None):
    P, ps, hT, yT = k.P, k.ps, k.hT, k.yT
    for fo in range(KC):
        s0 = k.load_w(wout[fo])
        for t in blocks:
            bank = t
            P.mm_group([lambda e, kc=kc, t=t, bank=bank, s0=s0, fo=fo: e.matmul(
                ps[bank][:, :], lhsT=k.wbuf[:, s0, kc, :], rhs=yT[:, kc, t * TB:(t + 1) * TB],
                start=(kc == 0), stop=(kc == KC - 1)) for kc in range(KC)],
                reads=[("wbuf", s0)] + [("yT", kc, t) for kc in range(KC)], writes=[("ps", bank)])
            P.op("dve", lambda e, t=t, bank=bank, fo=fo: e.scalar_tensor_tensor(
                out=hT[:, fo, t * TB:(t + 1) * TB], in0=ps[bank][:, :], scalar=k.mv(l, 2, fo, b),
                in1=hT[:, fo, t * TB:(t + 1) * TB], op0=ALU.mult, op1=ALU.add),
                reads=["mod"], writes=[("ps", bank), ("hT", fo, t)])
        if pending:
            for _ in range(3):
                for g_ in list(pending):
                    try:
                        next(g_)
                    except StopIteration:
                        pending.remove(g_)


def ln_block(k, l, j, b, t, nxt):
    P, ps, hT, uT, wk = k.P, k.ps, k.hT, k.uT, k.wk
    eps = LN_EPS / (ALPHA * ALPHA)
    wk.mark()
    sq = wk.alloc([2, TB], BF16)
    xb = wk.alloc([2, TB], BF16)
    mean = wk.alloc([TB], F32)
    rstd = wk.alloc([TB], F32)
    tmp = wk.alloc([2, TB], F32)
    sl = slice(t * TB, (t + 1) * TB)
    A, B = 6, 7
    for c in range(KC):
        P.op("act", lambda e, c=c: e.activation(out=xb[:, c % 2], in_=hT[:, c, sl], func=AF.Copy),
             reads=[("hT", c, t)], writes=[("lnxb", c % 2)])
        P.mm_group([lambda e, c=c: e.matmul(ps[A][:, :], lhsT=k.onesb1k, rhs=xb[:, c % 2], start=(c == 0), stop=(c == KC - 1))],
                   reads=[("lnxb", c % 2), "onesb1k"], writes=[("ps", A)])
        P.op("act", lambda e, c=c: e.activation(out=sq[:, c % 2], in_=hT[:, c, sl], func=AF.Square),
             reads=[("hT", c, t)], writes=[("lnsq", c % 2)])
        P.mm_group([lambda e, c=c: e.matmul(ps[B][:, :], lhsT=k.onesb1k, rhs=sq[:, c % 2], start=(c == 0), stop=(c == KC - 1))],
                   reads=[("lnsq", c % 2), "onesb1k"], writes=[("ps", B)])
    P.op("act", lambda e: e.activation(out=mean, in_=ps[A][:, :], func=AF.Copy), writes=[("ps", A), "lnmean"])
    P.op("dve", lambda e: e.tensor_tensor(out=tmp[:, 0], in0=mean, in1=mean, op=ALU.mult), reads=["lnmean"], writes=[("lntmp", 0)])
    P.op("dve", lambda e: e.tensor_tensor(out=rstd, in0=ps[B][:, :], in1=tmp[:, 0], op=ALU.subtract),
         reads=[("lntmp", 0)], writes=[("ps", B), "lnrstd"])
    P.op("act", lambda e: e.activation(out=rstd, in_=rstd, func=AF.Ln, bias=eps, scale=1.0), writes=["lnrstd"])
    P.op("act", lambda e: e.activation(out=rstd, in_=rstd, func=AF.Exp, scale=-0.5), writes=["lnrstd"])
    for c in range(KC):
        tb_ = tmp[:, c % 2]
        P.op("dve", lambda e, c=c, tb_=tb_: e.tensor_tensor(out=tb_, in0=hT[:, c, sl], in1=mean, op=ALU.subtract),
             reads=[("hT", c, t), "lnmean"], writes=[("lntmp", c % 2)])
        P.op("dve", lambda e, tb_=tb_: e.tensor_tensor(out=tb_, in0=tb_, in1=rstd, op=ALU.mult),
             reads=["lnrstd"], writes=[("lntmp", c % 2)])
        emit_affine(k, "act", hT[:, c, sl], tb_, k.lng[:, l, j, c:c + 1], k.lnb[:, l, j, c:c + 1],
                    reads=[("lntmp", c % 2), "lng", "lnb"], writes=[("hT", c, t)])
        if nxt is not None:
            mvf, l2, jsc, jsh = nxt
            emit_affine(k, "act", uT[:, c, sl], hT[:, c, sl], mvf(l2, jsc, c, b), mvf(l2, jsh, c, b),
                        reads=[("hT", c, t), "mod"], writes=[("uT", c, t)])
    wk.release()


def ffn(k, l, b, nxt):
    P, ps, hT, uT, aT, wk, dr = k.P, k.ps, k.hT, k.uT, k.aT, k.wk, k.dr
    wk.mark()
    sg = wk.alloc([2, TB], F32)
    wd = wk.alloc([2, NFT, 128], BF16)
    it = 0
    for sb in range(2):
        for ft in range(NFT):
            sgw = k.load_w(dr["ffg"][l, ft])
            suw = k.load_w(dr["ffu"][l, ft])
            for bi in range(2):
                t = 2 * sb + bi
                gb, ub = it % 2, 2 + it % 2
                P.mm_group([lambda e, kc=kc, t=t, gb=gb, sgw=sgw: e.matmul(
                    ps[gb][:, :], lhsT=k.wbuf[:, sgw, kc, :], rhs=uT[:, kc, t * TB:(t + 1) * TB],
                    start=(kc == 0), stop=(kc == KC - 1)) for kc in range(KC)],
                    reads=[("wbuf", sgw)] + [("uT", kc, t) for kc in range(KC)], writes=[("ps", gb)])
                P.mm_group([lambda e, kc=kc, t=t, ub=ub, suw=suw: e.matmul(
                    ps[ub][:, :], lhsT=k.wbuf[:, suw, kc, :], rhs=uT[:, kc, t * TB:(t + 1) * TB],
                    start=(kc == 0), stop=(kc == KC - 1)) for kc in range(KC)],
                    reads=[("wbuf", suw)] + [("uT", kc, t) for kc in range(KC)], writes=[("ps", ub)])
                P.op("act", lambda e, gb=gb, it=it: e.activation(out=sg[:, it % 2], in_=ps[gb][:, :], func=AF.Silu),
                     writes=[("ps", gb), ("sg", it % 2)])
                P.op("dve", lambda e, ub=ub, it=it, ft=ft, bi=bi: e.tensor_tensor(
                    out=aT[:, ft, bi * TB:(bi + 1) * TB], in0=sg[:, it % 2], in1=ps[ub][:, :], op=ALU.mult),
                    reads=[("sg", it % 2)], writes=[("ps", ub), ("aT", ft, bi)])
                it += 1
        for fo in range(KC):
            P.dma("pool", wd[:, fo % 2], dr["ffd"][l, fo], writes=[("wd", fo % 2)])
            for bi in range(2):
                t = 2 * sb + bi
                bank = 4 + bi
                P.mm_group([lambda e, kk=kk, bi=bi, bank=bank, fo=fo: e.matmul(
                    ps[bank][:, :], lhsT=wd[:, fo % 2, kk, :], rhs=aT[:, kk, bi * TB:(bi + 1) * TB],
                    start=(kk == 0), stop=(kk == NFT - 1)) for kk in range(NFT)],
                    reads=[("wd", fo % 2)] + [("aT", kk, bi) for kk in range(NFT)], writes=[("ps", bank)])
                P.op("dve", lambda e, t=t, bank=bank, fo=fo: e.scalar_tensor_tensor(
                    out=hT[:, fo, t * TB:(t + 1) * TB], in0=ps[bank][:, :], scalar=k.mv(l, 5, fo, b),
                    in1=hT[:, fo, t * TB:(t + 1) * TB], op0=ALU.mult, op1=ALU.add),
                    reads=["mod"], writes=[("ps", bank), ("hT", fo, t)])
        for bi in range(2):
            ln_block(k, l, 1, b, 2 * sb + bi, nxt)
    wk.release()


def ln_gen(k, l, j, b, t, nxt, L):
    P, ps, hT, uT = k.P, k.ps, k.hT, k.uT
    eps = LN_EPS / (ALPHA * ALPHA)
    sq, xb, mean, rstd, tmp, A, B, sid = L["sq"], L["xb"], L["mean"], L["rstd"], L["tmp"], L["A"], L["B"], L["sid"]
    sl = slice(t * TB, (t + 1) * TB)
    K_ = lambda nm, i=None: ("ln", sid, nm, i)
    for c in range(KC):
        P.op("act", lambda e, c=c: e.activation(out=xb[:, c % 2], in_=hT[:, c, sl], func=AF.Copy),
             reads=[("hT", c, t)], writes=[K_("xb", c % 2)])
        P.mm_group([lambda e, c=c: e.matmul(ps[A][:, :], lhsT=k.onesb1k, rhs=xb[:, c % 2], start=(c == 0), stop=(c == KC - 1))],
                   reads=[K_("xb", c % 2), "onesb1k"], writes=[("ps", A)])
        P.op("act", lambda e, c=c: e.activation(out=sq[:, c % 2], in_=hT[:, c, sl], func=AF.Square),
             reads=[("hT", c, t)], writes=[K_("sq", c % 2)])
        P.mm_group([lambda e, c=c: e.matmul(ps[B][:, :], lhsT=k.onesb1k, rhs=sq[:, c % 2], start=(c == 0), stop=(c == KC - 1))],
                   reads=[K_("sq", c % 2), "onesb1k"], writes=[("ps", B)])
        yield
    P.op("act", lambda e: e.activation(out=mean, in_=ps[A][:, :], func=AF.Copy), writes=[("ps", A), K_("mean")])
    P.op("dve", lambda e: e.tensor_tensor(out=tmp[:, 0], in0=mean, in1=mean, op=ALU.mult), reads=[K_("mean")], writes=[K_("tmp", 0)])
    P.op("dve", lambda e: e.tensor_tensor(out=rstd, in0=ps[B][:, :], in1=tmp[:, 0], op=ALU.subtract),
         reads=[K_("tmp", 0)], writes=[("ps", B), K_("rstd")])
    P.op("act", lambda e: e.activation(out=rstd, in_=rstd, func=AF.Ln, bias=eps, scale=1.0), writes=[K_("rstd")])
    P.op("act", lambda e: e.activation(out=rstd, in_=rstd, func=AF.Exp, scale=-0.5), writes=[K_("rstd")])
    yield
    for c in range(KC):
        tb_ = tmp[:, c % 2]
        P.op("dve", lambda e, c=c, tb_=tb_: e.tensor_tensor(out=tb_, in0=hT[:, c, sl], in1=mean, op=ALU.subtract),
             reads=[("hT", c, t), K_("mean")], writes=[K_("tmp", c % 2)])
        P.op("dve", lambda e, tb_=tb_: e.tensor_tensor(out=tb_, in0=tb_, in1=rstd, op=ALU.mult),
             reads=[K_("rstd")], writes=[K_("tmp", c % 2)])
        emit_affine(k, "act", hT[:, c, sl], tb_, k.lng[:, l, j, c:c + 1], k.lnb[:, l, j, c:c + 1],
                    reads=[K_("tmp", c % 2), "lng", "lnb"], writes=[("hT", c, t)])
        if nxt is not None:
            mvf, l2, jsc, jsh = nxt
            emit_affine(k, "act", uT[:, c, sl], hT[:, c, sl], mvf(l2, jsc, c, b), mvf(l2, jsh, c, b),
                        reads=[("hT", c, t), "mod"], writes=[("uT", c, t)])
        yield


def _run_rr(gens):
    gens = list(gens)
    while gens:
        for g_ in list(gens):
            try:
                next(g_)
            except StopIteration:
                gens.remove(g_)


def post_mixer(k, l, b, wout, nxt, early=None):
    P, ps, hT, uT, aT, wk, dr = k.P, k.ps, k.hT, k.uT, k.aT, k.wk, k.dr
    wk.mark()
    sg = wk.alloc([2, TB], F32)
    wd = wk.alloc([2, NFT, 128], BF16)
    Ls = []
    for sid, (A, B) in enumerate(((6, 7), (4, 5))):
        Ls.append(dict(sq=wk.alloc([2, TB], BF16), xb=wk.alloc([2, TB], BF16), mean=wk.alloc([TB], F32), rstd=wk.alloc([TB], F32),
                       tmp=wk.alloc([2, TB], F32), A=A, B=B, sid=sid))
    mvf = (k.mv, l, 4, 3)
    outproj(k, l, b, wout, blocks=(0, 1))
    first = [ln_gen(k, l, 0, b, 0, mvf, Ls[0]), ln_gen(k, l, 0, b, 1, mvf, Ls[1])]
    outproj(k, l, b, wout, blocks=(2, 3), pending=first)
    _run_rr(first)
    pending = [ln_gen(k, l, 0, b, 2, mvf, Ls[0]), ln_gen(k, l, 0, b, 3, mvf, Ls[1])]
    it = 0
    for sb in range(2):
        for ft in range(NFT):
            sgw = k.load_w(dr["ffg"][l, ft])
            suw = k.load_w(dr["ffu"][l, ft])
            for bi in range(2):
                t = 2 * sb + bi
                gb, ub = it % 2, 2 + it % 2
                P.mm_group([lambda e, kc=kc, t=t, gb=gb, sgw=sgw: e.matmul(
                    ps[gb][:, :], lhsT=k.wbuf[:, sgw, kc, :], rhs=uT[:, kc, t * TB:(t + 1) * TB],
                    start=(kc == 0), stop=(kc == KC - 1)) for kc in range(KC)],
                    reads=[("wbuf", sgw)] + [("uT", kc, t) for kc in range(KC)], writes=[("ps", gb)])
                P.mm_group([lambda e, kc=kc, t=t, ub=ub, suw=suw: e.matmul(
                    ps[ub][:, :], lhsT=k.wbuf[:, suw, kc, :], rhs=uT[:, kc, t * TB:(t + 1) * TB],
                    start=(kc == 0), stop=(kc == KC - 1)) for kc in range(KC)],
                    reads=[("wbuf", suw)] + [("uT", kc, t) for kc in range(KC)], writes=[("ps", ub)])
                P.op("act", lambda e, gb=gb, it=it: e.activation(out=sg[:, it % 2], in_=ps[gb][:, :], func=AF.Silu),
                     writes=[("ps", gb), ("sg", it % 2)])
                P.op("dve", lambda e, ub=ub, it=it, ft=ft, bi=bi: e.tensor_tensor(
                    out=aT[:, ft, bi * TB:(bi + 1) * TB], in0=sg[:, it % 2], in1=ps[ub][:, :], op=ALU.mult),
                    reads=[("sg", it % 2)], writes=[("ps", ub), ("aT", ft, bi)])
                it += 1
            for g_ in list(pending):
                try:
                    next(g_)
                except StopIteration:
                    pending.remove(g_)
        _run_rr(pending)
        pending = []
        if sb == 1 and early is not None:
            early()
        for fo in range(KC):
            P.dma("pool", wd[:, fo % 2], dr["ffd"][l, fo], writes=[("wd", fo % 2)])
            for bi in range(2):
                t = 2 * sb + bi
                bank = 4 + bi
                P.mm_group([lambda e, kk=kk, bi=bi, bank=bank, fo=fo: e.matmul(
                    ps[bank][:, :], lhsT=wd[:, fo % 2, kk, :], rhs=aT[:, kk, bi * TB:(bi + 1) * TB],
                    start=(kk == 0), stop=(kk == NFT - 1)) for kk in range(NFT)],
                    reads=[("wd", fo % 2)] + [("aT", kk, bi) for kk in range(NFT)], writes=[("ps", bank)])
                P.op("dve", lambda e, t=t, bank=bank, fo=fo: e.scalar_tensor_tensor(
                    out=hT[:, fo, t * TB:(t + 1) * TB], in0=ps[bank][:, :], scalar=k.mv(l, 5, fo, b),
                    in1=hT[:, fo, t * TB:(t + 1) * TB], op0=ALU.mult, op1=ALU.add),
                    reads=["mod"], writes=[("ps", bank), ("hT", fo, t)])
        lns = [ln_gen(k, l, 1, b, 2 * sb, nxt, Ls[0]), ln_gen(k, l, 1, b, 2 * sb + 1, nxt, Ls[1])]
        if sb == 0:
            pending = lns
        else:
            _run_rr(lns)
    wk.release()


def mixer1(k, b):
    P, ps, uT, yT, wk, dr = k.P, k.ps, k.uT, k.yT, k.wk, k.dr
    wk.mark()
    vtok = wk.alloc([3, 16, 128], BF16)
    qk = wk.alloc([2, 2, S], BF16)
    tab = wk.alloc([2, 5, 128], F32)
    ssb = wk.alloc([2, 256], F32)
    pT = wk.alloc([3, 256], BF16)
    numacc = k.scr_tail[0][:, k.scr_tail[1]:k.scr_tail[1] + 2 * S].bitcast(F32)
    denacc = wk.alloc([S], F32)
    qscale = 128.0 ** -0.5
    SB = (0, 1)
    NB = (2, 3)
    DB = (4, 5)
    PB = (6, 7)
    cnt = {"s": 0, "p": 0, "pb": 0, "vs": 0}
    vstage = wk.alloc([1, S], BF16)

    def vproj_pieces(hd, g):
        win, dl = DIL[g]
        tpc = (S // dl) // 128
        out = []
        state = {}
        vs = 0
        vT = vstage[:, vs]

        for t in range(NTB):
            def piece(t=t):
                if t == 0:
                    state["s0"] = k.load_w(dr["dwin"][g * 24 + 16 + hd])
                s0 = state["s0"]
                bank = PB[cnt["pb"] % 2]
                cnt["pb"] += 1
                P.mm_group([lambda e, kc=kc: e.matmul(ps[bank][:, :], lhsT=k.wbuf[:, s0, kc, :], rhs=uT[:, kc, t * TB:(t + 1) * TB],
                                                     start=(kc == 0), stop=(kc == KC - 1)) for kc in range(KC)],
                           reads=[("wbuf", s0)] + [("uT", kc, t) for kc in range(KC)], writes=[("ps", bank)])
                eng = "act" if t % 2 == 0 else "dve"
                if eng == "act":
                    P.op("act", lambda e: e.activation(out=vT[:, t * TB:(t + 1) * TB], in_=ps[bank][:, :], func=AF.Copy),
                         writes=[("ps", bank), ("vstage", vs)])
                else:
                    P.op("dve", lambda e: e.tensor_copy(out=vT[:, t * TB:(t + 1) * TB], in_=ps[bank][:, :]),
                         writes=[("ps", bank), ("vstage", vs)])
            out.append(piece)
        tiles = [(r, kt) for r in range(dl) for kt in range(tpc)]
        for q4 in range(4):
            def piece(q4=q4):
                bank = PB[cnt["pb"] % 2]
                cnt["pb"] += 1
                psT = ps[bank][:, :].bitcast(BF16)
                fns = []
                for j in range(4):
                    r, kt = tiles[q4 * 4 + j]
                    src = _tok_ap(vT, 128 * kt, 128, dl, r)
                    fns.append(lambda e, j=j, src=src: e.transpose(out=psT[:, j * 128:(j + 1) * 128], in_=src, identity=k.ident))
                P.mm_group(fns, reads=[("vstage", vs), "ident"], writes=[("ps", bank)])
                dst = vtok[:, g, q4 * 4:q4 * 4 + 4, :]
                P.op("act", lambda e: e.activation(out=dst, in_=psT[:, 0:512].rearrange("p (a b) -> p a b", a=4), func=AF.Copy),
                     writes=[("ps", bank), ("vtok", g)])
            out.append(piece)
        return out

    def qkproj_pieces(hd, g, st_):
        out = []
        state = {}
        for m in range(2):
            for t in range(NTB):
                def piece(m=m, t=t):
                    if t == 0:
                        state["s0"] = k.load_w(dr["dwin"][g * 24 + m * 8 + hd])
                    s0 = state["s0"]
                    bank = PB[cnt["pb"] % 2]
                    cnt["pb"] += 1
                    P.mm_group([lambda e, kc=kc: e.matmul(
                        ps[bank][:, :], lhsT=k.wbuf[:, s0, kc, :], rhs=uT[:, kc, t * TB:(t + 1) * TB],
                        start=(kc == 0), stop=(kc == KC - 1)) for kc in range(KC)],
                        reads=[("wbuf", s0)] + [("uT", kc, t) for kc in range(KC)], writes=[("ps", bank)])
                    dst = qk[:, st_, m, t * TB:(t + 1) * TB]
                    if m == 0:
                        P.op("act", lambda e: e.activation(out=dst, in_=ps[bank][:, :], func=AF.Copy, scale=qscale),
                             writes=[("ps", bank), ("qk", st_, 0)])
                    else:
                        P.op("dve", lambda e: e.tensor_copy(out=dst, in_=ps[bank][:, :]), writes=[("ps", bank), ("qk", st_, 1)])
                out.append(piece)
        return out

    units = [(hd, g) for hd in range(8) for g in range(3)]
    for g in range(3):
        for pc in vproj_pieces(0, g):
            pc()
    for pc in qkproj_pieces(0, 0, 0):
        pc()
    P.dma("sp", tab[:, 0], dr["tab1"][0], writes=[("tab", 0)])

    for ui, (hd, g) in enumerate(units):
        win, dl = DIL[g]
        tpc = (S // dl) // 128
        st_ = ui % 2
        qT = qk[:, st_, 0]
        kT = qk[:, st_, 1]
        tb = tab[:, hd % 2]
        filler = []
        if ui + 1 < len(units):
            nh, ng = units[ui + 1]
            filler += qkproj_pieces(nh, ng, (ui + 1) % 2)
        if g == 0:
            filler += vproj_pieces(hd, 2) if hd > 0 else []
            if hd + 1 < 8:
                P.dma("sp", tab[:, (hd + 1) % 2], dr["tab1"][hd + 1], writes=[("tab", (hd + 1) % 2)])
        if g == 2 and hd + 1 < 8:
            filler += vproj_pieces(hd + 1, 0) + vproj_pieces(hd + 1, 1)
        items = [(r, kt) for r in range(dl) for kt in range(tpc)]
        per_item = -(-len(filler) // len(items)) if filler else 0

        def stage_a(it):
            r, kt = it
            nq = 2 if kt + 1 < tpc else 1
            si = cnt["s"] % 2
            cnt["s"] += 1
            sb_ = SB[si]
            lk = _tok_ap(kT, 128 * kt, 128, dl, r)
            rq = _tok_ap(qT, 128 * kt, 128 * nq, dl, r)
            P.mm_group([lambda e: e.matmul(ps[sb_][:, 0:128 * nq], lhsT=lk, rhs=rq, start=True, stop=True)],
                       reads=[("qk", st_, 0), ("qk", st_, 1)], writes=[("ps", sb_)])
            return dict(si=si, sb=sb_, nq=nq)

        def stage_b(it, sd):
            si, sb_, nq = sd["si"], sd["sb"], sd["nq"]
            t0 = 4 if g == 2 else 2 * g
            tsl = tb[:, t0:t0 + nq, :].rearrange("p a b -> p (a b)")
            P.op("dve", lambda e: e.tensor_tensor(out=ssb[:, si, 0:128 * nq], in0=ps[sb_][:, 0:128 * nq], in1=tsl, op=ALU.add),
                 reads=[("tab", hd % 2)], writes=[("ps", sb_), ("ssb", si)])
            pi = cnt["p"] % 3
            cnt["p"] += 1
            P.op("act", lambda e: e.activation(out=pT[:, pi, 0:128 * nq], in_=ssb[:, si, 0:128 * nq], func=AF.Exp),
                 reads=[("ssb", si)], writes=[("pT", pi)])
            sd["pi"] = pi

        def stage_c(it, sd):
            r, kt = it
            pi, nq = sd["pi"], sd["nq"]
            ti = r * tpc + kt
            vt = vtok[:, g, ti, :]
            nb, db = NB[kt % 2], DB[kt % 2]
            P.mm_group([lambda e: e.matmul(ps[nb][:, 0:128], lhsT=vt, rhs=pT[:, pi, 0:128], start=(kt == 0), stop=True)],
                       reads=[("vtok", g), ("pT", pi)], writes=[("ps", nb)])
            P.mm_group([lambda e: e.matmul(ps[db][:, 0:128], lhsT=k.onesbf, rhs=pT[:, pi, 0:128], start=(kt == 0), stop=True)],
                       reads=["onesbf", ("pT", pi)], writes=[("ps", db)])
            na = _tok_ap(numacc, 128 * kt, 128, dl, r)
            da = _tok_ap(denacc, 128 * kt, 128, dl, r)
            blks = [kt // 4] if g == 0 else list(range(NTB))
            nk = [("numacc", tt) for tt in blks]
            dk = [("denacc", tt) for tt in blks]
            if g == 0:
                P.op("act", lambda e: e.activation(out=na, in_=ps[nb][:, 0:128], func=AF.Copy), writes=[("ps", nb)] + nk)
                P.op("act", lambda e: e.activation(out=da, in_=ps[db][:, 0:128], func=AF.Copy), writes=[("ps", db)] + dk)
            else:
                P.op("dve", lambda e: e.tensor_tensor(out=na, in0=ps[nb][:, 0:128], in1=na, op=ALU.add), writes=[("ps", nb)] + nk)
                P.op("dve", lambda e: e.tensor_tensor(out=da, in0=ps[db][:, 0:128], in1=da, op=ALU.add), writes=[("ps", db)] + dk)
            if nq == 2:
                nb2, db2 = NB[(kt + 1) % 2], DB[(kt + 1) % 2]
                P.mm_group([lambda e: e.matmul(ps[nb2][:, 0:128], lhsT=vt, rhs=pT[:, pi, 128:256], start=True, stop=False)],
                           reads=[("vtok", g), ("pT", pi)], writes=[("ps", nb2)])
                P.mm_group([lambda e: e.matmul(ps[db2][:, 0:128], lhsT=k.onesbf, rhs=pT[:, pi, 128:256], start=True, stop=False)],
                           reads=["onesbf", ("pT", pi)], writes=[("ps", db2)])

        sds = {0: stage_a(items[0])}
        fi = 0
        for ii, it in enumerate(items):
            if ii + 1 < len(items):
                sds[ii + 1] = stage_a(items[ii + 1])
            for _ in range(per_item):
                if fi < len(filler):
                    filler[fi]()
                    fi += 1
            stage_b(it, sds[ii])
            stage_c(it, sds[ii])
            del sds[ii]
        while fi < len(filler):
            filler[fi]()
            fi += 1
        if g == 2:
            for t in range(NTB):
                sl = slice(t * TB, (t + 1) * TB)
                P.op("act", lambda e, sl=sl: e.activation(out=denacc[:, sl], in_=denacc[:, sl], func=AF.Ln), writes=[("denacc", t)])
                P.op("act", lambda e, sl=sl: e.activation(out=denacc[:, sl], in_=denacc[:, sl], func=AF.Exp, scale=-1.0), writes=[("denacc", t)])
                P.op("dve", lambda e, sl=sl, hd=hd: e.tensor_tensor(out=yT[:, hd, sl], in0=numacc[:, sl], in1=denacc[:, sl], op=ALU.mult),
                     reads=[("numacc", t), ("denacc", t)], writes=[("yT", hd, t)])
    wk.release()


def _shared_inputs(inp, has0, has1):
    f = lambda a: np.ascontiguousarray(np.asarray(a, dtype=np.float32))
    sh = {}
    ada_w = f(inp["ada_w"])
    sh["adaw"] = np.stack([_wtiles(ada_w[l], [np.arange(g * 768, (g + 1) * 768) for g in range(8)]) for l in range(2)], 0)
    sh["adab"] = f(f(inp["ada_b"]).reshape(2, 48, 128).transpose(2, 0, 1))
    sh["lng"] = f(f(inp["ln_g"]).reshape(2, 2, KC, 128).transpose(3, 0, 1, 2))
    sh["lnb"] = f(f(inp["ln_b"]).reshape(2, 2, KC, 128).transpose(3, 0, 1, 2))
    t128 = lambda M: [np.arange(i * 128, (i + 1) * 128) for i in range(M // 128)]
    sh["ffg"] = np.stack([_wtiles(f(inp["ffn_w_gate"])[l], t128(DFF)) for l in range(2)], 0)
    sh["ffu"] = np.stack([_wtiles(f(inp["ffn_w_up"])[l], t128(DFF)) for l in range(2)], 0)
    sh["ffd"] = np.stack([_wtiles(f(inp["ffn_w_down"])[l], t128(D)) for l in range(2)], 0)
    rel_bias = f(inp["rel_bias"])
    ct = _const_tables(rel_bias)
    sh["b31"] = ct["b31"]
    if has1:
        sh["dwin"] = _wtiles(f(inp["dil_w_in"])[0], t128(9216))
        sh["dwout"] = _wtiles(f(inp["dil_w_out"])[0], t128(D))
        sh["tab1"] = ct["tab1"]
    if has0:
        sh.update(_layer0_inputs(inp, ct))
    return sh


def _layer0_inputs(inp, ct):
    f = lambda a: np.ascontiguousarray(np.asarray(a, dtype=np.float32))
    sh = {}
    W = f(inp["ab_w_in"])[0]
    ar = np.arange
    tiles = []
    sw = np.concatenate([ar(64, 128), ar(0, 64)])
    for hh in range(4):
        tiles += [128 * hh + ar(128), 128 * hh + sw, 512 + 128 * hh + ar(128), 512 + 128 * hh + sw,
                  1536 + 128 * hh + ar(128), 1024 + 128 * hh + ar(128)]
    for g in range(2):
        kc_, vc_ = 2560 + g * 64 + ar(64), 2688 + g * 64 + ar(64)
        ks_, kw_ = 2816 + g * 64 + ar(64), 3072 + g * 64 + ar(64)
        tiles += [2048 + g * 256 + ar(128), 2048 + g * 256 + 128 + ar(128), np.concatenate([kc_, vc_]),
                  np.concatenate([ks_, ks_]), np.concatenate([kw_, kw_])]
    sh["abw"] = _wtiles(W, tiles)
    tok = []
    for g in range(2):
        tok.append(np.concatenate([2944 + g * 64 + ar(64), 3200 + g * 64 + ar(64), 3328 + g * 12 + ar(12)]))
    sh["abtok"] = _wtiles(W, tok)
    sh["abwout"] = _wtiles(f(inp["ab_w_out"])[0], [ar(i * 128, (i + 1) * 128) for i in range(8)])
    sh["tab0"] = ct["tab0"]
    sh["m4"] = ct["m4"]
    sh["cmpt"] = ct["cmpt"]
    w1k = f(inp["cmp_k_w1"])[0].reshape(32, 64, 256).transpose(1, 0, 2)
    w1v = f(inp["cmp_v_w1"])[0].reshape(32, 64, 256).transpose(1, 0, 2)
    sh["w1kv"] = f(np.concatenate([w1k, w1v], 0))
    w2k = f(inp["cmp_k_w2"])[0].reshape(2, 128, 64).transpose(1, 0, 2)
    w2v = f(inp["cmp_v_w2"])[0].reshape(2, 128, 64).transpose(1, 0, 2)
    sh["w2kv"] = f(np.concatenate([w2k, w2k, w2v], 2))
    sh["poskv"] = f(np.concatenate([f(inp["cmp_pos_k"])[0].T, f(inp["cmp_pos_v"])[0].T], 0))
    sh["gngb"] = f(np.stack([f(inp["ret_gn_g"])[0].reshape(4, 128).T, f(inp["ret_gn_b"])[0].reshape(4, 128).T], 1))
    inv = (np.float32(10000.0) ** (-(np.arange(0, 128, 2, dtype=np.float32)) / np.float32(128))).astype(np.float32)
    ang = (np.arange(S, dtype=np.float32)[None, :] * inv[:, None]).astype(np.float32)
    cos = np.cos(ang.astype(np.float64)).astype(np.float32)
    sin = np.sin(ang.astype(np.float64)).astype(np.float32)
    rope = np.zeros((128, 2, S), np.float32)
    rope[0:64, 0], rope[64:128, 0] = cos, cos
    rope[0:64, 1], rope[64:128, 1] = -sin, sin
    sh["rope"] = rope
    gam = 1.0 - 2.0 ** (-5.0 - np.arange(4, dtype=np.float64))
    kk = np.arange(128, dtype=np.float64)
    retc = np.zeros((128, 8 + 128 + 512), np.float64)
    for h in range(4):
        retc[:, h] = gam[h] ** (-(kk + 1))
        retc[:, 4 + h] = gam[h] ** (127 - kk)
        retc[:, 136 + 128 * h:136 + 128 * (h + 1)] = (gam[h] ** (kk + 1) * 128.0 ** -0.5)[None, :]
    retc[:, 8:136] = (kk[None, :] >= kk[:, None]).astype(np.float64)
    sh["retc"] = retc.astype(np.float32)
    selc = np.zeros((128, 2, 8, 32), np.float32)
    jj = np.arange(32)[None, :]
    for i in range(8):
        q = (8 + i) * 128 + np.arange(128)
        bq = (q // 64)[:, None]
        forced = (jj == 0) | (jj == bq) | (jj == bq - 1)
        valid = jj <= bq
        selc[:, 0, i] = (valid & ~forced).astype(np.float32)
        selc[:, 1, i] = np.where(forced, 1e4, np.where(valid, 0.0, -1.0)).astype(np.float32)
    sh["selc"] = selc
    cs = np.arange(127) * 16
    ss = np.arange(32) * 64
    ov = np.clip(np.minimum(cs[:, None] + 32, ss[None, :] + 64) - np.maximum(cs[:, None], ss[None, :]), 0, None) / 32.0
    sh["ovl"] = ov.astype(np.float32)
    ex = np.zeros((128, 16, 128), np.float32)
    for kt in range(16):
        for kq in range(128):
            ex[2 * kt + kq // 64, kt, kq] = 1.0
    sh["exm"] = ex
    return sh


_PROG_CACHE = {}


def run_layers(inp, layers, nseq, ncores):
    has0, has1 = 0 in layers, 1 in layers
    key = (nseq, tuple(layers))
    if key not in _PROG_CACHE:
        _PROG_CACHE[key] = build_program(nseq, list(layers), has0, has1)
    nc = _PROG_CACHE[key]
    sh = _shared_inputs(inp, has0, has1)
    x = np.asarray(inp["x"], dtype=np.float32)
    c = np.asarray(inp["c"], dtype=np.float32)
    in_maps = []
    for i in range(ncores):
        xs = x[i * nseq:(i + 1) * nseq]
        xT = np.ascontiguousarray(xs.reshape(nseq, S, KC, 128).transpose(0, 3, 2, 1))
        cs = c[i * nseq:(i + 1) * nseq]
        cT = np.zeros((128, KC, 4), np.float32)
        cT[:, :, 0:nseq] = cs.reshape(nseq, KC, 128).transpose(2, 1, 0)
        m = dict(sh)
        m["xT"] = xT
        m["cT"] = cT
        in_maps.append(m)
    res = run_bass_kernel_spmd(nc, in_maps, core_ids=list(range(ncores)))
    outs = []
    for i in range(ncores):
        oT = res.results[i]["outT"]
        outs.append(np.ascontiguousarray(oT.transpose(0, 3, 2, 1)).reshape(nseq, S, D))
    return np.concatenate(outs, 0).astype(np.float32)


def kernel(**inputs):
    return run_layers(inputs, (0, 1), 4, NCORES)


def _proj_fm(k, wsrc, evac):
    P, ps, uT = k.P, k.ps, k.uT
    s0 = k.load_w(wsrc)
    for t in range(NTB):
        bank = 6 + k.pbc % 2
        k.pbc += 1
        P.mm_group([lambda e, kc=kc, t=t, bank=bank, s0=s0: e.matmul(
            ps[bank][:, :], lhsT=k.wbuf[:, s0, kc, :], rhs=uT[:, kc, t * TB:(t + 1) * TB],
            start=(kc == 0), stop=(kc == KC - 1)) for kc in range(KC)],
            reads=[("wbuf", s0)] + [("uT", kc, t) for kc in range(KC)], writes=[("ps", bank)])
        evac(t, bank)


def mixer0(k, b):
    k.pbc = 0
    retention(k, b)
    k.P.barrier()
    if DEBUG.get('ret'):
        return
    nsa(k, b)


def retention(k, b):
    P, ps, uT, yT, wk, dr = k.P, k.ps, k.uT, k.yT, k.wk, k.dr
    wk.mark()
    rope = wk.alloc([2, S], F32)
    qk = wk.alloc([2, S], BF16)
    vtok = wk.alloc([16, 128], BF16)
    t12 = wk.alloc([2, 2, TB], F32)
    retc = wk.alloc([648], F32)
    gngb = wk.alloc([2, 4], F32)
    scsb = wk.alloc([2, 128], BF16)
    kd = wk.alloc([2, 128], BF16)
    prev32 = wk.alloc([128], F32)
    prevbf = wk.alloc([2, 128], BF16)
    sq = wk.alloc([TB], F32)
    mean = wk.alloc([TB], F32)
    rstd = wk.alloc([TB], F32)
    tmp = wk.alloc([TB], F32)
    tail, toff = k.scr_tail
    yraw = tail[:, toff:toff + 2 * S].bitcast(F32)
    gr = tail[:, toff + 2 * S:toff + 3 * S]
    P.dma("sp", rope, dr["rope"], writes=["rope"])
    P.dma("sp", retc, dr["retc"], writes=["retc"])
    P.dma("sp", gngb, dr["gngb"], writes=["gngb"])
    qT, kT = qk[:, 0], qk[:, 1]
    tri = retc[:, 8:136]
    gam = [1.0 - 2.0 ** (-5.0 - h) for h in range(4)]

    def pbank():
        bank = 6 + k.pbc % 2
        k.pbc += 1
        return bank

    def proj_mm(s0, t, bank):
        P.mm_group([lambda e, kc=kc: e.matmul(ps[bank][:, :], lhsT=k.wbuf[:, s0, kc, :], rhs=uT[:, kc, t * TB:(t + 1) * TB],
                                             start=(kc == 0), stop=(kc == KC - 1)) for kc in range(KC)],
                   reads=[("wbuf", s0)] + [("uT", kc, t) for kc in range(KC)], writes=[("ps", bank)])

    def p_qkv(hh):
        for m in range(2):
            dst = qk[:, m]
            s_main = k.load_w(dr["abw"][hh * 6 + 2 * m])
            s_swap = k.load_w(dr["abw"][hh * 6 + 2 * m + 1])
            for t in range(NTB):
                sl = slice(t * TB, (t + 1) * TB)
                b0 = pbank()
                proj_mm(s_main, t, b0)
                t1 = t12[:, 0, t % 2]
                t2 = t12[:, 1, t % 2]
                P.op("dve", lambda e, b0=b0, t1=t1, sl=sl: e.tensor_tensor(out=t1, in0=ps[b0][:, :], in1=rope[:, 0, sl], op=ALU.mult),
                     reads=["rope"], writes=[("ps", b0), ("t1", t % 2)])
                b1 = pbank()
                proj_mm(s_swap, t, b1)
                P.op("dve", lambda e, b1=b1, t2=t2, sl=sl: e.tensor_tensor(out=t2, in0=ps[b1][:, :], in1=rope[:, 1, sl], op=ALU.mult),
                     reads=["rope"], writes=[("ps", b1), ("t2", t % 2)])
                P.op("dve", lambda e, dst=dst, t1=t1, t2=t2, sl=sl: e.tensor_tensor(out=dst[:, sl], in0=t1, in1=t2, op=ALU.add),
                     reads=[("t1", t % 2), ("t2", t % 2)], writes=[("rqk", m)])
                yield
        s0 = k.load_w(dr["abw"][hh * 6 + 5])
        for ti in range(16):
            qd = ti % 4
            if qd == 0:
                bank = pbank()
            P.mm_group([lambda e, kc=kc, ti=ti, bank=bank, qd=qd: e.matmul(
                ps[bank][:, qd * 128:(qd + 1) * 128], lhsT=uT[:, kc, ti * 128:(ti + 1) * 128], rhs=k.wbuf[:, s0, kc, :],
                start=(kc == 0), stop=(kc == KC - 1)) for kc in range(KC)],
                reads=[("wbuf", s0)] + [("uT", kc, ti // 4) for kc in range(KC)], writes=[("ps", bank)])
            if qd == 3:
                P.op("act", lambda e, ti=ti, bank=bank: e.activation(
                    out=vtok[:, ti - 3:ti + 1, :], in_=ps[bank][:, :].rearrange("p (a b) -> p a b", a=4), func=AF.Copy),
                    writes=[("ps", bank), "rvtok"])
                yield

    def p_g(hh):
        s0 = k.load_w(dr["abw"][hh * 6 + 4])
        for t in range(NTB):
            bank = pbank()
            proj_mm(s0, t, bank)
            P.op("act", lambda e, t=t, bank=bank: e.activation(out=gr[:, t * TB:(t + 1) * TB], in_=ps[bank][:, :], func=AF.Silu),
                 writes=[("ps", bank), "rgr"])
            yield

    def chunks(hh, filler):
        SBK = (0, 4)
        TBK = (2, 5)
        psTs = [ps[bk][:, :].bitcast(BF16) for bk in TBK]

        def st_a(n):
            cs = slice(n * 128, (n + 1) * 128)
            sbk = SBK[n % 2]
            P.mm_group([lambda e: e.matmul(ps[sbk][:, 0:128], lhsT=kT[:, cs], rhs=qT[:, cs], start=True, stop=True)],
                       reads=[("rqk", 0), ("rqk", 1)], writes=[("ps", sbk)])
            P.op("dve", lambda e: e.scalar_tensor_tensor(out=scsb[:, n % 2], in0=ps[sbk][:, 0:128], scalar=retc[:, hh:hh + 1],
                                                         in1=tri, op0=ALU.mult, op1=ALU.mult),
                 reads=["retc"], writes=[("ps", sbk), ("scsb", n % 2)])
            if n < 15:
                tbk = TBK[n % 2]
                psT = psTs[n % 2]
                P.mm_group([lambda e: e.transpose(out=psT[:, 0:128], in_=kT[:, cs], identity=k.ident)],
                           reads=[("rqk", 1), "ident"], writes=[("ps", tbk)])
                P.op("act", lambda e: e.activation(out=kd[:, n % 2], in_=psT[:, 0:128], func=AF.Identity,
                                                   scale=retc[:, 4 + hh:5 + hh], bias=0.0),
                     reads=["retc"], writes=[("ps", tbk), ("kd", n % 2)])

        def st_b(n):
            cs = slice(n * 128, (n + 1) * 128)
            fns = [lambda e: e.matmul(ps[1][:, 0:128], lhsT=vtok[:, n, :], rhs=scsb[:, n % 2], start=True, stop=(n == 0))]
            rd = ["rvtok", ("scsb", n % 2)]
            if n > 0:
                fns.append(lambda e: e.matmul(ps[1][:, 0:128], lhsT=prevbf[:, n % 2], rhs=qT[:, cs], start=False, stop=True))
                rd += [("prevbf", n % 2), ("rqk", 0)]
            P.mm_group(fns, reads=rd, writes=[("ps", 1)])
            P.op("dve", lambda e: e.tensor_tensor(out=yraw[:, cs], in0=ps[1][:, 0:128],
                                                  in1=retc[:, 136 + 128 * hh:136 + 128 * (hh + 1)], op=ALU.mult),
                 reads=["retc"], writes=[("ps", 1), "yraw"])
            if n < 15:
                P.mm_group([lambda e: e.matmul(ps[3][:, 0:128], lhsT=kd[:, n % 2], rhs=vtok[:, n, :], start=True, stop=True)],
                           reads=[("kd", n % 2), "rvtok"], writes=[("ps", 3)])
                if n == 0:
                    P.op("dve", lambda e: e.tensor_copy(out=prev32, in_=ps[3][:, 0:128]), writes=[("ps", 3), "prev32"])
                else:
                    gC = float(gam[hh] ** 128)
                    P.op("dve", lambda e: e.scalar_tensor_tensor(out=prev32, in0=prev32, scalar=gC, in1=ps[3][:, 0:128],
                                                                 op0=ALU.mult, op1=ALU.add),
                         writes=[("ps", 3), "prev32"])
                P.op("act", lambda e: e.activation(out=prevbf[:, (n + 1) % 2], in_=prev32, func=AF.Copy),
                     reads=["prev32"], writes=[("prevbf", (n + 1) % 2)])

        st_a(0)
        for n in range(16):
            if n + 1 < 16:
                st_a(n + 1)
            st_b(n)
            if filler is not None and n % 4 == 1:
                try:
                    next(filler)
                except StopIteration:
                    filler = None
        if filler is not None:
            for _ in filler:
                pass

    def gnorm(hh):
        for t in range(NTB):
            sl = slice(t * TB, (t + 1) * TB)
            P.op("act", lambda e, sl=sl: e.activation(out=sq, in_=yraw[:, sl], func=AF.Square), reads=["yraw"], writes=["rsq"])
            P.mm_group([lambda e, sl=sl: e.matmul(ps[4][:, :], lhsT=k.ones128, rhs=yraw[:, sl], start=True, stop=True)],
                       reads=["yraw", "ones128"], writes=[("ps", 4)])
            P.mm_group([lambda e: e.matmul(ps[5][:, :], lhsT=k.ones128, rhs=sq, start=True, stop=True)],
                       reads=["rsq", "ones128"], writes=[("ps", 5)])
            yield
            P.op("act", lambda e: e.activation(out=mean, in_=ps[4][:, :], func=AF.Copy), writes=[("ps", 4), "rmean"])
            P.op("dve", lambda e: e.tensor_tensor(out=tmp, in0=mean, in1=mean, op=ALU.mult), reads=["rmean"], writes=["rtmp"])
            P.op("dve", lambda e: e.tensor_tensor(out=rstd, in0=ps[5][:, :], in1=tmp, op=ALU.subtract), reads=["rtmp"], writes=[("ps", 5), "rrstd"])
            P.op("act", lambda e: e.activation(out=rstd, in_=rstd, func=AF.Ln, bias=LN_EPS, scale=1.0), writes=["rrstd"])
            P.op("act", lambda e: e.activation(out=rstd, in_=rstd, func=AF.Exp, scale=-0.5), writes=["rrstd"])
            yield
            P.op("dve", lambda e, sl=sl: e.tensor_tensor(out=tmp, in0=yraw[:, sl], in1=mean, op=ALU.subtract), reads=["yraw", "rmean"], writes=["rtmp"])
            P.op("dve", lambda e: e.tensor_tensor(out=tmp, in0=tmp, in1=rstd, op=ALU.mult), reads=["rrstd"], writes=["rtmp"])
            emit_affine(k, "act", tmp, tmp, gngb[:, 0, hh:hh + 1], gngb[:, 1, hh:hh + 1], reads=["gngb"], writes=["rtmp"])
            P.op("dve", lambda e, sl=sl: e.tensor_tensor(out=yT[:, hh, sl], in0=tmp, in1=gr[:, sl], op=ALU.mult),
                 reads=["rtmp", "rgr"], writes=[("yT", hh, t)])
            yield

    dbg = DEBUG.get("ret")
    for _ in p_qkv(0):
        pass
    for hh in range(4):
        chunks(hh, p_g(hh))
        if dbg:
            for t in range(NTB):
                sl = slice(t * TB, (t + 1) * TB)
                P.op("dve", lambda e, sl=sl, hh=hh: e.tensor_copy(out=yT[:, hh, sl], in_=yraw[:, sl]), reads=["yraw"], writes=[("yT", hh, t)])
                P.op("dve", lambda e, sl=sl, hh=hh: e.tensor_copy(out=yT[:, 4 + hh, sl], in_=gr[:, sl]), reads=["rgr"], writes=[("yT", 4 + hh, t)])
            if hh + 1 < 4:
                for _ in p_qkv(hh + 1):
                    pass
            continue
        gens = [gnorm(hh)]
        if hh + 1 < 4:
            gens.append(p_qkv(hh + 1))
        _run_rr(gens)
    wk.release()


def nsa(k, b):
    P, ps, uT, yT, wk, dr = k.P, k.ps, k.uT, k.yT, k.wk, k.dr
    wk.mark()
    qn = wk.alloc([2, S], BF16)
    kcvc = wk.alloc([S], BF16)
    kslc = wk.alloc([S], BF16)
    kwin = wk.alloc([S], BF16)
    vaug = wk.alloc([16, 2, 65], BF16)
    sig = wk.alloc([16, 12], F32)
    shared = wk.alloc([2240], BF16)
    wtok = shared[:, 0:1120].rearrange('p (a b) -> p a b', a=KC)
    hidraw = wk.alloc([512], BF16)
    hid = hidraw[:, 0:508].rearrange('p (a b c) -> p a b c', a=2, b=2)
    kcT = wk.alloc([127], BF16)
    vcaug = wk.alloc([97], BF16)
    ovl32 = wk.alloc([32], F32)
    w1buf = shared[:, 1120:2144].rearrange('p (a b c) -> p a b c', a=2, b=2)
    ctab = shared[:, 0:2048].rearrange('p (a b c) -> p a b c', a=2, b=2)
    zpad = wk.alloc([224], BF16)
    w2 = wk.alloc([2, 192], BF16)
    posb = wk.alloc([32], BF16)
    cmpbias = wk.alloc([TB], F32)
    tab = cmpbias[:, 0:256].rearrange('p (a b) -> p a b', a=2)
    tabtmp = cmpbias[:, 256:512].rearrange('p (a b) -> p a b', a=2)
    Ocmp = wk.alloc([4, 4, 97], F32)
    Oraw = wk.alloc([4, 2, 65], F32)
    tabhl = wk.alloc([4, 2, 2, 128], BF16)
    m4b = wk.alloc([128], BF16)
    ynb = wk.alloc([4, 256], BF16)
    selc = wk.alloc([2, 8, 32], F32)
    tk = wk.alloc([160], F32)
    negb = wk.alloc([32], BF16)
    fbuf = wk.alloc([4, 3], F32)
    acc32 = hidraw[:, 0:256].bitcast(F32).rearrange('p (a b) -> p a b', a=2)
    tail, toff = k.scr_tail
    negT = tail[:, toff:toff + S]
    exm = tail[:, toff + S:toff + 2 * S].rearrange("p (a b) -> p a b", a=16)
    Ebuf = tail[:, toff + 2 * S:toff + 2 * S + 3 * TB].rearrange("p (a b) -> p a b", a=3)

    P.dma("pool", exm, dr["exm"], writes=["exm"])
    P.op("dve", lambda e: e.memset(negT, 0.0), writes=["negT"])
    P.dma("pool", m4b, dr["m4"], writes=["m4b"])
    P.dma("sp", selc, dr["selc"], writes=["selc"])
    P.dma("pool", w2, dr["w2kv"], writes=["w2"])
    P.dma("pool", posb, dr["poskv"], writes=["posb"])
    P.dma("sp", ovl32[0:127, :], dr["ovl"], writes=["ovl32"])
    P.op("dve", lambda e: e.memset(vaug[:, :, :, 64:65], 1.0), writes=["vaug"])
    P.op("dve", lambda e: e.memset(zpad[:, 128:224], 0.0), writes=["zpad"])
    P.op("dve", lambda e: e.tensor_copy(out=zpad[:, 0:128], in_=k.ident), reads=["ident"], writes=["zpad"])
    P.op("dve", lambda e: e.memset(vcaug[:, 64:65], 1.0), writes=["vcaug"])
    P.op("dve", lambda e: e.tensor_copy(out=vcaug[0:127, 65:97], in_=ovl32[0:127, :]), reads=["ovl32"], writes=["vcaug"])

    if not k.hc_done:
        k.hc_done = True
        for head in range(8):
            P.dma("sp", cmpbias, dr["cmpt"][head], writes=["cmpbias"])
            P.op("dve", lambda e, head=head: e.tensor_scalar(out=cmpbias, in0=cmpbias, scalar1=k.b31[:, head:head + 1], scalar2=None,
                                                             op0=ALU.subtract), reads=["b31"], writes=["cmpbias"])
            P.op("dve", lambda e: e.tensor_copy(out=ctab[:, 0, 0, :], in_=cmpbias), reads=["cmpbias"], writes=["ctab_s"])
            P.op("dve", lambda e: e.tensor_tensor(out=cmpbias, in0=cmpbias, in1=ctab[:, 0, 0, :], op=ALU.subtract),
                 reads=["ctab_s"], writes=["cmpbias"])
            P.op("dve", lambda e: e.tensor_copy(out=ctab[:, 0, 1, :], in_=cmpbias), reads=["cmpbias"], writes=["ctab_s"])
            P.dma("sp", dr["cscr"][head], ctab[:, 0], reads=["ctab_s"], writes=[("cscr", head)])
        P.barrier()
        for pc in range(16):
            P.dma("pool", w1buf[:, pc % 2], dr["w1kv"][:, 2 * pc:2 * pc + 2, :], writes=[("w1buf", pc % 2)])
            for tt in range(2):
                t = 2 * pc + tt
                for kv in range(2):
                    pb = 64 * kv
                    for mt in range(2):
                        bank = kv * 2 + mt
                        P.mm_group([lambda e, pb=pb, pc=pc, tt=tt, mt=mt, bank=bank, t=t: e.matmul(
                            ps[bank][:, 0:1], lhsT=w1buf[pb:pb + 64, pc % 2, tt, mt * 128:(mt + 1) * 128],
                            rhs=posb[pb:pb + 64, t:t + 1], start=(t == 0), stop=(t == 31))],
                            reads=[("w1buf", pc % 2), "posb"], writes=[("ps", bank)])
        for i in range(4):
            P.op("dve", lambda e, i=i: e.tensor_copy(out=k.hc[:, i:i + 1], in_=ps[i][:, 0:1]), writes=[("ps", i), "hc"])

    for g in range(2):
        base = 24 + g * 5
        for pr in range(2):
            def ev_q(t, bank, pr=pr):
                P.op("act", lambda e: e.activation(out=qn[:, pr, t * TB:(t + 1) * TB], in_=ps[bank][:, :], func=AF.Copy, scale=0.125),
                     writes=[("ps", bank), ("qn", pr)])
            _proj_fm(k, dr["abw"][base + pr], ev_q)
        for idx, dst, nm in ((2, kcvc, "kcvc"), (3, kslc, "kslc"), (4, kwin, "kwin")):
            def ev_k(t, bank, dst=dst, nm=nm):
                P.op("dve", lambda e: e.tensor_copy(out=dst[:, t * TB:(t + 1) * TB], in_=ps[bank][:, :]), writes=[("ps", bank), nm])
            _proj_fm(k, dr["abw"][base + idx], ev_k)
        P.dma("pool", wtok, dr["abtok"][g], writes=["wtok"])
        for ti in range(16):
            bank = 6 + k.pbc % 2
            k.pbc += 1
            P.mm_group([lambda e, kc=kc, ti=ti, bank=bank: e.matmul(
                ps[bank][:, 0:140], lhsT=uT[:, kc, ti * 128:(ti + 1) * 128], rhs=wtok[:, kc, :],
                start=(kc == 0), stop=(kc == KC - 1)) for kc in range(KC)],
                reads=["wtok"] + [("uT", kc, ti // 4) for kc in range(KC)], writes=[("ps", bank)])
            P.op("act", lambda e, ti=ti, bank=bank: e.activation(
                out=vaug[:, ti, :, 0:64], in_=ps[bank][:, 0:128].rearrange("p (a b) -> p a b", a=2), func=AF.Copy),
                writes=[("ps", bank), "vaug"])
            P.op("act", lambda e, ti=ti, bank=bank: e.activation(out=sig[:, ti, :], in_=ps[bank][:, 128:140], func=AF.Sigmoid),
                 writes=[("ps", bank), "sig"])
        for pc in range(16):
            P.dma("pool", w1buf[:, pc % 2], dr["w1kv"][:, 2 * pc:2 * pc + 2, :], writes=[("w1buf", pc % 2)])
            for tt in range(2):
                t = 2 * pc + tt
                for kv in range(2):
                    pb = 64 * kv
                    rhs = kcvc[pb:pb + 64, :].rearrange("p (l s) -> p l s", s=16)[:, 0:127, t % 16] if t < 16 else \
                        kcvc[pb:pb + 64, :].rearrange("p (l s) -> p l s", s=16)[:, 1:128, t - 16]
                    for mt in range(2):
                        bank = kv * 2 + mt
                        P.mm_group([lambda e, pb=pb, pc=pc, tt=tt, mt=mt, bank=bank, t=t, rhs=rhs: e.matmul(
                            ps[bank][:, 0:127], lhsT=w1buf[pb:pb + 64, pc % 2, tt, mt * 128:(mt + 1) * 128],
                            rhs=rhs, start=(t == 0), stop=(t == 31))],
                            reads=[("w1buf", pc % 2), "kcvc"], writes=[("ps", bank)])
        for kv in range(2):
            for mt in range(2):
                bank = kv * 2 + mt
                P.op("act", lambda e, kv=kv, mt=mt, bank=bank: e.activation(
                    out=hid[:, kv, mt, :], in_=ps[bank][:, 0:127], func=AF.Silu, bias=k.hc[:, bank:bank + 1], scale=1.0),
                    reads=["hc"], writes=[("ps", bank), "hid"])
        P.mm_group([lambda e, mt=mt: e.matmul(ps[4][:, 0:127], lhsT=w2[:, mt, 0:128], rhs=hid[:, 0, mt, :], start=(mt == 0), stop=(mt == 1))
                    for mt in range(2)], reads=["w2", "hid"], writes=[("ps", 4)])
        P.op("dve", lambda e: e.tensor_copy(out=kcT, in_=ps[4][:, 0:127]), writes=[("ps", 4), "kcT"])
        P.mm_group([lambda e, mt=mt: e.matmul(ps[5][0:127, 0:64], lhsT=hid[:, 1, mt, :], rhs=w2[:, mt, 128:192], start=(mt == 0), stop=(mt == 1))
                    for mt in range(2)], reads=["w2", "hid"], writes=[("ps", 5)])
        P.op("act", lambda e: e.activation(out=vcaug[0:127, 0:64], in_=ps[5][0:127, 0:64], func=AF.Copy), writes=[("ps", 5), "vcaug"])

        P.barrier()
        for r in range(4):
            head = g * 4 + r
            P.dma("sp", tab, dr["tab0"][head], writes=["cmpbias"])
            P.op("dve", lambda e, head=head: e.tensor_scalar(out=tab, in0=tab, scalar1=k.b31[:, head:head + 1], scalar2=None, op0=ALU.subtract),
                 reads=["b31"], writes=["cmpbias"])
            P.op("dve", lambda e, r=r: e.tensor_copy(out=tabhl[:, r, :, 0, :], in_=tab), reads=["cmpbias"], writes=["tabhl"])
            P.op("dve", lambda e, r=r: e.tensor_tensor(out=tabtmp, in0=tab, in1=tabhl[:, r, :, 0, :], op=ALU.subtract),
                 reads=["tabhl"], writes=["cmpbias"])
            P.op("dve", lambda e, r=r: e.tensor_copy(out=tabhl[:, r, :, 1, :], in_=tabtmp), reads=["cmpbias"], writes=["tabhl"])
        sc = {"s": 0, "e": 0, "c": 0}
        for qb in range(4):
            q0 = qb * TB
            ncv = min(127, 32 * qb + 31)
            zs = 96 - 32 * qb
            for r in range(4):
                head = g * 4 + r
                pb = 64 * (r % 2)
                qv = qn[pb:pb + 64, r // 2, q0:q0 + TB]
                cslot = sc["c"] % 2
                sc["c"] += 1
                P.dma("sp", ctab[:, cslot], dr["cscr"][head], reads=[("cscr", head)], writes=[("ctab", cslot)])
                sb_ = sc["s"] % 2
                sc["s"] += 1
                P.mm_group([lambda e, sb_=sb_, pb=pb, qv=qv, ncv=ncv: e.matmul(ps[sb_][0:ncv, :], lhsT=kcT[pb:pb + 64, 0:ncv], rhs=qv, start=True, stop=False),
                            lambda e, sb_=sb_, ncv=ncv, zs=zs, cslot=cslot: e.matmul(ps[sb_][0:ncv, :], lhsT=zpad[:, zs:zs + ncv], rhs=ctab[:, cslot, 0, :],
                                                                                   start=False, stop=False),
                            lambda e, sb_=sb_, ncv=ncv, zs=zs, cslot=cslot: e.matmul(ps[sb_][0:ncv, :], lhsT=zpad[:, zs:zs + ncv], rhs=ctab[:, cslot, 1, :],
                                                                                   start=False, stop=True)],
                           reads=["kcT", ("qn", r // 2), "zpad", ("ctab", cslot)], writes=[("ps", sb_)])
                ei = sc["e"] % 3
                sc["e"] += 1
                P.op("act", lambda e, sb_=sb_, ei=ei, ncv=ncv: e.activation(out=Ebuf[0:ncv, ei, :], in_=ps[sb_][0:ncv, :], func=AF.Exp),
                     writes=[("ps", sb_), ("E", ei)])
                for tq in range(4):
                    P.mm_group([lambda e, tq=tq, ei=ei, ncv=ncv: e.matmul(ps[6][:, tq * 97:(tq + 1) * 97], lhsT=Ebuf[0:ncv, ei, tq * 128:(tq + 1) * 128],
                                                                         rhs=vcaug[0:ncv, :], start=True, stop=True)],
                               reads=[("E", ei), "vcaug"], writes=[("ps", 6)])
                P.op("act", lambda e, r=r: e.activation(out=Ocmp[:, :, r, :], in_=ps[6][:, 0:388].rearrange("p (a b) -> p a b", a=4), func=AF.Copy),
                     writes=[("ps", 6), "Ocmp"])
            if qb >= 2:
                for tq in range(4):
                    i = 4 * qb + tq - 8
                    rdc, imp, scr, wrk, mx = tk[:, 0:4], tk[:, 8:40], tk[:, 40:72], tk[:, 72:104], tk[:, 104:120]
                    P.op("dve", lambda e, tq=tq, rdc=rdc: e.tensor_scalar(out=rdc, in0=Ocmp[:, tq, :, 64], scalar1=1e-30, scalar2=None, op0=ALU.max),
                         reads=["Ocmp"], writes=["tk"])
                    P.op("dve", lambda e, rdc=rdc: e.reciprocal(out=rdc, in_=rdc), writes=["tk"])
                    P.op("dve", lambda e, tq=tq, rdc=rdc, imp=imp: e.tensor_scalar(out=imp, in0=Ocmp[:, tq, 0, 65:97], scalar1=rdc[:, 0:1],
                                                                                 scalar2=None, op0=ALU.mult), reads=["Ocmp"], writes=["tk"])
                    for r in range(1, 4):
                        P.op("dve", lambda e, tq=tq, r=r, rdc=rdc, imp=imp: e.scalar_tensor_tensor(
                            out=imp, in0=Ocmp[:, tq, r, 65:97], scalar=rdc[:, r:r + 1], in1=imp, op0=ALU.mult, op1=ALU.add),
                            reads=["Ocmp"], writes=["tk"])
                    P.op("dve", lambda e, i=i, imp=imp, scr=scr: e.tensor_tensor(out=scr, in0=imp, in1=selc[:, 0, i, :], op=ALU.mult),
                         reads=["selc"], writes=["tk"])
                    P.op("dve", lambda e, i=i, scr=scr: e.tensor_tensor(out=scr, in0=scr, in1=selc[:, 1, i, :], op=ALU.add),
                         reads=["selc"], writes=["tk"])
                    P.op("dve", lambda e, scr=scr, mx=mx: e.max(out=mx[:, 0:8], in_=scr), writes=["tk"])
                    P.op("dve", lambda e, scr=scr, mx=mx, wrk=wrk: e.match_replace(out=wrk, in_to_replace=mx[:, 0:8], in_values=scr, imm_value=-3.0),
                         writes=["tk"])
                    P.op("dve", lambda e, mx=mx, wrk=wrk: e.max(out=mx[:, 8:16], in_=wrk), writes=["tk"])
                    P.op("dve", lambda e, scr=scr, mx=mx: e.tensor_scalar(out=negb, in0=scr, scalar1=mx[:, 15:16], scalar2=NEG,
                                                                         op0=ALU.is_lt, op1=ALU.mult), writes=["tk", "negb"])
                    psT = ps[7][:, :].bitcast(BF16)
                    P.mm_group([lambda e, psT=psT: e.transpose(out=psT[0:32, 0:128], in_=negb, identity=k.ident)],
                               reads=["negb", "ident"], writes=[("ps", 7)])
                    negdst = negT[0:32, q0 + tq * 128:q0 + (tq + 1) * 128]
                    P.op("act", lambda e, psT=psT, negdst=negdst: e.activation(out=negdst, in_=psT[0:32, 0:128], func=AF.Copy),
                         writes=[("ps", 7), "negT"])
            items = []
            for r in range(4):
                for br in (0, 1):
                    kt_lo = 0 if br == 0 else max(4 * qb - 4, 0)
                    kts = list(range(kt_lo, 4 * qb + 4))
                    for kt in kts:
                        items.append((r, br, kt, kt == kts[0] and br == 0, kt == kts[-1], kt == kts[-1] and br == 1))

            def stage_a(it):
                r, br, kt, first_of_head, last_of_branch, last_of_head = it
                head = g * 4 + r
                pb = 64 * (r % 2)
                ksrc, knm = (kslc, "kslc") if br == 0 else (kwin, "kwin")
                c0 = max(kt - 4 * qb, 0)
                c1 = 3 if br == 0 else min(kt + 4 - 4 * qb, 3)
                cols = slice(c0 * 128, (c1 + 1) * 128)
                qv = qn[pb:pb + 64, r // 2, q0 + c0 * 128:q0 + (c1 + 1) * 128]
                sb_ = sc["s"] % 2
                sc["s"] += 1
                use_sel = (br == 0 and qb >= 2)
                fns = [lambda e: e.matmul(ps[sb_][:, cols], lhsT=ksrc[pb:pb + 64, kt * 128:(kt + 1) * 128], rhs=qv, start=True, stop=False)]
                rd = [knm, ("qn", r // 2)]
                if use_sel:
                    negsl = negT[:, q0 + c0 * 128:q0 + (c1 + 1) * 128]
                    fns.append(lambda e: e.matmul(ps[sb_][:, cols], lhsT=exm[:, kt, :], rhs=negsl, start=False, stop=False))
                    rd += ["exm", "negT"]
                specials = []
                for tq in range(c0, c1 + 1):
                    off = 4 * qb + tq - kt
                    if off == 0:
                        specials += [(tq, tabhl[:, r, 0, 0, :]), (tq, tabhl[:, r, 0, 1, :])]
                    elif off == 1:
                        specials += [(tq, tabhl[:, r, 1, 0, :]), (tq, tabhl[:, r, 1, 1, :])]
                    elif off == 4 and br == 1:
                        specials += [(tq, m4b)]
                for si, (tq, src) in enumerate(specials):
                    fns.append(lambda e, tq=tq, src=src, lastsp=(si == len(specials) - 1): e.matmul(
                        ps[sb_][:, tq * 128:(tq + 1) * 128], lhsT=k.ident, rhs=src, start=False, stop=lastsp))
                if specials:
                    rd += ["tabhl", "m4b", "ident"]
                P.mm_group(fns, reads=rd, writes=[("ps", sb_)])
                return dict(sb=sb_, c0=c0, c1=c1, cols=cols)

            def stage_b(it, st_):
                r, br, kt, first_of_head, last_of_branch, last_of_head = it
                sb_, c0, c1, cols = st_["sb"], st_["c0"], st_["c1"], st_["cols"]
                ei = sc["e"] % 3
                sc["e"] += 1
                P.op("act", lambda e: e.activation(out=Ebuf[:, ei, cols], in_=ps[sb_][:, cols], func=AF.Exp),
                     writes=[("ps", sb_), ("E", ei)])
                st_["ei"] = ei

            def stage_c(it, st_):
                r, br, kt, first_of_head, last_of_branch, last_of_head = it
                ei, c0, c1 = st_["ei"], st_["c0"], st_["c1"]
                for tq in range(c0, c1 + 1):
                    qt = 4 * qb + tq
                    first = kt == (0 if br == 0 else max(qt - 4, 0))
                    last = kt == qt
                    P.mm_group([lambda e, tq=tq, first=first, last=last: e.matmul(
                        ps[2 + tq][:, 0:65], lhsT=Ebuf[:, ei, tq * 128:(tq + 1) * 128], rhs=vaug[:, kt, br, :], start=first, stop=last)],
                        reads=[("E", ei), "vaug"], writes=[("ps", 2 + tq)])
                if last_of_branch:
                    for tq in range(4):
                        P.op("act", lambda e, tq=tq: e.activation(out=Oraw[:, tq, br, :], in_=ps[2 + tq][:, 0:65], func=AF.Copy),
                             writes=[("ps", 2 + tq), "Oraw"])
                if last_of_head:
                    P.op("dve", lambda e: e.tensor_scalar(out=fbuf[:, :, 0], in0=Ocmp[:, :, r, 64], scalar1=1e-30, scalar2=None, op0=ALU.max),
                         reads=["Ocmp"], writes=["fbuf"])
                    P.op("dve", lambda e: e.tensor_scalar(out=fbuf[:, :, 1:3], in0=Oraw[:, :, :, 64], scalar1=1e-30, scalar2=None, op0=ALU.max),
                         reads=["Oraw"], writes=["fbuf"])
                    P.op("dve", lambda e: e.reciprocal(out=fbuf, in_=fbuf), writes=["fbuf"])
                    sigsl = sig[:, 4 * qb:4 * qb + 4, 3 * r:3 * r + 3]
                    P.op("dve", lambda e: e.tensor_tensor(out=fbuf, in0=fbuf, in1=sigsl, op=ALU.mult),
                         reads=["sig"], writes=["fbuf"])
                    for tq in range(4):
                        a32 = acc32[:, tq % 2]
                        P.op("dve", lambda e, tq=tq, a32=a32: e.tensor_scalar(out=a32, in0=Ocmp[:, tq, r, 0:64], scalar1=fbuf[:, tq, 0:1],
                                                                              scalar2=None, op0=ALU.mult),
                             reads=["Ocmp", "fbuf"], writes=[("acc32", tq % 2)])
                        P.op("dve", lambda e, tq=tq, a32=a32: e.scalar_tensor_tensor(out=a32, in0=Oraw[:, tq, 0, 0:64], scalar=fbuf[:, tq, 1:2],
                                                                                    in1=a32, op0=ALU.mult, op1=ALU.add),
                             reads=["Oraw", "fbuf"], writes=[("acc32", tq % 2)])
                        P.op("dve", lambda e, tq=tq, a32=a32: e.scalar_tensor_tensor(out=ynb[:, tq, r * 64:(r + 1) * 64], in0=Oraw[:, tq, 1, 0:64],
                                                                                    scalar=fbuf[:, tq, 2:3], in1=a32, op0=ALU.mult, op1=ALU.add),
                             reads=["Oraw", "fbuf", ("acc32", tq % 2)], writes=["ynb"])

            LOOK = DEBUG.get("look", 1)
            sts = {}
            for jj in range(min(LOOK, len(items))):
                sts[jj] = stage_a(items[jj])
            for ii, it in enumerate(items):
                if ii + LOOK < len(items):
                    sts[ii + LOOK] = stage_a(items[ii + LOOK])
                stage_b(it, sts[ii])
                stage_c(it, sts[ii])
                del sts[ii]
            psT = ps[7][:, :].bitcast(BF16)
            P.mm_group([lambda e, tq=tq, j=j, psT=psT: e.transpose(out=psT[:, (tq * 2 + j) * 128:(tq * 2 + j + 1) * 128],
                                                                  in_=ynb[:, tq, j * 128:(j + 1) * 128], identity=k.ident)
                        for tq in range(4) for j in range(2)], reads=["ynb", "ident"], writes=[("ps", 7)])
            for j in range(2):
                P.op("act", lambda e, j=j, psT=psT, g=g, q0=q0: e.activation(
                    out=yT[:, 4 + 2 * g + j, q0:q0 + TB].rearrange("p (a b) -> p a b", a=4),
                    in_=psT.rearrange("p (a j b) -> p a j b", a=4, j=2)[:, :, j, :], func=AF.Copy),
                    writes=[("ps", 7), ("yT", 4 + 2 * g + j, qb)])
    wk.release()
```

```python
import math
from contextlib import ExitStack

import numpy as np
import ml_dtypes

import concourse.bass as bass
import concourse.mybir as mybir
from concourse.bass_utils import run_bass_kernel_spmd

F32 = mybir.dt.float32
BF16 = mybir.dt.bfloat16
AF = mybir.ActivationFunctionType
ALU = mybir.AluOpType

NCORES = 8
D = 1024
S = 2048
KC = 8
TB = 512
NTB = S // TB
DFF = 2816
NFT = DFF // 128
ALPHA = (2 * 2) ** 0.25
LN_EPS = 1e-5
NEG = -30000.0
DEBUG = {}


class Prog:
    ENGS = ("pe", "act", "dve", "pool", "sp")

    def __init__(self, nc, stack, n_dma_sems=12):
        self.nc = nc
        self.streams = {e: [] for e in self.ENGS}
        self.cnt = {e: 0 for e in self.ENGS}
        self.sem = {}
        for e in self.ENGS:
            self.sem[e] = stack.enter_context(nc.semaphore("sem_" + e))
        self.dma_sems = {}
        self.dma_tot = {}
        self.dma_rr = {}
        for q in ("sp", "pool"):
            self.dma_sems[q] = [stack.enter_context(nc.semaphore("dsem_%s_%d" % (q, i))) for i in range(n_dma_sems)]
            self.dma_tot[q] = [0] * n_dma_sems
            self.dma_rr[q] = 0
        self.waited = {}
        self.res = {}
        self.n_ops = 0

    def _need(self, eng, reads, writes):
        need = {}

        def add(tok):
            if tok is None:
                return
            k, v = tok
            if eng == "pe" and k == "pe":
                return
            if need.get(k, 0) < v:
                need[k] = v

        for key in reads:
            r = self.res.get(key)
            if r is not None:
                add(r[0])
        for key in writes:
            r = self.res.get(key)
            if r is not None:
                add(r[0])
                for k, v in r[1].items():
                    add((k, v))
        out = []
        for k, v in need.items():
            if self.waited.get((eng, k), 0) < v:
                self.waited[(eng, k)] = v
                out.append((k, v))
        return out

    def _semh(self, k):
        if isinstance(k, tuple):
            return self.dma_sems[k[1]][k[2]]
        return self.sem[k]

    def _commit(self, tok, reads, writes):
        k, v = tok
        for key in reads:
            r = self.res.setdefault(key, [None, {}])
            if r[1].get(k, 0) < v:
                r[1][k] = v
        for key in writes:
            self.res[key] = [tok, {}]

    def op(self, eng, fn, reads=(), writes=()):
        st = self.streams[eng]
        for k, v in self._need(eng, reads, writes):
            st.append(("wait", k, v))
        self.cnt[eng] += 1
        tok = (eng, self.cnt[eng])
        st.append(("op", fn, True))
        self._commit(tok, reads, writes)
        self.n_ops += 1
        return tok

    def mm_group(self, fns, reads=(), writes=()):
        st = self.streams["pe"]
        for k, v in self._need("pe", reads, writes):
            st.append(("wait", k, v))
        self.cnt["pe"] += 1
        tok = ("pe", self.cnt["pe"])
        for i, fn in enumerate(fns):
            st.append(("op", fn, i == len(fns) - 1))
        self._commit(tok, reads, writes)
        self.n_ops += len(fns)
        return tok

    def dma(self, q, out, in_, reads=(), writes=()):
        st = self.streams[q]
        i = self.dma_rr[q]
        self.dma_rr[q] = (i + 1) % len(self.dma_sems[q])
        key = ("dma", q, i)
        prev = self.dma_tot[q][i]
        if prev and self.waited.get((q, key), 0) < prev:
            self.waited[(q, key)] = prev
            st.append(("wait", key, prev))
        for k, v in self._need(q, reads, writes):
            st.append(("wait", k, v))
        self.dma_tot[q][i] += 16
        tok = (key, self.dma_tot[q][i])
        st.append(("dma", out, in_, self.dma_sems[q][i]))
        self._commit(tok, reads, writes)
        return tok

    def barrier(self):
        toks = [(e, self.cnt[e]) for e in self.ENGS if self.cnt[e]]
        for q in ("sp", "pool"):
            for i, tot in enumerate(self.dma_tot[q]):
                if tot:
                    toks.append((("dma", q, i), tot))
        for e in self.ENGS:
            for tok in toks:
                if tok[0] != e or e != "pe":
                    self.wait_tok(e, tok)

    def wait_tok(self, eng, tok):
        k, v = tok
        if self.waited.get((eng, k), 0) < v:
            self.waited[(eng, k)] = v
            self.streams[eng].append(("wait", k, v))

    def replay(self):
        nc = self.nc
        engmap = {"pe": "tensor", "act": "scalar", "dve": "vector", "pool": "gpsimd", "sp": "sync"}
        waited = {e: set() for e in self.ENGS}
        for ename in self.ENGS:
            for it in self.streams[ename]:
                if it[0] == "wait" and not isinstance(it[1], tuple):
                    waited[it[1]].add(it[2])
        remap = {}
        for e in self.ENGS:
            remap[e] = {v: i + 1 for i, v in enumerate(sorted(waited[e]))}
        with nc.Block() as block:
            for ename in self.ENGS:
                lst = self.streams[ename]
                semh = self.sem[ename]
                rm_own = remap[ename]

                def body(e, lst=lst, semh=semh, rm_own=rm_own):
                    idx = 0
                    for it in lst:
                        if it[0] == "wait":
                            k_, v = it[1], it[2]
                            if isinstance(k_, tuple):
                                e.wait_ge(self.dma_sems[k_[1]][k_[2]], v)
                            else:
                                e.wait_ge(self.sem[k_], remap[k_][v])
                        elif it[0] == "op":
                            ins = it[1](e)
                            if it[2]:
                                idx += 1
                                if idx in rm_own:
                                    ins.then_inc(semh, 1)
                        else:
                            e.dma_start(out=it[1], in_=it[2]).then_inc(it[3], 16)

                getattr(block, engmap[ename])(body)


class Arena:
    def __init__(self, nc, stack, name, nbytes):
        assert nbytes % 4 == 0
        self.t = stack.enter_context(nc.sbuf_tensor(name, [128, nbytes // 4], F32))
        self.nbytes = nbytes
        self.off = 0
        self.marks = []

    def alloc(self, shape, dtype, parts=128):
        esz = 4 if dtype == F32 else 2
        n = int(np.prod(shape))
        nb = (n * esz + 3) // 4 * 4
        assert self.off + nb <= self.nbytes, ("arena overflow", self.off, nb, self.nbytes)
        w0 = self.off // 4
        ap = self.t[0:parts, w0:w0 + nb // 4]
        if dtype != F32:
            ap = ap.bitcast(dtype)
            ap = ap[:, 0:n]
        self.off += nb
        if len(shape) == 2:
            ap = ap.rearrange("p (a b) -> p a b", a=shape[0])
        elif len(shape) == 3:
            ap = ap.rearrange("p (a b c) -> p a b c", a=shape[0], b=shape[1])
        elif len(shape) == 4:
            ap = ap.rearrange("p (a b c d) -> p a b c d", a=shape[0], b=shape[1], c=shape[2])
        return ap

    def mark(self):
        self.marks.append(self.off)

    def release(self):
        self.off = self.marks.pop()


def _wtiles(W, tile_cols):
    K = W.shape[0]
    out = []
    for cols in tile_cols:
        sub = W[:, cols]
        out.append(sub.reshape(K // 128, 128, len(cols)).transpose(1, 0, 2))
    return np.ascontiguousarray(np.stack(out, 0), dtype=np.float32)


def _t5_bucket(dist):
    n = np.maximum(dist, 0)
    nf = np.maximum(n, 1).astype(np.float32)
    large = 16 + (np.log(nf / np.float32(16)) / np.float32(math.log(128 / 16)) * np.float32(16)).astype(np.int32)
    large = np.minimum(large, 31)
    return np.where(n < 16, n, large)


def _bias_tile(rel_bias, dist, valid):
    b = _t5_bucket(dist)
    t = rel_bias[b]
    t = np.where(valid[..., None], t, np.float32(NEG))
    return np.ascontiguousarray(t.transpose(2, 0, 1), dtype=np.float32)


DIL = ((128, 1), (512, 4), (2048, 16))


def _const_tables(rel_bias):
    kk = np.arange(128)[:, None]
    qq = np.arange(128)[None, :]
    c = {}
    tabs = []
    for (win, dl) in DIL:
        md = win // dl
        d0 = qq - kk
        tabs.append(_bias_tile(rel_bias, d0 * dl, (d0 >= 0) & (d0 <= md)))
        d1 = 128 + qq - kk
        tabs.append(_bias_tile(rel_bias, d1 * dl, (d1 >= 0) & (d1 <= md)))
    tab1 = np.stack([tabs[0], tabs[1], tabs[2], tabs[3], tabs[4]], 1)
    c["tab1"] = np.ascontiguousarray(tab1.transpose(0, 2, 1, 3))
    d0 = qq - kk
    T0 = _bias_tile(rel_bias, d0, d0 >= 0)
    d1 = 128 + qq - kk
    T1 = _bias_tile(rel_bias, d1, d1 >= 0)
    tab0 = np.stack([T0, T1], 1)
    c["tab0"] = np.ascontiguousarray(tab0.transpose(0, 2, 1, 3))
    d4 = 512 + qq - kk
    c["m4"] = np.where(d4 <= 511, np.float32(0), np.float32(NEG)).astype(np.float32)
    c["b31"] = np.ascontiguousarray(np.broadcast_to(rel_bias[31][None, :], (128, 8)), dtype=np.float32)
    cpp = (np.arange(128) - 96)[:, None]
    xx = np.arange(512)[None, :]
    dist = xx - 16 * cpp - 31
    cb = rel_bias[_t5_bucket(dist)]
    cb = np.where((dist >= 0)[..., None], cb, np.float32(NEG))
    c["cmpt"] = np.ascontiguousarray(cb.transpose(2, 0, 1), dtype=np.float32)
    return c


class K:
    pass


def _tok_ap(ap2d, start_l, n, dl, r):
    if dl == 1:
        return ap2d[:, start_l:start_l + n]
    return ap2d.rearrange("p (l s) -> p l s", s=dl)[:, start_l:start_l + n, r]


def build_program(nseq, layers, has0, has1):
    nc = bass.Bass("TRN2", target_bir_lowering=False)
    k = K()
    k.nc = nc
    k.nseq = nseq
    dr = {}

    def din(name, shape):
        dr[name] = nc.dram_tensor(name, list(shape), F32, kind="ExternalInput").ap()

    din("xT", [nseq, 128, KC, S])
    din("cT", [128, KC, 4])
    din("adaw", [2, 8, 128, KC, 768])
    din("adab", [128, 2, 48])
    din("lng", [128, 2, 2, KC])
    din("lnb", [128, 2, 2, KC])
    din("ffg", [2, NFT, 128, KC, 128])
    din("ffu", [2, NFT, 128, KC, 128])
    din("ffd", [2, KC, 128, NFT, 128])
    din("b31", [128, 8])
    if has1:
        din("dwin", [72, 128, KC, 128])
        din("dwout", [KC, 128, KC, 128])
        din("tab1", [8, 128, 5, 128])
    if has0:
        din("abw", [34, 128, KC, 128])
        din("abtok", [2, 128, KC, 140])
        din("abwout", [KC, 128, KC, 128])
        din("tab0", [8, 128, 2, 128])
        din("m4", [128, 128])
        din("cmpt", [8, 128, 512])
        dr["cscr"] = nc.dram_tensor("cscr", [8, 128, 2, 512], BF16).ap()
        din("w1kv", [128, 32, 256])
        din("w2kv", [128, 2, 192])
        din("poskv", [128, 32])
        din("gngb", [128, 2, 4])
        din("rope", [128, 2, S])
        din("retc", [128, 4 * 2 + 128 + 4 * 128])
        din("selc", [128, 2, 8, 32])
        din("ovl", [127, 32])
        din("exm", [128, 16, 128])
    outT = nc.dram_tensor("outT", [nseq, 128, KC, S], F32, kind="ExternalOutput").ap()
    k.dr = dr

    with ExitStack() as st:
        P = Prog(nc, st)
        k.P = P
        ps = [st.enter_context(nc.psum_tensor("ps%d" % i, [128, 512], F32)) for i in range(8)]
        k.ps = ps
        per = Arena(nc, st, "persist", 65536 + 32768 + 45056 + 8192)
        hT = per.alloc([KC, S], F32)
        uT = per.alloc([KC, S], BF16)
        scrA = per.alloc([45056 // 2], BF16)
        k.hT, k.uT = hT, uT
        k.yT = scrA[:, 0:KC * S].rearrange("p (c t) -> p c t", c=KC)
        k.aT = scrA[:, 0:NFT * 1024].rearrange("p (c t) -> p c t", c=NFT)
        k.scr_tail = (scrA, KC * S)
        mod = per.alloc([2, 48, 4], F32)
        lng = per.alloc([2, 2, KC], F32)
        lnb = per.alloc([2, 2, KC], F32)
        adab = per.alloc([2, 48], F32)
        ones32 = per.alloc([128], F32)
        onesbf = per.alloc([128], BF16)
        identf = per.alloc([128], F32)
        ident = per.alloc([128], BF16)
        b31 = per.alloc([8], F32)
        cT = per.alloc([KC, 4], F32)
        k.hc = per.alloc([4], F32)
        k.ones128 = per.alloc([128], F32)
        k.onesb1k = per.alloc([128], BF16)
        k.hc_done = False
        scT = per.alloc([KC, 4], BF16)
        k.mod, k.lng, k.lnb, k.ones32, k.onesbf, k.ident, k.b31 = mod, lng, lnb, ones32, onesbf, ident, b31
        wk = Arena(nc, st, "work", (int(nc.sbuf_bytes_remaining) - 512) // 4 * 4)
        k.wk = wk
        print("persist bytes", per.off, "of", per.nbytes, "work", wk.nbytes)

        NW = 4
        wbuf = per_w = None
        wk.mark()
        wbuf = wk.alloc([NW, KC, 128], BF16)
        k.wbuf = wbuf
        k.wrr = 0

        def load_w(src, nsl=1):
            s0 = (k.wrr + nsl - 1) // nsl * nsl
            if s0 + nsl > NW:
                s0 = 0
            k.wrr = (s0 + nsl) % NW
            srcs = src if isinstance(src, list) else [src]
            for j, s_ in enumerate(srcs):
                P.dma("pool", wbuf[:, s0 + j], s_, writes=[("wbuf", s0 + j)])
            return s0
        k.load_w = load_w

        P.dma("sp", lng, dr["lng"], writes=["lng"])
        P.dma("sp", lnb, dr["lnb"], writes=["lnb"])
        P.dma("sp", adab, dr["adab"], writes=["adab"])
        P.dma("sp", b31, dr["b31"], writes=["b31"])
        P.dma("sp", cT, dr["cT"], writes=["cT"])
        P.op("dve", lambda e: e.memset(ones32, 1.0 / 1024), writes=["ones32"])
        P.op("dve", lambda e: e.memset(onesbf, 1.0), writes=["onesbf"])
        P.op("dve", lambda e: e.memset(k.ones128, 1.0 / 128), writes=["ones128"])
        P.op("dve", lambda e: e.memset(k.onesb1k, 1.0 / 1024), writes=["onesb1k"])
        P.op("pool", lambda e: e.memset(identf, 0.0), writes=["identf"])
        P.op("pool", lambda e: e.affine_select(out=identf, in_=identf, pattern=[[-1, 128]], compare_op=ALU.not_equal,
                                               fill=1.0, base=0, channel_multiplier=1), writes=["identf"])
        P.op("dve", lambda e: e.tensor_copy(out=ident, in_=identf), reads=["identf"], writes=["ident"])
        P.op("act", lambda e: e.activation(out=scT[:, :, 0:nseq], in_=cT[:, :, 0:nseq], func=AF.Silu), reads=["cT"], writes=["scT"])

        wk.mark()
        adabuf = wk.alloc([2, KC, 768], BF16)
        for l in layers:
            for grp in range(8):
                ab = adabuf[:, grp % 2]
                P.dma("pool", ab, dr["adaw"][l, grp], writes=[("adabuf", grp % 2)])
                bank = grp % 2
                for j in range(6):
                    ft = grp * 6 + j
                    P.mm_group([lambda e, kc=kc, j=j, ab=ab, bank=bank: e.matmul(
                        ps[bank][:, j * 4:j * 4 + nseq], lhsT=ab[:, kc, j * 128:(j + 1) * 128], rhs=scT[:, kc, 0:nseq],
                        start=(kc == 0), stop=(kc == KC - 1)) for kc in range(KC)],
                        reads=[("adabuf", grp % 2), "scT"], writes=[("ps", bank)])
                for j in range(6):
                    ft = grp * 6 + j
                    P.op("dve", lambda e, j=j, ft=ft, l=l, bank=bank: e.tensor_scalar(
                        out=mod[:, l, ft, 0:nseq], in0=ps[bank][:, j * 4:j * 4 + nseq], scalar1=adab[:, l, ft:ft + 1],
                        scalar2=None, op0=ALU.add), reads=["adab"], writes=[("ps", bank), "mod"])
            for j0 in (8, 32):
                P.op("dve", lambda e, l=l, j0=j0: e.tensor_scalar(out=mod[:, l, j0:j0 + 8, :], in0=mod[:, l, j0:j0 + 8, :],
                                                                  scalar1=1.0, scalar2=None, op0=ALU.add), writes=["mod"])
            for j0 in (16, 40):
                P.op("dve", lambda e, l=l, j0=j0: e.tensor_scalar(out=mod[:, l, j0:j0 + 8, :], in0=mod[:, l, j0:j0 + 8, :],
                                                                  scalar1=1.0 / ALPHA, scalar2=None, op0=ALU.mult), writes=["mod"])
        wk.release()
        P.barrier()

        def mv(l, j, c, b):
            return mod[:, l, j * 8 + c, b:b + 1]
        k.mv = mv

        def load_x(b_, tlo, thi):
            cs_ = slice(tlo * TB, thi * TB)
            for c in range(KC):
                P.dma("sp", hT[:, c, cs_], dr["xT"][b_, :, c, cs_], writes=[("hT", c, t) for t in range(tlo, thi)])
            l0_ = layers[0]
            for c in range(KC):
                for t in range(tlo, thi):
                    eng = "act" if (c + t) % 2 == 0 else "dve"
                    emit_affine(k, eng, uT[:, c, t * TB:(t + 1) * TB], hT[:, c, t * TB:(t + 1) * TB],
                                mv(l0_, 1, c, b_), mv(l0_, 0, c, b_), reads=[("hT", c, t), "mod"], writes=[("uT", c, t)])

        def store_out(b_, tlo, thi):
            cs_ = slice(tlo * TB, thi * TB)
            for c in range(KC):
                P.dma("sp", outT[b_, :, c, cs_], hT[:, c, cs_], reads=[("hT", c, t) for t in range(tlo, thi)])

        for b in range(nseq):
            if b == 0:
                load_x(0, 0, NTB)
            else:
                load_x(b, 2, NTB)
            for li, l in enumerate(layers):
                if l == 0:
                    mixer0(k, b)
                    wout = dr["abwout"]
                else:
                    mixer1(k, b)
                    wout = dr["dwout"]
                P.barrier()
                if DEBUG.get("stop") == "mixer":
                    for c in range(KC):
                        for t in range(NTB):
                            P.op("dve", lambda e, c=c, t=t: e.tensor_copy(out=hT[:, c, t * TB:(t + 1) * TB], in_=k.yT[:, c, t * TB:(t + 1) * TB]),
                                 reads=[("yT", c, t)], writes=[("hT", c, t)])
                    break
                nxt = None
                if li + 1 < len(layers):
                    nxt = (mv, layers[li + 1], 1, 0)
                early = None
                if li + 1 == len(layers) and not DEBUG.get("stop"):
                    def early(b=b):
                        store_out(b, 0, 2)
                        if b + 1 < nseq:
                            load_x(b + 1, 0, 2)
                post_mixer(k, l, b, wout, nxt, early)
                P.barrier()
            if DEBUG.get("stop"):
                store_out(b, 0, NTB)
            else:
                store_out(b, 2, NTB)
        for i, s_ in enumerate(P.dma_sems["sp"]):
            if P.dma_tot["sp"][i]:
                P.streams["sp"].append(("wait", ("dma", "sp", i), P.dma_tot["sp"][i]))
        print("ops", P.n_ops, {e: len(v) for e, v in P.streams.items()})
        P.replay()
    return nc


def emit_affine(k, eng, out, in_, scale_ap, bias_ap, reads, writes):
    if eng == "act":
        k.P.op("act", lambda e: e.activation(out=out, in_=in_, func=AF.Identity, scale=scale_ap, bias=bias_ap),
               reads=reads, writes=writes)
    else:
        k.P.op(eng, lambda e: e.tensor_scalar(out=out, in0=in_, scalar1=scale_ap, scalar2=bias_ap, op0=ALU.mult, op1=ALU.add),
               reads=reads, writes=writes)


def outproj(k, l, b, wout, blocks=(0, 1, 2, 3), pending=None):
    P, ps, hT, yT = k.P, k.ps, k.hT, k.yT
    for fo in range(KC):
        s0 = k.load_w(wout[fo])
        for t in blocks:
            bank = t
            P.mm_group([lambda e, kc=kc, t=t, bank=bank, s0=s0, fo=fo: e.matmul(
                ps[bank][:, :], lhsT=k.wbuf[:, s0, kc, :], rhs=yT[:, kc, t * TB:(t + 1) * TB],
                start=(kc == 0), stop=(kc == KC - 1)) for kc in range(KC)],
                reads=[("wbuf", s0)] + [("yT", kc, t) for kc in range(KC)], writes=[("ps", bank)])
            P.op("dve", lambda e, t=t, bank=bank, fo=fo: e.scalar_tensor_tensor(
                out=hT[:, fo, t * TB:(t + 1) * TB], in0=ps[bank][:, :], scalar=k.mv(l, 2, fo, b),
                in1=hT[:, fo, t * TB:(t + 1) * TB], op0=ALU.mult, op1=ALU.add),
                reads=["mod"], writes=[("ps", bank), ("hT", fo, t)])
        if pending:
            for _ in range(3):
                for g_ in list(pending):
                    try:
                        next(g_)
                    except StopIteration:
                        pending.remove(g_)


def ln_block(k, l, j, b, t, nxt):
    P, ps, hT, uT, wk = k.P, k.ps, k.hT, k.uT, k.wk
    eps = LN_EPS / (ALPHA * ALPHA)
    wk.mark()
    sq = wk.alloc([2, TB], BF16)
    xb = wk.alloc([2, TB], BF16)
    mean = wk.alloc([TB], F32)
    rstd = wk.alloc([TB], F32)
    tmp = wk.alloc([2, TB], F32)
    sl = slice(t * TB, (t + 1) * TB)
    A, B = 6, 7
    for c in range(KC):
        P.op("act", lambda e, c=c: e.activation(out=xb[:, c % 2], in_=hT[:, c, sl], func=AF.Copy),
             reads=[("hT", c, t)], writes=[("lnxb", c % 2)])
        P.mm_group([lambda e, c=c: e.matmul(ps[A][:, :], lhsT=k.onesb1k, rhs=xb[:, c % 2], start=(c == 0), stop=(c == KC - 1))],
                   reads=[("lnxb", c % 2), "onesb1k"], writes=[("ps", A)])
        P.op("act", lambda e, c=c: e.activation(out=sq[:, c % 2], in_=hT[:, c, sl], func=AF.Square),
             reads=[("hT", c, t)], writes=[("lnsq", c % 2)])
        P.mm_group([lambda e, c=c: e.matmul(ps[B][:, :], lhsT=k.onesb1k, rhs=sq[:, c % 2], start=(c == 0), stop=(c == KC - 1))],
                   reads=[("lnsq", c % 2), "onesb1k"], writes=[("ps", B)])
    P.op("act", lambda e: e.activation(out=mean, in_=ps[A][:, :], func=AF.Copy), writes=[("ps", A), "lnmean"])
    P.op("dve", lambda e: e.tensor_tensor(out=tmp[:, 0], in0=mean, in1=mean, op=ALU.mult), reads=["lnmean"], writes=[("lntmp", 0)])
    P.op("dve", lambda e: e.tensor_tensor(out=rstd, in0=ps[B][:, :], in1=tmp[:, 0], op=ALU.subtract),
         reads=[("lntmp", 0)], writes=[("ps", B), "lnrstd"])
    P.op("act", lambda e: e.activation(out=rstd, in_=rstd, func=AF.Ln, bias=eps, scale=1.0), writes=["lnrstd"])
    P.op("act", lambda e: e.activation(out=rstd, in_=rstd, func=AF.Exp, scale=-0.5), writes=["lnrstd"])
    for c in range(KC):
        tb_ = tmp[:, c % 2]
        P.op("dve", lambda e, c=c, tb_=tb_: e.tensor_tensor(out=tb_, in0=hT[:, c, sl], in1=mean, op=ALU.subtract),
             reads=[("hT", c, t), "lnmean"], writes=[("lntmp", c % 2)])
        P.op("dve", lambda e, tb_=tb_: e.tensor_tensor(out=tb_, in0=tb_, in1=rstd, op=ALU.mult),
             reads=["lnrstd"], writes=[("lntmp", c % 2)])
        emit_affine(k, "act", hT[:, c, sl], tb_, k.lng[:, l, j, c:c + 1], k.lnb[:, l, j, c:c + 1],
                    reads=[("lntmp", c % 2), "lng", "lnb"], writes=[("hT", c, t)])
        if nxt is not None:
            mvf, l2, jsc, jsh = nxt
            emit_affine(k, "act", uT[:, c, sl], hT[:, c, sl], mvf(l2, jsc, c, b), mvf(l2, jsh, c, b),
                        reads=[("hT", c, t), "mod"], writes=[("uT", c, t)])
    wk.release()


def ffn(k, l, b, nxt):
    P, ps, hT, uT, aT, wk, dr = k.P, k.ps, k.hT, k.uT, k.aT, k.wk, k.dr
    wk.mark()
    sg = wk.alloc([2, TB], F32)
    wd = wk.alloc([2, NFT, 128], BF16)
    it = 0
    for sb in range(2):
        for ft in range(NFT):
            sgw = k.load_w(dr["ffg"][l, ft])
            suw = k.load_w(dr["ffu"][l, ft])
            for bi in range(2):
                t = 2 * sb + bi
                gb, ub = it % 2, 2 + it % 2
                P.mm_group([lambda e, kc=kc, t=t, gb=gb, sgw=sgw: e.matmul(
                    ps[gb][:, :], lhsT=k.wbuf[:, sgw, kc, :], rhs=uT[:, kc, t * TB:(t + 1) * TB],
                    start=(kc == 0), stop=(kc == KC - 1)) for kc in range(KC)],
                    reads=[("wbuf", sgw)] + [("uT", kc, t) for kc in range(KC)], writes=[("ps", gb)])
                P.mm_group([lambda e, kc=kc, t=t, ub=ub, suw=suw: e.matmul(
                    ps[ub][:, :], lhsT=k.wbuf[:, suw, kc, :], rhs=uT[:, kc, t * TB:(t + 1) * TB],
                    start=(kc == 0), stop=(kc == KC - 1)) for kc in range(KC)],
                    reads=[("wbuf", suw)] + [("uT", kc, t) for kc in range(KC)], writes=[("ps", ub)])
                P.op("act", lambda e, gb=gb, it=it: e.activation(out=sg[:, it % 2], in_=ps[gb][:, :], func=AF.Silu),
                     writes=[("ps", gb), ("sg", it % 2)])
                P.op("dve", lambda e, ub=ub, it=it, ft=ft, bi=bi: e.tensor_tensor(
                    out=aT[:, ft, bi * TB:(bi + 1) * TB], in0=sg[:, it % 2], in1=ps[ub][:, :], op=ALU.mult),
                    reads=[("sg", it % 2)], writes=[("ps", ub), ("aT", ft, bi)])
                it += 1
        for fo in range(KC):
            P.dma("pool", wd[:, fo % 2], dr["ffd"][l, fo], writes=[("wd", fo % 2)])
            for bi in range(2):
                t = 2 * sb + bi
                bank = 4 + bi
                P.mm_group([lambda e, kk=kk, bi=bi, bank=bank, fo=fo: e.matmul(
                    ps[bank][:, :], lhsT=wd[:, fo % 2, kk, :], rhs=aT[:, kk, bi * TB:(bi + 1) * TB],
                    start=(kk == 0), stop=(kk == NFT - 1)) for kk in range(NFT)],
                    reads=[("wd", fo % 2)] + [("aT", kk, bi) for kk in range(NFT)], writes=[("ps", bank)])
                P.op("dve", lambda e, t=t, bank=bank, fo=fo: e.scalar_tensor_tensor(
                    out=hT[:, fo, t * TB:(t + 1) * TB], in0=ps[bank][:, :], scalar=k.mv(l, 5, fo, b),
                    in1=hT[:, fo, t * TB:(t + 1) * TB], op0=ALU.mult, op1=ALU.add),
                    reads=["mod"], writes=[("ps", bank), ("hT", fo, t)])
        for bi in range(2):
            ln_block(k, l, 1, b, 2 * sb + bi, nxt)
    wk.release()


def ln_gen(k, l, j, b, t, nxt, L):
    P, ps, hT, uT = k.P, k.ps, k.hT, k.uT
    eps = LN_EPS / (ALPHA * ALPHA)
    sq, xb, mean, rstd, tmp, A, B, sid = L["sq"], L["xb"], L["mean"], L["rstd"], L["tmp"], L["A"], L["B"], L["sid"]
    sl = slice(t * TB, (t + 1) * TB)
    K_ = lambda nm, i=None: ("ln", sid, nm, i)
    for c in range(KC):
        P.op("act", lambda e, c=c: e.activation(out=xb[:, c % 2], in_=hT[:, c, sl], func=AF.Copy),
             reads=[("hT", c, t)], writes=[K_("xb", c % 2)])
        P.mm_group([lambda e, c=c: e.matmul(ps[A][:, :], lhsT=k.onesb1k, rhs=xb[:, c % 2], start=(c == 0), stop=(c == KC - 1))],
                   reads=[K_("xb", c % 2), "onesb1k"], writes=[("ps", A)])
        P.op("act", lambda e, c=c: e.activation(out=sq[:, c % 2], in_=hT[:, c, sl], func=AF.Square),
             reads=[("hT", c, t)], writes=[K_("sq", c % 2)])
        P.mm_group([lambda e, c=c: e.matmul(ps[B][:, :], lhsT=k.onesb1k, rhs=sq[:, c % 2], start=(c == 0), stop=(c == KC - 1))],
                   reads=[K_("sq", c % 2), "onesb1k"], writes=[("ps", B)])
        yield
    P.op("act", lambda e: e.activation(out=mean, in_=ps[A][:, :], func=AF.Copy), writes=[("ps", A), K_("mean")])
    P.op("dve", lambda e: e.tensor_tensor(out=tmp[:, 0], in0=mean, in1=mean, op=ALU.mult), reads=[K_("mean")], writes=[K_("tmp", 0)])
    P.op("dve", lambda e: e.tensor_tensor(out=rstd, in0=ps[B][:, :], in1=tmp[:, 0], op=ALU.subtract),
         reads=[K_("tmp", 0)], writes=[("ps", B), K_("rstd")])
    P.op("act", lambda e: e.activation(out=rstd, in_=rstd, func=AF.Ln, bias=eps, scale=1.0), writes=[K_("rstd")])
    P.op("act", lambda e: e.activation(out=rstd, in_=rstd, func=AF.Exp, scale=-0.5), writes=[K_("rstd")])
    yield
    for c in range(KC):
        tb_ = tmp[:, c % 2]
        P.op("dve", lambda e, c=c, tb_=tb_: e.tensor_tensor(out=tb_, in0=hT[:, c, sl], in1=mean, op=ALU.subtract),
             reads=[("hT", c, t), K_("mean")], writes=[K_("tmp", c % 2)])
        P.op("dve", lambda e, tb_=tb_: e.tensor_tensor(out=tb_, in0=tb_, in1=rstd, op=ALU.mult),
             reads=[K_("rstd")], writes=[K_("tmp", c % 2)])
        emit_affine(k, "act", hT[:, c, sl], tb_, k.lng[:, l, j, c:c + 1], k.lnb[:, l, j, c:c + 1],
                    reads=[K_("tmp", c % 2), "lng", "lnb"], writes=[("hT", c, t)])
        if nxt is not None:
            mvf, l2, jsc, jsh = nxt
            emit_affine(k, "act", uT[:, c, sl], hT[:, c, sl], mvf(l2, jsc, c, b), mvf(l2, jsh, c, b),
                        reads=[("hT", c, t), "mod"], writes=[("uT", c, t)])
        yield


def _run_rr(gens):
    gens = list(gens)
    while gens:
        for g_ in list(gens):
            try:
                next(g_)
            except StopIteration:
                gens.remove(g_)


def post_mixer(k, l, b, wout, nxt, early=None):
    P, ps, hT, uT, aT, wk, dr = k.P, k.ps, k.hT, k.uT, k.aT, k.wk, k.dr
    wk.mark()
    sg = wk.alloc([2, TB], F32)
    wd = wk.alloc([2, NFT, 128], BF16)
    Ls = []
    for sid, (A, B) in enumerate(((6, 7), (4, 5))):
        Ls.append(dict(sq=wk.alloc([2, TB], BF16), xb=wk.alloc([2, TB], BF16), mean=wk.alloc([TB], F32), rstd=wk.alloc([TB], F32),
                       tmp=wk.alloc([2, TB], F32), A=A, B=B, sid=sid))
    mvf = (k.mv, l, 4, 3)
    outproj(k, l, b, wout, blocks=(0, 1))
    first = [ln_gen(k, l, 0, b, 0, mvf, Ls[0]), ln_gen(k, l, 0, b, 1, mvf, Ls[1])]
    outproj(k, l, b, wout, blocks=(2, 3), pending=first)
    _run_rr(first)
    pending = [ln_gen(k, l, 0, b, 2, mvf, Ls[0]), ln_gen(k, l, 0, b, 3, mvf, Ls[1])]
    it = 0
    for sb in range(2):
        for ft in range(NFT):
            sgw = k.load_w(dr["ffg"][l, ft])
            suw = k.load_w(dr["ffu"][l, ft])
            for bi in range(2):
                t = 2 * sb + bi
                gb, ub = it % 2, 2 + it % 2
                P.mm_group([lambda e, kc=kc, t=t, gb=gb, sgw=sgw: e.matmul(
                    ps[gb][:, :], lhsT=k.wbuf[:, sgw, kc, :], rhs=uT[:, kc, t * TB:(t + 1) * TB],
                    start=(kc == 0), stop=(kc == KC - 1)) for kc in range(KC)],
                    reads=[("wbuf", sgw)] + [("uT", kc, t) for kc in range(KC)], writes=[("ps", gb)])
                P.mm_group([lambda e, kc=kc, t=t, ub=ub, suw=suw: e.matmul(
                    ps[ub][:, :], lhsT=k.wbuf[:, suw, kc, :], rhs=uT[:, kc, t * TB:(t + 1) * TB],
                    start=(kc == 0), stop=(kc == KC - 1)) for kc in range(KC)],
                    reads=[("wbuf", suw)] + [("uT", kc, t) for kc in range(KC)], writes=[("ps", ub)])
                P.op("act", lambda e, gb=gb, it=it: e.activation(out=sg[:, it % 2], in_=ps[gb][:, :], func=AF.Silu),
                     writes=[("ps", gb), ("sg", it % 2)])
                P.op("dve", lambda e, ub=ub, it=it, ft=ft, bi=bi: e.tensor_tensor(
                    out=aT[:, ft, bi * TB:(bi + 1) * TB], in0=sg[:, it % 2], in1=ps[ub][:, :], op=ALU.mult),
                    reads=[("sg", it % 2)], writes=[("ps", ub), ("aT", ft, bi)])
                it += 1
            for g_ in list(pending):
                try:
                    next(g_)
                except StopIteration:
                    pending.remove(g_)
        _run_rr(pending)
        pending = []
        if sb == 1 and early is not None:
            early()
        for fo in range(KC):
            P.dma("pool", wd[:, fo % 2], dr["ffd"][l, fo], writes=[("wd", fo % 2)])
            for bi in range(2):
                t = 2 * sb + bi
                bank = 4 + bi
                P.mm_group([lambda e, kk=kk, bi=bi, bank=bank, fo=fo: e.matmul(
                    ps[bank][:, :], lhsT=wd[:, fo % 2, kk, :], rhs=aT[:, kk, bi * TB:(bi + 1) * TB],
                    start=(kk == 0), stop=(kk == NFT - 1)) for kk in range(NFT)],
                    reads=[("wd", fo % 2)] + [("aT", kk, bi) for kk in range(NFT)], writes=[("ps", bank)])
                P.op("dve", lambda e, t=t, bank=bank, fo=fo: e.scalar_tensor_tensor(
                    out=hT[:, fo, t * TB:(t + 1) * TB], in0=ps[bank][:, :], scalar=k.mv(l, 5, fo, b),
                    in1=hT[:, fo, t * TB:(t + 1) * TB], op0=ALU.mult, op1=ALU.add),
                    reads=["mod"], writes=[("ps", bank), ("hT", fo, t)])
        lns = [ln_gen(k, l, 1, b, 2 * sb, nxt, Ls[0]), ln_gen(k, l, 1, b, 2 * sb + 1, nxt, Ls[1])]
        if sb == 0:
            pending = lns
        else:
            _run_rr(lns)
    wk.release()


def mixer1(k, b):
    P, ps, uT, yT, wk, dr = k.P, k.ps, k.uT, k.yT, k.wk, k.dr
    wk.mark()
    vtok = wk.alloc([3, 16, 128], BF16)
    qk = wk.alloc([2, 2, S], BF16)
    tab = wk.alloc([2, 5, 128], F32)
    ssb = wk.alloc([2, 256], F32)
    pT = wk.alloc([3, 256], BF16)
    numacc = k.scr_tail[0][:, k.scr_tail[1]:k.scr_tail[1] + 2 * S].bitcast(F32)
    denacc = wk.alloc([S], F32)
    qscale = 128.0 ** -0.5
    SB = (0, 1)
    NB = (2, 3)
    DB = (4, 5)
    PB = (6, 7)
    cnt = {"s": 0, "p": 0, "pb": 0, "vs": 0}

    def vproj_pieces(hd, g):
        win, dl = DIL[g]
        tpc = (S // dl) // 128
        out = []
        state = {}

        def loadw():
            state["s0"] = k.load_w(dr["dwin"][g * 24 + 16 + hd])
        for r in range(dl):
            for kt in range(tpc):
                ti = r * tpc + kt

                def piece(r=r, kt=kt, ti=ti):
                    if ti == 0:
                        loadw()
                    s0 = state["s0"]
                    qd = ti % 4
                    if qd == 0:
                        state["bank"] = PB[cnt["pb"] % 2]
                        cnt["pb"] += 1
                    bank = state["bank"]
                    P.mm_group([lambda e, kc=kc: e.matmul(
                        ps[bank][:, qd * 128:(qd + 1) * 128], lhsT=_tok_ap(uT[:, kc, :], 128 * kt, 128, dl, r),
                        rhs=k.wbuf[:, s0, kc, :], start=(kc == 0), stop=(kc == KC - 1)) for kc in range(KC)],
                        reads=[("wbuf", s0)] + [("uT", kc, tt) for kc in range(KC) for tt in range(NTB)],
                        writes=[("ps", bank)])
                    if qd == 3:
                        P.op("act", lambda e: e.activation(
                            out=vtok[:, g, ti - 3:ti + 1, :], in_=ps[bank][:, :].rearrange("p (a b) -> p a b", a=4), func=AF.Copy),
                            writes=[("ps", bank), ("vtok", g)])
                out.append(piece)
        return out

    def qkproj_pieces(hd, g, st_):
        out = []
        state = {}
        for m in range(2):
            for t in range(NTB):
                def piece(m=m, t=t):
                    if t == 0:
                        state["s0"] = k.load_w(dr["dwin"][g * 24 + m * 8 + hd])
                    s0 = state["s0"]
                    bank = PB[cnt["pb"] % 2]
                    cnt["pb"] += 1
                    P.mm_group([lambda e, kc=kc: e.matmul(
                        ps[bank][:, :], lhsT=k.wbuf[:, s0, kc, :], rhs=uT[:, kc, t * TB:(t + 1) * TB],
                        start=(kc == 0), stop=(kc == KC - 1)) for kc in range(KC)],
                        reads=[("wbuf", s0)] + [("uT", kc, t) for kc in range(KC)], writes=[("ps", bank)])
                    dst = qk[:, st_, m, t * TB:(t + 1) * TB]
                    if m == 0:
                        P.op("act", lambda e: e.activation(out=dst, in_=ps[bank][:, :], func=AF.Copy, scale=qscale),
                             writes=[("ps", bank), ("qk", st_, 0)])
                    else:
                        P.op("dve", lambda e: e.tensor_copy(out=dst, in_=ps[bank][:, :]), writes=[("ps", bank), ("qk", st_, 1)])
                out.append(piece)
        return out

    units = [(hd, g) for hd in range(8) for g in range(3)]
    for g in range(3):
        for pc in vproj_pieces(0, g):
            pc()
    for pc in qkproj_pieces(0, 0, 0):
        pc()
    P.dma("sp", tab[:, 0], dr["tab1"][0], writes=[("tab", 0)])

    for ui, (hd, g) in enumerate(units):
        win, dl = DIL[g]
        tpc = (S // dl) // 128
        st_ = ui % 2
        qT = qk[:, st_, 0]
        kT = qk[:, st_, 1]
        tb = tab[:, hd % 2]
        filler = []
        if ui + 1 < len(units):
            nh, ng = units[ui + 1]
            filler += qkproj_pieces(nh, ng, (ui + 1) % 2)
        if g == 0:
            filler += vproj_pieces(hd, 2) if hd > 0 else []
            if hd + 1 < 8:
                P.dma("sp", tab[:, (hd + 1) % 2], dr["tab1"][hd + 1], writes=[("tab", (hd + 1) % 2)])
        if g == 2 and hd + 1 < 8:
            filler += vproj_pieces(hd + 1, 0) + vproj_pieces(hd + 1, 1)
        items = [(r, kt) for r in range(dl) for kt in range(tpc)]
        per_item = -(-len(filler) // len(items)) if filler else 0

        def stage_a(it):
            r, kt = it
            nq = 2 if kt + 1 < tpc else 1
            si = cnt["s"] % 2
            cnt["s"] += 1
            sb_ = SB[si]
            lk = _tok_ap(kT, 128 * kt, 128, dl, r)
            rq = _tok_ap(qT, 128 * kt, 128 * nq, dl, r)
            P.mm_group([lambda e: e.matmul(ps[sb_][:, 0:128 * nq], lhsT=lk, rhs=rq, start=True, stop=True)],
                       reads=[("qk", st_, 0), ("qk", st_, 1)], writes=[("ps", sb_)])
            return dict(si=si, sb=sb_, nq=nq)

        def stage_b(it, sd):
            si, sb_, nq = sd["si"], sd["sb"], sd["nq"]
            t0 = 4 if g == 2 else 2 * g
            tsl = tb[:, t0:t0 + nq, :].rearrange("p a b -> p (a b)")
            P.op("dve", lambda e: e.tensor_tensor(out=ssb[:, si, 0:128 * nq], in0=ps[sb_][:, 0:128 * nq], in1=tsl, op=ALU.add),
                 reads=[("tab", hd % 2)], writes=[("ps", sb_), ("ssb", si)])
            pi = cnt["p"] % 3
            cnt["p"] += 1
            P.op("act", lambda e: e.activation(out=pT[:, pi, 0:128 * nq], in_=ssb[:, si, 0:128 * nq], func=AF.Exp),
                 reads=[("ssb", si)], writes=[("pT", pi)])
            sd["pi"] = pi

        def stage_c(it, sd):
            r, kt = it
            pi, nq = sd["pi"], sd["nq"]
            ti = r * tpc + kt
            vt = vtok[:, g, ti, :]
            nb, db = NB[kt % 2], DB[kt % 2]
            P.mm_group([lambda e: e.matmul(ps[nb][:, 0:128], lhsT=vt, rhs=pT[:, pi, 0:128], start=(kt == 0), stop=True)],
                       reads=[("vtok", g), ("pT", pi)], writes=[("ps", nb)])
            P.mm_group([lambda e: e.matmul(ps[db][:, 0:128], lhsT=k.onesbf, rhs=pT[:, pi, 0:128], start=(kt == 0), stop=True)],
                       reads=["onesbf", ("pT", pi)], writes=[("ps", db)])
            na = _tok_ap(numacc, 128 * kt, 128, dl, r)
            da = _tok_ap(denacc, 128 * kt, 128, dl, r)
            blks = [kt // 4] if g == 0 else list(range(NTB))
            nk = [("numacc", tt) for tt in blks]
            dk = [("denacc", tt) for tt in blks]
            if g == 0:
                P.op("act", lambda e: e.activation(out=na, in_=ps[nb][:, 0:128], func=AF.Copy), writes=[("ps", nb)] + nk)
                P.op("act", lambda e: e.activation(out=da, in_=ps[db][:, 0:128], func=AF.Copy), writes=[("ps", db)] + dk)
            else:
                P.op("dve", lambda e: e.tensor_tensor(out=na, in0=ps[nb][:, 0:128], in1=na, op=ALU.add), writes=[("ps", nb)] + nk)
                P.op("dve", lambda e: e.tensor_tensor(out=da, in0=ps[db][:, 0:128], in1=da, op=ALU.add), writes=[("ps", db)] + dk)
            if nq == 2:
                nb2, db2 = NB[(kt + 1) % 2], DB[(kt + 1) % 2]
                P.mm_group([lambda e: e.matmul(ps[nb2][:, 0:128], lhsT=vt, rhs=pT[:, pi, 128:256], start=True, stop=False)],
                           reads=[("vtok", g), ("pT", pi)], writes=[("ps", nb2)])
                P.mm_group([lambda e: e.matmul(ps[db2][:, 0:128], lhsT=k.onesbf, rhs=pT[:, pi, 128:256], start=True, stop=False)],
                           reads=["onesbf", ("pT", pi)], writes=[("ps", db2)])

        sds = {0: stage_a(items[0])}
        fi = 0
        for ii, it in enumerate(items):
            if ii + 1 < len(items):
                sds[ii + 1] = stage_a(items[ii + 1])
            for _ in range(per_item):
                if fi < len(filler):
                    filler[fi]()
                    fi += 1
            stage_b(it, sds[ii])
            stage_c(it, sds[ii])
            del sds[ii]
        while fi < len(filler):
            filler[fi]()
            fi += 1
        if g == 2:
            for t in range(NTB):
                sl = slice(t * TB, (t + 1) * TB)
                P.op("act", lambda e, sl=sl: e.activation(out=denacc[:, sl], in_=denacc[:, sl], func=AF.Ln), writes=[("denacc", t)])
                P.op("act", lambda e, sl=sl: e.activation(out=denacc[:, sl], in_=denacc[:, sl], func=AF.Exp, scale=-1.0), writes=[("denacc", t)])
                P.op("dve", lambda e, sl=sl, hd=hd: e.tensor_tensor(out=yT[:, hd, sl], in0=numacc[:, sl], in1=denacc[:, sl], op=ALU.mult),
                     reads=[("numacc", t), ("denacc", t)], writes=[("yT", hd, t)])
    wk.release()


def _shared_inputs(inp, has0, has1):
    f = lambda a: np.ascontiguousarray(np.asarray(a, dtype=np.float32))
    sh = {}
    ada_w = f(inp["ada_w"])
    sh["adaw"] = np.stack([_wtiles(ada_w[l], [np.arange(g * 768, (g + 1) * 768) for g in range(8)]) for l in range(2)], 0)
    sh["adab"] = f(f(inp["ada_b"]).reshape(2, 48, 128).transpose(2, 0, 1))
    sh["lng"] = f(f(inp["ln_g"]).reshape(2, 2, KC, 128).transpose(3, 0, 1, 2))
    sh["lnb"] = f(f(inp["ln_b"]).reshape(2, 2, KC, 128).transpose(3, 0, 1, 2))
    t128 = lambda M: [np.arange(i * 128, (i + 1) * 128) for i in range(M // 128)]
    sh["ffg"] = np.stack([_wtiles(f(inp["ffn_w_gate"])[l], t128(DFF)) for l in range(2)], 0)
    sh["ffu"] = np.stack([_wtiles(f(inp["ffn_w_up"])[l], t128(DFF)) for l in range(2)], 0)
    sh["ffd"] = np.stack([_wtiles(f(inp["ffn_w_down"])[l], t128(D)) for l in range(2)], 0)
    rel_bias = f(inp["rel_bias"])
    ct = _const_tables(rel_bias)
    sh["b31"] = ct["b31"]
    if has1:
        sh["dwin"] = _wtiles(f(inp["dil_w_in"])[0], t128(9216))
        sh["dwout"] = _wtiles(f(inp["dil_w_out"])[0], t128(D))
        sh["tab1"] = ct["tab1"]
    if has0:
        sh.update(_layer0_inputs(inp, ct))
    return sh


def _layer0_inputs(inp, ct):
    f = lambda a: np.ascontiguousarray(np.asarray(a, dtype=np.float32))
    sh = {}
    W = f(inp["ab_w_in"])[0]
    ar = np.arange
    tiles = []
    sw = np.concatenate([ar(64, 128), ar(0, 64)])
    for hh in range(4):
        tiles += [128 * hh + ar(128), 128 * hh + sw, 512 + 128 * hh + ar(128), 512 + 128 * hh + sw,
                  1536 + 128 * hh + ar(128), 1024 + 128 * hh + ar(128)]
    for g in range(2):
        kc_, vc_ = 2560 + g * 64 + ar(64), 2688 + g * 64 + ar(64)
        ks_, kw_ = 2816 + g * 64 + ar(64), 3072 + g * 64 + ar(64)
        tiles += [2048 + g * 256 + ar(128), 2048 + g * 256 + 128 + ar(128), np.concatenate([kc_, vc_]),
                  np.concatenate([ks_, ks_]), np.concatenate([kw_, kw_])]
    sh["abw"] = _wtiles(W, tiles)
    tok = []
    for g in range(2):
        tok.append(np.concatenate([2944 + g * 64 + ar(64), 3200 + g * 64 + ar(64), 3328 + g * 12 + ar(12)]))
    sh["abtok"] = _wtiles(W, tok)
    sh["abwout"] = _wtiles(f(inp["ab_w_out"])[0], [ar(i * 128, (i + 1) * 128) for i in range(8)])
    sh["tab0"] = ct["tab0"]
    sh["m4"] = ct["m4"]
    sh["cmpt"] = ct["cmpt"]
    w1k = f(inp["cmp_k_w1"])[0].reshape(32, 64, 256).transpose(1, 0, 2)
    w1v = f(inp["cmp_v_w1"])[0].reshape(32, 64, 256).transpose(1, 0, 2)
    sh["w1kv"] = f(np.concatenate([w1k, w1v], 0))
    w2k = f(inp["cmp_k_w2"])[0].reshape(2, 128, 64).transpose(1, 0, 2)
    w2v = f(inp["cmp_v_w2"])[0].reshape(2, 128, 64).transpose(1, 0, 2)
    sh["w2kv"] = f(np.concatenate([w2k, w2k, w2v], 2))
    sh["poskv"] = f(np.concatenate([f(inp["cmp_pos_k"])[0].T, f(inp["cmp_pos_v"])[0].T], 0))
    sh["gngb"] = f(np.stack([f(inp["ret_gn_g"])[0].reshape(4, 128).T, f(inp["ret_gn_b"])[0].reshape(4, 128).T], 1))
    inv = (np.float32(10000.0) ** (-(np.arange(0, 128, 2, dtype=np.float32)) / np.float32(128))).astype(np.float32)
    ang = (np.arange(S, dtype=np.float32)[None, :] * inv[:, None]).astype(np.float32)
    cos = np.cos(ang.astype(np.float64)).astype(np.float32)
    sin = np.sin(ang.astype(np.float64)).astype(np.float32)
    rope = np.zeros((128, 2, S), np.float32)
    rope[0:64, 0], rope[64:128, 0] = cos, cos
    rope[0:64, 1], rope[64:128, 1] = -sin, sin
    sh["rope"] = rope
    gam = 1.0 - 2.0 ** (-5.0 - np.arange(4, dtype=np.float64))
    kk = np.arange(128, dtype=np.float64)
    retc = np.zeros((128, 8 + 128 + 512), np.float64)
    for h in range(4):
        retc[:, h] = gam[h] ** (-(kk + 1))
        retc[:, 4 + h] = gam[h] ** (127 - kk)
        retc[:, 136 + 128 * h:136 + 128 * (h + 1)] = (gam[h] ** (kk + 1) * 128.0 ** -0.5)[None, :]
    retc[:, 8:136] = (kk[None, :] >= kk[:, None]).astype(np.float64)
    sh["retc"] = retc.astype(np.float32)
    selc = np.zeros((128, 2, 8, 32), np.float32)
    jj = np.arange(32)[None, :]
    for i in range(8):
        q = (8 + i) * 128 + np.arange(128)
        bq = (q // 64)[:, None]
        forced = (jj == 0) | (jj == bq) | (jj == bq - 1)
        valid = jj <= bq
        selc[:, 0, i] = (valid & ~forced).astype(np.float32)
        selc[:, 1, i] = np.where(forced, 1e4, np.where(valid, 0.0, -1.0)).astype(np.float32)
    sh["selc"] = selc
    cs = np.arange(127) * 16
    ss = np.arange(32) * 64
    ov = np.clip(np.minimum(cs[:, None] + 32, ss[None, :] + 64) - np.maximum(cs[:, None], ss[None, :]), 0, None) / 32.0
    sh["ovl"] = ov.astype(np.float32)
    ex = np.zeros((128, 16, 128), np.float32)
    for kt in range(16):
        for kq in range(128):
            ex[2 * kt + kq // 64, kt, kq] = 1.0
    sh["exm"] = ex
    return sh


_PROG_CACHE = {}


def run_layers(inp, layers, nseq, ncores):
    has0, has1 = 0 in layers, 1 in layers
    key = (nseq, tuple(layers))
    if key not in _PROG_CACHE:
        _PROG_CACHE[key] = build_program(nseq, list(layers), has0, has1)
    nc = _PROG_CACHE[key]
    sh = _shared_inputs(inp, has0, has1)
    x = np.asarray(inp["x"], dtype=np.float32)
    c = np.asarray(inp["c"], dtype=np.float32)
    in_maps = []
    for i in range(ncores):
        xs = x[i * nseq:(i + 1) * nseq]
        xT = np.ascontiguousarray(xs.reshape(nseq, S, KC, 128).transpose(0, 3, 2, 1))
        cs = c[i * nseq:(i + 1) * nseq]
        cT = np.zeros((128, KC, 4), np.float32)
        cT[:, :, 0:nseq] = cs.reshape(nseq, KC, 128).transpose(2, 1, 0)
        m = dict(sh)
        m["xT"] = xT
        m["cT"] = cT
        in_maps.append(m)
    res = run_bass_kernel_spmd(nc, in_maps, core_ids=list(range(ncores)))
    outs = []
    for i in range(ncores):
        oT = res.results[i]["outT"]
        outs.append(np.ascontiguousarray(oT.transpose(0, 3, 2, 1)).reshape(nseq, S, D))
    return np.concatenate(outs, 0).astype(np.float32)


def kernel(**inputs):
    return run_layers(inputs, (0, 1), 4, NCORES)


def _proj_fm(k, wsrc, evac):
    P, ps, uT = k.P, k.ps, k.uT
    s0 = k.load_w(wsrc)
    for t in range(NTB):
        bank = 6 + k.pbc % 2
        k.pbc += 1
        P.mm_group([lambda e, kc=kc, t=t, bank=bank, s0=s0: e.matmul(
            ps[bank][:, :], lhsT=k.wbuf[:, s0, kc, :], rhs=uT[:, kc, t * TB:(t + 1) * TB],
            start=(kc == 0), stop=(kc == KC - 1)) for kc in range(KC)],
            reads=[("wbuf", s0)] + [("uT", kc, t) for kc in range(KC)], writes=[("ps", bank)])
        evac(t, bank)


def mixer0(k, b):
    k.pbc = 0
    retention(k, b)
    k.P.barrier()
    if DEBUG.get('ret'):
        return
    nsa(k, b)


def retention(k, b):
    P, ps, uT, yT, wk, dr = k.P, k.ps, k.uT, k.yT, k.wk, k.dr
    wk.mark()
    rope = wk.alloc([2, S], F32)
    qk = wk.alloc([2, S], BF16)
    vtok = wk.alloc([16, 128], BF16)
    t12 = wk.alloc([2, 2, TB], F32)
    retc = wk.alloc([648], F32)
    gngb = wk.alloc([2, 4], F32)
    scsb = wk.alloc([2, 128], BF16)
    kd = wk.alloc([2, 128], BF16)
    prev32 = wk.alloc([128], F32)
    prevbf = wk.alloc([2, 128], BF16)
    sq = wk.alloc([TB], F32)
    mean = wk.alloc([TB], F32)
    rstd = wk.alloc([TB], F32)
    tmp = wk.alloc([TB], F32)
    tail, toff = k.scr_tail
    yraw = tail[:, toff:toff + 2 * S].bitcast(F32)
    gr = tail[:, toff + 2 * S:toff + 3 * S]
    P.dma("sp", rope, dr["rope"], writes=["rope"])
    P.dma("sp", retc, dr["retc"], writes=["retc"])
    P.dma("sp", gngb, dr["gngb"], writes=["gngb"])
    qT, kT = qk[:, 0], qk[:, 1]
    tri = retc[:, 8:136]
    gam = [1.0 - 2.0 ** (-5.0 - h) for h in range(4)]

    def pbank():
        bank = 6 + k.pbc % 2
        k.pbc += 1
        return bank

    def proj_mm(s0, t, bank):
        P.mm_group([lambda e, kc=kc: e.matmul(ps[bank][:, :], lhsT=k.wbuf[:, s0, kc, :], rhs=uT[:, kc, t * TB:(t + 1) * TB],
                                             start=(kc == 0), stop=(kc == KC - 1)) for kc in range(KC)],
                   reads=[("wbuf", s0)] + [("uT", kc, t) for kc in range(KC)], writes=[("ps", bank)])

    def p_qkv(hh):
        for m in range(2):
            dst = qk[:, m]
            s_main = k.load_w(dr["abw"][hh * 6 + 2 * m])
            s_swap = k.load_w(dr["abw"][hh * 6 + 2 * m + 1])
            for t in range(NTB):
                sl = slice(t * TB, (t + 1) * TB)
                b0 = pbank()
                proj_mm(s_main, t, b0)
                t1 = t12[:, 0, t % 2]
                t2 = t12[:, 1, t % 2]
                P.op("dve", lambda e, b0=b0, t1=t1, sl=sl: e.tensor_tensor(out=t1, in0=ps[b0][:, :], in1=rope[:, 0, sl], op=ALU.mult),
                     reads=["rope"], writes=[("ps", b0), ("t1", t % 2)])
                b1 = pbank()
                proj_mm(s_swap, t, b1)
                P.op("dve", lambda e, b1=b1, t2=t2, sl=sl: e.tensor_tensor(out=t2, in0=ps[b1][:, :], in1=rope[:, 1, sl], op=ALU.mult),
                     reads=["rope"], writes=[("ps", b1), ("t2", t % 2)])
                P.op("dve", lambda e, dst=dst, t1=t1, t2=t2, sl=sl: e.tensor_tensor(out=dst[:, sl], in0=t1, in1=t2, op=ALU.add),
                     reads=[("t1", t % 2), ("t2", t % 2)], writes=[("rqk", m)])
                yield
        s0 = k.load_w(dr["abw"][hh * 6 + 5])
        for ti in range(16):
            qd = ti % 4
            if qd == 0:
                bank = pbank()
            P.mm_group([lambda e, kc=kc, ti=ti, bank=bank, qd=qd: e.matmul(
                ps[bank][:, qd * 128:(qd + 1) * 128], lhsT=uT[:, kc, ti * 128:(ti + 1) * 128], rhs=k.wbuf[:, s0, kc, :],
                start=(kc == 0), stop=(kc == KC - 1)) for kc in range(KC)],
                reads=[("wbuf", s0)] + [("uT", kc, ti // 4) for kc in range(KC)], writes=[("ps", bank)])
            if qd == 3:
                P.op("act", lambda e, ti=ti, bank=bank: e.activation(
                    out=vtok[:, ti - 3:ti + 1, :], in_=ps[bank][:, :].rearrange("p (a b) -> p a b", a=4), func=AF.Copy),
                    writes=[("ps", bank), "rvtok"])
                yield

    def p_g(hh):
        s0 = k.load_w(dr["abw"][hh * 6 + 4])
        for t in range(NTB):
            bank = pbank()
            proj_mm(s0, t, bank)
            P.op("act", lambda e, t=t, bank=bank: e.activation(out=gr[:, t * TB:(t + 1) * TB], in_=ps[bank][:, :], func=AF.Silu),
                 writes=[("ps", bank), "rgr"])
            yield

    def chunks(hh, filler):
        SBK = (0, 4)
        TBK = (2, 5)
        psTs = [ps[bk][:, :].bitcast(BF16) for bk in TBK]

        def st_a(n):
            cs = slice(n * 128, (n + 1) * 128)
            sbk = SBK[n % 2]
            P.mm_group([lambda e: e.matmul(ps[sbk][:, 0:128], lhsT=kT[:, cs], rhs=qT[:, cs], start=True, stop=True)],
                       reads=[("rqk", 0), ("rqk", 1)], writes=[("ps", sbk)])
            P.op("dve", lambda e: e.scalar_tensor_tensor(out=scsb[:, n % 2], in0=ps[sbk][:, 0:128], scalar=retc[:, hh:hh + 1],
                                                         in1=tri, op0=ALU.mult, op1=ALU.mult),
                 reads=["retc"], writes=[("ps", sbk), ("scsb", n % 2)])
            if n < 15:
                tbk = TBK[n % 2]
                psT = psTs[n % 2]
                P.mm_group([lambda e: e.transpose(out=psT[:, 0:128], in_=kT[:, cs], identity=k.ident)],
                           reads=[("rqk", 1), "ident"], writes=[("ps", tbk)])
                P.op("act", lambda e: e.activation(out=kd[:, n % 2], in_=psT[:, 0:128], func=AF.Identity,
                                                   scale=retc[:, 4 + hh:5 + hh], bias=0.0),
                     reads=["retc"], writes=[("ps", tbk), ("kd", n % 2)])

        def st_b(n):
            cs = slice(n * 128, (n + 1) * 128)
            fns = [lambda e: e.matmul(ps[1][:, 0:128], lhsT=vtok[:, n, :], rhs=scsb[:, n % 2], start=True, stop=(n == 0))]
            rd = ["rvtok", ("scsb", n % 2)]
            if n > 0:
                fns.append(lambda e: e.matmul(ps[1][:, 0:128], lhsT=prevbf[:, n % 2], rhs=qT[:, cs], start=False, stop=True))
                rd += [("prevbf", n % 2), ("rqk", 0)]
            P.mm_group(fns, reads=rd, writes=[("ps", 1)])
            P.op("dve", lambda e: e.tensor_tensor(out=yraw[:, cs], in0=ps[1][:, 0:128],
                                                  in1=retc[:, 136 + 128 * hh:136 + 128 * (hh + 1)], op=ALU.mult),
                 reads=["retc"], writes=[("ps", 1), "yraw"])
            if n < 15:
                P.mm_group([lambda e: e.matmul(ps[3][:, 0:128], lhsT=kd[:, n % 2], rhs=vtok[:, n, :], start=True, stop=True)],
                           reads=[("kd", n % 2), "rvtok"], writes=[("ps", 3)])
                if n == 0:
                    P.op("dve", lambda e: e.tensor_copy(out=prev32, in_=ps[3][:, 0:128]), writes=[("ps", 3), "prev32"])
                else:
                    gC = float(gam[hh] ** 128)
                    P.op("dve", lambda e: e.scalar_tensor_tensor(out=prev32, in0=prev32, scalar=gC, in1=ps[3][:, 0:128],
                                                                 op0=ALU.mult, op1=ALU.add),
                         writes=[("ps", 3), "prev32"])
                P.op("act", lambda e: e.activation(out=prevbf[:, (n + 1) % 2], in_=prev32, func=AF.Copy),
                     reads=["prev32"], writes=[("prevbf", (n + 1) % 2)])

        st_a(0)
        for n in range(16):
            if n + 1 < 16:
                st_a(n + 1)
            st_b(n)
            if filler is not None and n % 4 == 1:
                try:
                    next(filler)
                except StopIteration:
                    filler = None
        if filler is not None:
            for _ in filler:
                pass

    def gnorm(hh):
        for t in range(NTB):
            sl = slice(t * TB, (t + 1) * TB)
            P.op("act", lambda e, sl=sl: e.activation(out=sq, in_=yraw[:, sl], func=AF.Square), reads=["yraw"], writes=["rsq"])
            P.mm_group([lambda e, sl=sl: e.matmul(ps[4][:, :], lhsT=k.ones128, rhs=yraw[:, sl], start=True, stop=True)],
                       reads=["yraw", "ones128"], writes=[("ps", 4)])
            P.mm_group([lambda e: e.matmul(ps[5][:, :], lhsT=k.ones128, rhs=sq, start=True, stop=True)],
                       reads=["rsq", "ones128"], writes=[("ps", 5)])
            yield
            P.op("act", lambda e: e.activation(out=mean, in_=ps[4][:, :], func=AF.Copy), writes=[("ps", 4), "rmean"])
            P.op("dve", lambda e: e.tensor_tensor(out=tmp, in0=mean, in1=mean, op=ALU.mult), reads=["rmean"], writes=["rtmp"])
            P.op("dve", lambda e: e.tensor_tensor(out=rstd, in0=ps[5][:, :], in1=tmp, op=ALU.subtract), reads=["rtmp"], writes=[("ps", 5), "rrstd"])
            P.op("act", lambda e: e.activation(out=rstd, in_=rstd, func=AF.Ln, bias=LN_EPS, scale=1.0), writes=["rrstd"])
            P.op("act", lambda e: e.activation(out=rstd, in_=rstd, func=AF.Exp, scale=-0.5), writes=["rrstd"])
            yield
            P.op("dve", lambda e, sl=sl: e.tensor_tensor(out=tmp, in0=yraw[:, sl], in1=mean, op=ALU.subtract), reads=["yraw", "rmean"], writes=["rtmp"])
            P.op("dve", lambda e: e.tensor_tensor(out=tmp, in0=tmp, in1=rstd, op=ALU.mult), reads=["rrstd"], writes=["rtmp"])
            emit_affine(k, "act", tmp, tmp, gngb[:, 0, hh:hh + 1], gngb[:, 1, hh:hh + 1], reads=["gngb"], writes=["rtmp"])
            P.op("dve", lambda e, sl=sl: e.tensor_tensor(out=yT[:, hh, sl], in0=tmp, in1=gr[:, sl], op=ALU.mult),
                 reads=["rtmp", "rgr"], writes=[("yT", hh, t)])
            yield

    dbg = DEBUG.get("ret")
    for _ in p_qkv(0):
        pass
    for hh in range(4):
        chunks(hh, p_g(hh))
        if dbg:
            for t in range(NTB):
                sl = slice(t * TB, (t + 1) * TB)
                P.op("dve", lambda e, sl=sl, hh=hh: e.tensor_copy(out=yT[:, hh, sl], in_=yraw[:, sl]), reads=["yraw"], writes=[("yT", hh, t)])
                P.op("dve", lambda e, sl=sl, hh=hh: e.tensor_copy(out=yT[:, 4 + hh, sl], in_=gr[:, sl]), reads=["rgr"], writes=[("yT", 4 + hh, t)])
            if hh + 1 < 4:
                for _ in p_qkv(hh + 1):
                    pass
            continue
        gens = [gnorm(hh)]
        if hh + 1 < 4:
            gens.append(p_qkv(hh + 1))
        _run_rr(gens)
    wk.release()


def nsa(k, b):
    P, ps, uT, yT, wk, dr = k.P, k.ps, k.uT, k.yT, k.wk, k.dr
    wk.mark()
    qn = wk.alloc([2, S], BF16)
    kcvc = wk.alloc([S], BF16)
    kslc = wk.alloc([S], BF16)
    kwin = wk.alloc([S], BF16)
    vaug = wk.alloc([16, 2, 65], BF16)
    sig = wk.alloc([16, 12], F32)
    shared = wk.alloc([2240], BF16)
    wtok = shared[:, 0:1120].rearrange('p (a b) -> p a b', a=KC)
    hidraw = wk.alloc([512], BF16)
    hid = hidraw[:, 0:508].rearrange('p (a b c) -> p a b c', a=2, b=2)
    kcT = wk.alloc([127], BF16)
    vcaug = wk.alloc([97], BF16)
    ovl32 = wk.alloc([32], F32)
    w1buf = shared[:, 1120:2144].rearrange('p (a b c) -> p a b c', a=2, b=2)
    ctab = shared[:, 0:2048].rearrange('p (a b c) -> p a b c', a=2, b=2)
    zpad = wk.alloc([224], BF16)
    w2 = wk.alloc([2, 192], BF16)
    posb = wk.alloc([32], BF16)
    cmpbias = wk.alloc([TB], F32)
    tab = cmpbias[:, 0:256].rearrange('p (a b) -> p a b', a=2)
    tabtmp = cmpbias[:, 256:512].rearrange('p (a b) -> p a b', a=2)
    Ocmp = wk.alloc([4, 4, 97], F32)
    Oraw = wk.alloc([4, 2, 65], F32)
    tabhl = wk.alloc([4, 2, 2, 128], BF16)
    m4b = wk.alloc([128], BF16)
    ynb = wk.alloc([4, 256], BF16)
    selc = wk.alloc([2, 8, 32], F32)
    tk = wk.alloc([160], F32)
    negb = wk.alloc([32], BF16)
    fbuf = wk.alloc([4, 3], F32)
    acc32 = hidraw[:, 0:256].bitcast(F32).rearrange('p (a b) -> p a b', a=2)
    tail, toff = k.scr_tail
    negT = tail[:, toff:toff + S]
    exm = tail[:, toff + S:toff + 2 * S].rearrange("p (a b) -> p a b", a=16)
    Ebuf = tail[:, toff + 2 * S:toff + 2 * S + 3 * TB].rearrange("p (a b) -> p a b", a=3)

    P.dma("pool", exm, dr["exm"], writes=["exm"])
    P.op("dve", lambda e: e.memset(negT, 0.0), writes=["negT"])
    P.dma("pool", m4b, dr["m4"], writes=["m4b"])
    P.dma("sp", selc, dr["selc"], writes=["selc"])
    P.dma("pool", w2, dr["w2kv"], writes=["w2"])
    P.dma("pool", posb, dr["poskv"], writes=["posb"])
    P.dma("sp", ovl32[0:127, :], dr["ovl"], writes=["ovl32"])
    P.op("dve", lambda e: e.memset(vaug[:, :, :, 64:65], 1.0), writes=["vaug"])
    P.op("dve", lambda e: e.memset(zpad[:, 128:224], 0.0), writes=["zpad"])
    P.op("dve", lambda e: e.tensor_copy(out=zpad[:, 0:128], in_=k.ident), reads=["ident"], writes=["zpad"])
    P.op("dve", lambda e: e.memset(vcaug[:, 64:65], 1.0), writes=["vcaug"])
    P.op("dve", lambda e: e.tensor_copy(out=vcaug[0:127, 65:97], in_=ovl32[0:127, :]), reads=["ovl32"], writes=["vcaug"])

    if not k.hc_done:
        k.hc_done = True
        for head in range(8):
            P.dma("sp", cmpbias, dr["cmpt"][head], writes=["cmpbias"])
            P.op("dve", lambda e, head=head: e.tensor_scalar(out=cmpbias, in0=cmpbias, scalar1=k.b31[:, head:head + 1], scalar2=None,
                                                             op0=ALU.subtract), reads=["b31"], writes=["cmpbias"])
            P.op("dve", lambda e: e.tensor_copy(out=ctab[:, 0, 0, :], in_=cmpbias), reads=["cmpbias"], writes=["ctab_s"])
            P.op("dve", lambda e: e.tensor_tensor(out=cmpbias, in0=cmpbias, in1=ctab[:, 0, 0, :], op=ALU.subtract),
                 reads=["ctab_s"], writes=["cmpbias"])
            P.op("dve", lambda e: e.tensor_copy(out=ctab[:, 0, 1, :], in_=cmpbias), reads=["cmpbias"], writes=["ctab_s"])
            P.dma("sp", dr["cscr"][head], ctab[:, 0], reads=["ctab_s"], writes=[("cscr", head)])
        P.barrier()
        for pc in range(16):
            P.dma("pool", w1buf[:, pc % 2], dr["w1kv"][:, 2 * pc:2 * pc + 2, :], writes=[("w1buf", pc % 2)])
            for tt in range(2):
                t = 2 * pc + tt
                for kv in range(2):
                    pb = 64 * kv
                    for mt in range(2):
                        bank = kv * 2 + mt
                        P.mm_group([lambda e, pb=pb, pc=pc, tt=tt, mt=mt, bank=bank, t=t: e.matmul(
                            ps[bank][:, 0:1], lhsT=w1buf[pb:pb + 64, pc % 2, tt, mt * 128:(mt + 1) * 128],
                            rhs=posb[pb:pb + 64, t:t + 1], start=(t == 0), stop=(t == 31))],
                            reads=[("w1buf", pc % 2), "posb"], writes=[("ps", bank)])
        for i in range(4):
            P.op("dve", lambda e, i=i: e.tensor_copy(out=k.hc[:, i:i + 1], in_=ps[i][:, 0:1]), writes=[("ps", i), "hc"])

    for g in range(2):
        base = 24 + g * 5
        for pr in range(2):
            def ev_q(t, bank, pr=pr):
                P.op("act", lambda e: e.activation(out=qn[:, pr, t * TB:(t + 1) * TB], in_=ps[bank][:, :], func=AF.Copy, scale=0.125),
                     writes=[("ps", bank), ("qn", pr)])
            _proj_fm(k, dr["abw"][base + pr], ev_q)
        for idx, dst, nm in ((2, kcvc, "kcvc"), (3, kslc, "kslc"), (4, kwin, "kwin")):
            def ev_k(t, bank, dst=dst, nm=nm):
                P.op("dve", lambda e: e.tensor_copy(out=dst[:, t * TB:(t + 1) * TB], in_=ps[bank][:, :]), writes=[("ps", bank), nm])
            _proj_fm(k, dr["abw"][base + idx], ev_k)
        P.dma("pool", wtok, dr["abtok"][g], writes=["wtok"])
        for ti in range(16):
            bank = 6 + k.pbc % 2
            k.pbc += 1
            P.mm_group([lambda e, kc=kc, ti=ti, bank=bank: e.matmul(
                ps[bank][:, 0:140], lhsT=uT[:, kc, ti * 128:(ti + 1) * 128], rhs=wtok[:, kc, :],
                start=(kc == 0), stop=(kc == KC - 1)) for kc in range(KC)],
                reads=["wtok"] + [("uT", kc, ti // 4) for kc in range(KC)], writes=[("ps", bank)])
            P.op("act", lambda e, ti=ti, bank=bank: e.activation(
                out=vaug[:, ti, :, 0:64], in_=ps[bank][:, 0:128].rearrange("p (a b) -> p a b", a=2), func=AF.Copy),
                writes=[("ps", bank), "vaug"])
            P.op("act", lambda e, ti=ti, bank=bank: e.activation(out=sig[:, ti, :], in_=ps[bank][:, 128:140], func=AF.Sigmoid),
                 writes=[("ps", bank), "sig"])
        for pc in range(16):
            P.dma("pool", w1buf[:, pc % 2], dr["w1kv"][:, 2 * pc:2 * pc + 2, :], writes=[("w1buf", pc % 2)])
            for tt in range(2):
                t = 2 * pc + tt
                for kv in range(2):
                    pb = 64 * kv
                    rhs = kcvc[pb:pb + 64, :].rearrange("p (l s) -> p l s", s=16)[:, 0:127, t % 16] if t < 16 else \
                        kcvc[pb:pb + 64, :].rearrange("p (l s) -> p l s", s=16)[:, 1:128, t - 16]
                    for mt in range(2):
                        bank = kv * 2 + mt
                        P.mm_group([lambda e, pb=pb, pc=pc, tt=tt, mt=mt, bank=bank, t=t, rhs=rhs: e.matmul(
                            ps[bank][:, 0:127], lhsT=w1buf[pb:pb + 64, pc % 2, tt, mt * 128:(mt + 1) * 128],
                            rhs=rhs, start=(t == 0), stop=(t == 31))],
                            reads=[("w1buf", pc % 2), "kcvc"], writes=[("ps", bank)])
        for kv in range(2):
            for mt in range(2):
                bank = kv * 2 + mt
                P.op("act", lambda e, kv=kv, mt=mt, bank=bank: e.activation(
                    out=hid[:, kv, mt, :], in_=ps[bank][:, 0:127], func=AF.Silu, bias=k.hc[:, bank:bank + 1], scale=1.0),
                    reads=["hc"], writes=[("ps", bank), "hid"])
        P.mm_group([lambda e, mt=mt: e.matmul(ps[4][:, 0:127], lhsT=w2[:, mt, 0:128], rhs=hid[:, 0, mt, :], start=(mt == 0), stop=(mt == 1))
                    for mt in range(2)], reads=["w2", "hid"], writes=[("ps", 4)])
        P.op("dve", lambda e: e.tensor_copy(out=kcT, in_=ps[4][:, 0:127]), writes=[("ps", 4), "kcT"])
        P.mm_group([lambda e, mt=mt: e.matmul(ps[5][0:127, 0:64], lhsT=hid[:, 1, mt, :], rhs=w2[:, mt, 128:192], start=(mt == 0), stop=(mt == 1))
                    for mt in range(2)], reads=["w2", "hid"], writes=[("ps", 5)])
        P.op("act", lambda e: e.activation(out=vcaug[0:127, 0:64], in_=ps[5][0:127, 0:64], func=AF.Copy), writes=[("ps", 5), "vcaug"])

        P.barrier()
        for r in range(4):
            head = g * 4 + r
            P.dma("sp", tab, dr["tab0"][head], writes=["cmpbias"])
            P.op("dve", lambda e, head=head: e.tensor_scalar(out=tab, in0=tab, scalar1=k.b31[:, head:head + 1], scalar2=None, op0=ALU.subtract),
                 reads=["b31"], writes=["cmpbias"])
            P.op("dve", lambda e, r=r: e.tensor_copy(out=tabhl[:, r, :, 0, :], in_=tab), reads=["cmpbias"], writes=["tabhl"])
            P.op("dve", lambda e, r=r: e.tensor_tensor(out=tabtmp, in0=tab, in1=tabhl[:, r, :, 0, :], op=ALU.subtract),
                 reads=["tabhl"], writes=["cmpbias"])
            P.op("dve", lambda e, r=r: e.tensor_copy(out=tabhl[:, r, :, 1, :], in_=tabtmp), reads=["cmpbias"], writes=["tabhl"])
        sc = {"s": 0, "e": 0, "c": 0}
        for qb in range(4):
            q0 = qb * TB
            ncv = min(127, 32 * qb + 31)
            zs = 96 - 32 * qb
            for r in range(4):
                head = g * 4 + r
                pb = 64 * (r % 2)
                qv = qn[pb:pb + 64, r // 2, q0:q0 + TB]
                cslot = sc["c"] % 2
                sc["c"] += 1
                P.dma("sp", ctab[:, cslot], dr["cscr"][head], reads=[("cscr", head)], writes=[("ctab", cslot)])
                sb_ = sc["s"] % 2
                sc["s"] += 1
                P.mm_group([lambda e, sb_=sb_, pb=pb, qv=qv, ncv=ncv: e.matmul(ps[sb_][0:ncv, :], lhsT=kcT[pb:pb + 64, 0:ncv], rhs=qv, start=True, stop=False),
                            lambda e, sb_=sb_, ncv=ncv, zs=zs, cslot=cslot: e.matmul(ps[sb_][0:ncv, :], lhsT=zpad[:, zs:zs + ncv], rhs=ctab[:, cslot, 0, :],
                                                                                   start=False, stop=False),
                            lambda e, sb_=sb_, ncv=ncv, zs=zs, cslot=cslot: e.matmul(ps[sb_][0:ncv, :], lhsT=zpad[:, zs:zs + ncv], rhs=ctab[:, cslot, 1, :],
                                                                                   start=False, stop=True)],
                           reads=["kcT", ("qn", r // 2), "zpad", ("ctab", cslot)], writes=[("ps", sb_)])
                ei = sc["e"] % 3
                sc["e"] += 1
                P.op("act", lambda e, sb_=sb_, ei=ei, ncv=ncv: e.activation(out=Ebuf[0:ncv, ei, :], in_=ps[sb_][0:ncv, :], func=AF.Exp),
                     writes=[("ps", sb_), ("E", ei)])
                for tq in range(4):
                    P.mm_group([lambda e, tq=tq, ei=ei, ncv=ncv: e.matmul(ps[6][:, tq * 97:(tq + 1) * 97], lhsT=Ebuf[0:ncv, ei, tq * 128:(tq + 1) * 128],
                                                                         rhs=vcaug[0:ncv, :], start=True, stop=True)],
                               reads=[("E", ei), "vcaug"], writes=[("ps", 6)])
                P.op("act", lambda e, r=r: e.activation(out=Ocmp[:, :, r, :], in_=ps[6][:, 0:388].rearrange("p (a b) -> p a b", a=4), func=AF.Copy),
                     writes=[("ps", 6), "Ocmp"])
            if qb >= 2:
                for tq in range(4):
                    i = 4 * qb + tq - 8
                    rdc, imp, scr, wrk, mx = tk[:, 0:4], tk[:, 8:40], tk[:, 40:72], tk[:, 72:104], tk[:, 104:120]
                    P.op("dve", lambda e, tq=tq, rdc=rdc: e.tensor_scalar(out=rdc, in0=Ocmp[:, tq, :, 64], scalar1=1e-30, scalar2=None, op0=ALU.max),
                         reads=["Ocmp"], writes=["tk"])
                    P.op("dve", lambda e, rdc=rdc: e.reciprocal(out=rdc, in_=rdc), writes=["tk"])
                    P.op("dve", lambda e, tq=tq, rdc=rdc, imp=imp: e.tensor_scalar(out=imp, in0=Ocmp[:, tq, 0, 65:97], scalar1=rdc[:, 0:1],
                                                                                 scalar2=None, op0=ALU.mult), reads=["Ocmp"], writes=["tk"])
                    for r in range(1, 4):
                        P.op("dve", lambda e, tq=tq, r=r, rdc=rdc, imp=imp: e.scalar_tensor_tensor(
                            out=imp, in0=Ocmp[:, tq, r, 65:97], scalar=rdc[:, r:r + 1], in1=imp, op0=ALU.mult, op1=ALU.add),
                            reads=["Ocmp"], writes=["tk"])
                    P.op("dve", lambda e, i=i, imp=imp, scr=scr: e.tensor_tensor(out=scr, in0=imp, in1=selc[:, 0, i, :], op=ALU.mult),
                         reads=["selc"], writes=["tk"])
                    P.op("dve", lambda e, i=i, scr=scr: e.tensor_tensor(out=scr, in0=scr, in1=selc[:, 1, i, :], op=ALU.add),
                         reads=["selc"], writes=["tk"])
                    P.op("dve", lambda e, scr=scr, mx=mx: e.max(out=mx[:, 0:8], in_=scr), writes=["tk"])
                    P.op("dve", lambda e, scr=scr, mx=mx, wrk=wrk: e.match_replace(out=wrk, in_to_replace=mx[:, 0:8], in_values=scr, imm_value=-3.0),
                         writes=["tk"])
                    P.op("dve", lambda e, mx=mx, wrk=wrk: e.max(out=mx[:, 8:16], in_=wrk), writes=["tk"])
                    P.op("dve", lambda e, scr=scr, mx=mx: e.tensor_scalar(out=negb, in0=scr, scalar1=mx[:, 15:16], scalar2=NEG,
                                                                         op0=ALU.is_lt, op1=ALU.mult), writes=["tk", "negb"])
                    psT = ps[7][:, :].bitcast(BF16)
                    P.mm_group([lambda e, psT=psT: e.transpose(out=psT[0:32, 0:128], in_=negb, identity=k.ident)],
                               reads=["negb", "ident"], writes=[("ps", 7)])
                    negdst = negT[0:32, q0 + tq * 128:q0 + (tq + 1) * 128]
                    P.op("act", lambda e, psT=psT, negdst=negdst: e.activation(out=negdst, in_=psT[0:32, 0:128], func=AF.Copy),
                         writes=[("ps", 7), "negT"])
            items = []
            for r in range(4):
                for br in (0, 1):
                    kt_lo = 0 if br == 0 else max(4 * qb - 4, 0)
                    kts = list(range(kt_lo, 4 * qb + 4))
                    for kt in kts:
                        items.append((r, br, kt, kt == kts[0] and br == 0, kt == kts[-1], kt == kts[-1] and br == 1))

            def stage_a(it):
                r, br, kt, first_of_head, last_of_branch, last_of_head = it
                head = g * 4 + r
                pb = 64 * (r % 2)
                ksrc, knm = (kslc, "kslc") if br == 0 else (kwin, "kwin")
                c0 = max(kt - 4 * qb, 0)
                c1 = 3 if br == 0 else min(kt + 4 - 4 * qb, 3)
                cols = slice(c0 * 128, (c1 + 1) * 128)
                qv = qn[pb:pb + 64, r // 2, q0 + c0 * 128:q0 + (c1 + 1) * 128]
                sb_ = sc["s"] % 2
                sc["s"] += 1
                use_sel = (br == 0 and qb >= 2)
                fns = [lambda e: e.matmul(ps[sb_][:, cols], lhsT=ksrc[pb:pb + 64, kt * 128:(kt + 1) * 128], rhs=qv, start=True, stop=False)]
                rd = [knm, ("qn", r // 2)]
                if use_sel:
                    negsl = negT[:, q0 + c0 * 128:q0 + (c1 + 1) * 128]
                    fns.append(lambda e: e.matmul(ps[sb_][:, cols], lhsT=exm[:, kt, :], rhs=negsl, start=False, stop=False))
                    rd += ["exm", "negT"]
                specials = []
                for tq in range(c0, c1 + 1):
                    off = 4 * qb + tq - kt
                    if off == 0:
                        specials += [(tq, tabhl[:, r, 0, 0, :]), (tq, tabhl[:, r, 0, 1, :])]
                    elif off == 1:
                        specials += [(tq, tabhl[:, r, 1, 0, :]), (tq, tabhl[:, r, 1, 1, :])]
                    elif off == 4 and br == 1:
                        specials += [(tq, m4b)]
                for si, (tq, src) in enumerate(specials):
                    fns.append(lambda e, tq=tq, src=src, lastsp=(si == len(specials) - 1): e.matmul(
                        ps[sb_][:, tq * 128:(tq + 1) * 128], lhsT=k.ident, rhs=src, start=False, stop=lastsp))
                if specials:
                    rd += ["tabhl", "m4b", "ident"]
                P.mm_group(fns, reads=rd, writes=[("ps", sb_)])
                return dict(sb=sb_, c0=c0, c1=c1, cols=cols)

            def stage_b(it, st_):
                r, br, kt, first_of_head, last_of_branch, last_of_head = it
                sb_, c0, c1, cols = st_["sb"], st_["c0"], st_["c1"], st_["cols"]
                ei = sc["e"] % 3
                sc["e"] += 1
                P.op("act", lambda e: e.activation(out=Ebuf[:, ei, cols], in_=ps[sb_][:, cols], func=AF.Exp),
                     writes=[("ps", sb_), ("E", ei)])
                st_["ei"] = ei

            def stage_c(it, st_):
                r, br, kt, first_of_head, last_of_branch, last_of_head = it
                ei, c0, c1 = st_["ei"], st_["c0"], st_["c1"]
                for tq in range(c0, c1 + 1):
                    qt = 4 * qb + tq
                    first = kt == (0 if br == 0 else max(qt - 4, 0))
                    last = kt == qt
                    P.mm_group([lambda e, tq=tq, first=first, last=last: e.matmul(
                        ps[2 + tq][:, 0:65], lhsT=Ebuf[:, ei, tq * 128:(tq + 1) * 128], rhs=vaug[:, kt, br, :], start=first, stop=last)],
                        reads=[("E", ei), "vaug"], writes=[("ps", 2 + tq)])
                if last_of_branch:
                    for tq in range(4):
                        P.op("act", lambda e, tq=tq: e.activation(out=Oraw[:, tq, br, :], in_=ps[2 + tq][:, 0:65], func=AF.Copy),
                             writes=[("ps", 2 + tq), "Oraw"])
                if last_of_head:
                    P.op("dve", lambda e: e.tensor_scalar(out=fbuf[:, :, 0], in0=Ocmp[:, :, r, 64], scalar1=1e-30, scalar2=None, op0=ALU.max),
                         reads=["Ocmp"], writes=["fbuf"])
                    P.op("dve", lambda e: e.tensor_scalar(out=fbuf[:, :, 1:3], in0=Oraw[:, :, :, 64], scalar1=1e-30, scalar2=None, op0=ALU.max),
                         reads=["Oraw"], writes=["fbuf"])
                    P.op("dve", lambda e: e.reciprocal(out=fbuf, in_=fbuf), writes=["fbuf"])
                    sigsl = sig[:, 4 * qb:4 * qb + 4, 3 * r:3 * r + 3]
                    P.op("dve", lambda e: e.tensor_tensor(out=fbuf, in0=fbuf, in1=sigsl, op=ALU.mult),
                         reads=["sig"], writes=["fbuf"])
                    for tq in range(4):
                        a32 = acc32[:, tq % 2]
                        P.op("dve", lambda e, tq=tq, a32=a32: e.tensor_scalar(out=a32, in0=Ocmp[:, tq, r, 0:64], scalar1=fbuf[:, tq, 0:1],
                                                                              scalar2=None, op0=ALU.mult),
                             reads=["Ocmp", "fbuf"], writes=[("acc32", tq % 2)])
                        P.op("dve", lambda e, tq=tq, a32=a32: e.scalar_tensor_tensor(out=a32, in0=Oraw[:, tq, 0, 0:64], scalar=fbuf[:, tq, 1:2],
                                                                                    in1=a32, op0=ALU.mult, op1=ALU.add),
                             reads=["Oraw", "fbuf"], writes=[("acc32", tq % 2)])
                        P.op("dve", lambda e, tq=tq, a32=a32: e.scalar_tensor_tensor(out=ynb[:, tq, r * 64:(r + 1) * 64], in0=Oraw[:, tq, 1, 0:64],
                                                                                    scalar=fbuf[:, tq, 2:3], in1=a32, op0=ALU.mult, op1=ALU.add),
                             reads=["Oraw", "fbuf", ("acc32", tq % 2)], writes=["ynb"])

            LOOK = DEBUG.get("look", 1)
            sts = {}
            for jj in range(min(LOOK, len(items))):
                sts[jj] = stage_a(items[jj])
            for ii, it in enumerate(items):
                if ii + LOOK < len(items):
                    sts[ii + LOOK] = stage_a(items[ii + LOOK])
                stage_b(it, sts[ii])
                stage_c(it, sts[ii])
                del sts[ii]
            psT = ps[7][:, :].bitcast(BF16)
            P.mm_group([lambda e, tq=tq, j=j, psT=psT: e.transpose(out=psT[:, (tq * 2 + j) * 128:(tq * 2 + j + 1) * 128],
                                                                  in_=ynb[:, tq, j * 128:(j + 1) * 128], identity=k.ident)
                        for tq in range(4) for j in range(2)], reads=["ynb", "ident"], writes=[("ps", 7)])
            for j in range(2):
                P.op("act", lambda e, j=j, psT=psT, g=g, q0=q0: e.activation(
                    out=yT[:, 4 + 2 * g + j, q0:q0 + TB].rearrange("p (a b) -> p a b", a=4),
                    in_=psT.rearrange("p (a j b) -> p a j b", a=4, j=2)[:, :, j, :], func=AF.Copy),
                    writes=[("ps", 7), ("yT", 4 + 2 * g + j, qb)])
    wk.release()
```
